# Optimizing a Trainium2 kernel written in Bass

```python
import jax, jax.numpy as jnp
from jax import lax
import numpy as np

D_MODEL = 1024
BATCH = 2
SEQ = 8192
DEPTH = 2

CHUNK = 64
D_MIX = D_MODEL
HEAD_DIM = 64
ATT_WIDTH = 3 * D_MIX // 8
ATT_HEADS = ATT_WIDTH // HEAD_DIM
ATT_PAST_CHUNKS = 8
ATT_BAND = (ATT_PAST_CHUNKS + 1) * CHUNK
REL_CLIP = 128
ML_WIDTH = 3 * D_MIX // 8
ML_HEADS = ML_WIDTH // HEAD_DIM
CONV_WIDTH = 4
GLA_WIDTH = D_MIX - ATT_WIDTH - ML_WIDTH
GLA_HEADS = 4
GLA_DV = GLA_WIDTH // GLA_HEADS
GLA_DK = GLA_DV // 2
GLA_KW = GLA_HEADS * GLA_DK
GLA_RANK = 16
GLA_TAU = 16.0
IN_SPLITS = (ATT_WIDTH, ATT_WIDTH, ATT_WIDTH, ATT_WIDTH,
             ML_WIDTH, ML_WIDTH, ML_WIDTH, ML_WIDTH, ML_HEADS, ML_HEADS, ML_WIDTH,
             GLA_KW, GLA_KW, GLA_WIDTH, GLA_RANK, GLA_WIDTH)
D_IN = sum(IN_SPLITS)
EPS = 1e-6
NEG = -1e30

kernel_name = "hymba_chunk_attn_mlstm_gla_trunk"


def rmsnorm(x, g):
    xf = x.astype(jnp.float32)
    y = xf * lax.rsqrt(jnp.mean(xf * xf, axis=-1, keepdims=True) + EPS)
    return (y * g.astype(jnp.float32)).astype(x.dtype)


def head_rmsnorm(x, g, n_heads):
    b, s, w = x.shape
    y = rmsnorm(x.reshape(b, s, n_heads, w // n_heads), g.reshape(n_heads, w // n_heads))
    return y.reshape(b, s, w)


def causal_conv(x, w):
    k_w = w.shape[0]
    s = x.shape[1]
    xp = jnp.pad(x, ((0, 0), (k_w - 1, 0), (0, 0)))
    return sum(xp[:, j:j + s] * w[j] for j in range(k_w))


def band_chunk_attention(q, k, v, rel_bias):
    bsz, s, h, d = q.shape
    nc = s // CHUNK
    pad = ATT_PAST_CHUNKS * CHUNK
    qc = q.reshape(bsz, nc, CHUNK, h, d)
    kp = jnp.pad(k, ((0, 0), (pad, 0), (0, 0), (0, 0))).reshape(bsz, nc + ATT_PAST_CHUNKS, CHUNK, h, d)
    vp = jnp.pad(v, ((0, 0), (pad, 0), (0, 0), (0, 0))).reshape(bsz, nc + ATT_PAST_CHUNKS, CHUNK, h, d)
    kb = jnp.concatenate([kp[:, m:m + nc] for m in range(ATT_PAST_CHUNKS + 1)], axis=2)
    vb = jnp.concatenate([vp[:, m:m + nc] for m in range(ATT_PAST_CHUNKS + 1)], axis=2)
    scores = jnp.einsum('bnqhd,bnkhd->bnhqk', qc, kb).astype(jnp.float32) * (d ** -0.5)
    rel = jnp.arange(CHUNK)[:, None] + pad - jnp.arange(ATT_BAND)[None, :]
    bias = rel_bias[:, jnp.clip(rel, -REL_CLIP, REL_CLIP) + REL_CLIP].astype(jnp.float32)
    kpos = (jnp.arange(nc)[:, None] - ATT_PAST_CHUNKS) * CHUNK + jnp.arange(ATT_BAND)[None, :]
    valid = kpos >= 0
    scores = jnp.where(valid[None, :, None, None, :], scores + bias[None, None], NEG)
    p = jax.nn.softmax(scores, axis=-1).astype(v.dtype)
    o = jnp.einsum('bnhqk,bnkhd->bnqhd', p, vb)
    return o.reshape(bsz, s, h * d)


def mlstm_chunkwise(q, k, v, i_pre, f_pre):
    bsz, s, h, d = q.shape
    nc = s // CHUNK
    L = CHUNK
    dt = q.dtype
    qc = q.astype(jnp.float32).reshape(bsz, nc, L, h, d)
    kc = (k.astype(jnp.float32) * (d ** -0.5)).reshape(bsz, nc, L, h, d)
    vc = v.astype(jnp.float32).reshape(bsz, nc, L, h, d)
    ig = i_pre.astype(jnp.float32).reshape(bsz, nc, L, h).transpose(0, 1, 3, 2)
    lf = jax.nn.log_sigmoid(f_pre.astype(jnp.float32)).reshape(bsz, nc, L, h).transpose(0, 1, 3, 2)
    bcum = jnp.cumsum(lf, axis=-1)
    g = bcum[..., -1]
    a = g[..., None] - bcum + ig
    m_loc = jnp.max(a, axis=-1)
    w = jnp.exp(a - m_loc[..., None])
    c_loc = jnp.einsum('bnhj,bnjhv,bnjhk->bnhvk', w, vc, kc)
    n_loc = jnp.einsum('bnhj,bnjhk->bnhk', w, kc)

    def step(carry, xs):
        c_st, n_st, m_st = carry
        g_c, m_l, c_l, n_l = xs
        m_new = jnp.maximum(g_c + m_st, m_l)
        s_old = jnp.exp(g_c + m_st - m_new)
        s_new = jnp.exp(m_l - m_new)
        c_new = s_old[..., None, None] * c_st + s_new[..., None, None] * c_l
        n_new = s_old[..., None] * n_st + s_new[..., None] * n_l
        return (c_new, n_new, m_new), (c_st, n_st, m_st)

    init = (jnp.zeros((bsz, h, d, d), jnp.float32), jnp.zeros((bsz, h, d), jnp.float32),
            jnp.zeros((bsz, h), jnp.float32))
    xs = (jnp.moveaxis(g, 1, 0), jnp.moveaxis(m_loc, 1, 0), jnp.moveaxis(c_loc, 1, 0), jnp.moveaxis(n_loc, 1, 0))
    _, (c_prev, n_prev, m_prev) = lax.scan(step, init, xs)
    c_prev = jnp.moveaxis(c_prev, 0, 1)
    n_prev = jnp.moveaxis(n_prev, 0, 1)
    m_prev = jnp.moveaxis(m_prev, 0, 1)

    causal = jnp.tril(jnp.ones((L, L), dtype=bool))
    log_d = jnp.where(causal, bcum[..., :, None] - bcum[..., None, :] + ig[..., None, :], NEG)
    inter_log = bcum + m_prev[..., None]
    m_row = jnp.maximum(inter_log, jnp.max(log_d, axis=-1))
    d_mat = jnp.exp(log_d - m_row[..., None])
    inter_w = jnp.exp(inter_log - m_row)
    sc = jnp.einsum('bnthd,bnjhd->bnhtj', qc, kc) * d_mat
    num = jnp.einsum('bnhtj,bnjhv->bnthv', sc, vc) + \
        inter_w.transpose(0, 1, 3, 2)[..., None] * jnp.einsum('bnhvk,bnthk->bnthv', c_prev, qc)
    den = jnp.sum(sc, axis=-1) + inter_w * jnp.einsum('bnhk,bnthk->bnht', n_prev, qc)
    den = jnp.maximum(jnp.abs(den), jnp.exp(-m_row))
    out = num / den.transpose(0, 1, 3, 2)[..., None]
    return out.reshape(bsz, s, h * d).astype(dt)


def gla_chunkwise(q, k, v, log_alpha):
    bsz, s, h, dk = q.shape
    dv = v.shape[-1]
    nc = s // CHUNK
    L = CHUNK
    dt = q.dtype

    def to_chunks(t):
        return jnp.moveaxis(t.astype(jnp.float32).reshape(bsz, nc, L, h, t.shape[-1]), 1, 0)

    xs = (to_chunks(q * (dk ** -0.5)), to_chunks(k), to_chunks(v), to_chunks(log_alpha))
    causal = jnp.tril(jnp.ones((L, L), dtype=bool))

    def step(state, chunk):
        qx, kx, vx, la = chunk
        bcum = jnp.cumsum(la, axis=1)
        diff = bcum[:, :, None] - bcum[:, None, :]
        decay = jnp.where(causal[None, :, :, None, None], jnp.exp(jnp.minimum(diff, 0.0)), 0.0)
        att = jnp.einsum('bthd,bjhd,btjhd->bhtj', qx, kx, decay)
        o_intra = jnp.einsum('bhtj,bjhv->bthv', att, vx)
        o_inter = jnp.einsum('bthd,bhdv->bthv', qx * jnp.exp(bcum), state)
        btot = bcum[:, -1]
        k_dec = kx * jnp.exp(btot[:, None] - bcum)
        state_new = jnp.exp(btot)[..., None] * state + jnp.einsum('bjhd,bjhv->bhdv', k_dec, vx)
        return state_new, o_intra + o_inter

    _, o = lax.scan(step, jnp.zeros((bsz, h, dk, dv), jnp.float32), xs)
    return jnp.moveaxis(o, 0, 1).reshape(bsz, s, h * dv).astype(dt)


def hybrid_layer(x, norm_g, w_in, b_gates, conv_w, w_alpha, b_alpha, rel_bias, ml_norm_g, gla_norm_g, w_out):
    bsz, s, _ = x.shape
    hn = rmsnorm(x, norm_g)
    z = jnp.einsum('bsd,de->bse', hn, w_in)
    split_idx = [int(i) for i in np.cumsum(IN_SPLITS)[:-1]]
    (aq, ak, av, ag, mq, mk, mv, mo, mi, mf, mg, gq, gk, gv, ga, gg) = jnp.split(z, split_idx, axis=-1)

    att = band_chunk_attention(aq.reshape(bsz, s, ATT_HEADS, HEAD_DIM), ak.reshape(bsz, s, ATT_HEADS, HEAD_DIM),
                               av.reshape(bsz, s, ATT_HEADS, HEAD_DIM), rel_bias)
    y_a = att * jax.nn.silu(ag)

    qk = jax.nn.silu(causal_conv(jnp.concatenate([mq, mk], axis=-1), conv_w))
    mq_c, mk_c = qk[..., :ML_WIDTH], qk[..., ML_WIDTH:]
    h_ml = mlstm_chunkwise(mq_c.reshape(bsz, s, ML_HEADS, HEAD_DIM), mk_c.reshape(bsz, s, ML_HEADS, HEAD_DIM),
                           mv.reshape(bsz, s, ML_HEADS, HEAD_DIM),
                           mi + b_gates[:ML_HEADS], mf + b_gates[ML_HEADS:])
    y_b = jax.nn.sigmoid(mo) * head_rmsnorm(h_ml, ml_norm_g, ML_HEADS) * jax.nn.silu(mg)

    log_alpha = jax.nn.log_sigmoid((jnp.einsum('bsr,rk->bsk', ga, w_alpha) + b_alpha).astype(jnp.float32)) / GLA_TAU
    h_gla = gla_chunkwise(gq.reshape(bsz, s, GLA_HEADS, GLA_DK), gk.reshape(bsz, s, GLA_HEADS, GLA_DK),
                          gv.reshape(bsz, s, GLA_HEADS, GLA_DV), log_alpha.reshape(bsz, s, GLA_HEADS, GLA_DK))
    y_c = head_rmsnorm(h_gla, gla_norm_g, GLA_HEADS) * jax.nn.silu(gg)

    y = jnp.concatenate([y_a, y_b, y_c], axis=-1)
    return x + jnp.einsum('bse,ed->bsd', y, w_out)


def setup_inputs(seed: int = 0) -> dict:
    key = jax.random.key(seed)
    ks = jax.random.split(key, 12)
    f32 = jnp.float32
    x = jax.random.normal(ks[0], (BATCH, SEQ, D_MODEL), f32)
    norm_g = 1.0 + 0.1 * jax.random.normal(ks[1], (DEPTH, D_MODEL), f32)
    w_in = jax.random.normal(ks[2], (DEPTH, D_MODEL, D_IN), f32) * D_MODEL ** -0.5
    f_bias = jnp.broadcast_to(jnp.linspace(3.0, 6.0, ML_HEADS, dtype=f32), (DEPTH, ML_HEADS))
    i_bias = 0.1 * jax.random.normal(ks[3], (DEPTH, ML_HEADS), f32)
    b_gates = jnp.concatenate([i_bias, f_bias + 0.1 * jax.random.normal(ks[4], (DEPTH, ML_HEADS), f32)], axis=-1)
    conv_w = jax.random.normal(ks[5], (DEPTH, CONV_WIDTH, 2 * ML_WIDTH), f32) * CONV_WIDTH ** -0.5
    w_alpha = jax.random.normal(ks[6], (DEPTH, GLA_RANK, GLA_KW), f32) * GLA_RANK ** -0.5
    b_alpha = 0.1 * jax.random.normal(ks[7], (DEPTH, GLA_KW), f32)
    rel_bias = 0.1 * jax.random.normal(ks[8], (DEPTH, ATT_HEADS, 2 * REL_CLIP + 1), f32)
    ml_norm_g = 1.0 + 0.1 * jax.random.normal(ks[9], (DEPTH, ML_WIDTH), f32)
    gla_norm_g = 1.0 + 0.1 * jax.random.normal(ks[10], (DEPTH, GLA_WIDTH), f32)
    k_out, k_fin = jax.random.split(ks[11])
    w_out = jax.random.normal(k_out, (DEPTH, D_MIX, D_MODEL), f32) * D_MIX ** -0.5
    final_g = 1.0 + 0.1 * jax.random.normal(k_fin, (D_MODEL,), f32)
    return {"x": x, "norm_g": norm_g, "w_in": w_in, "b_gates": b_gates, "conv_w": conv_w,
            "w_alpha": w_alpha, "b_alpha": b_alpha, "rel_bias": rel_bias, "ml_norm_g": ml_norm_g,
            "gla_norm_g": gla_norm_g, "w_out": w_out, "final_g": final_g}


def reference(x, norm_g, w_in, b_gates, conv_w, w_alpha, b_alpha, rel_bias, ml_norm_g, gla_norm_g, w_out, final_g):
    for l in range(DEPTH):
        x = hybrid_layer(x, norm_g[l], w_in[l], b_gates[l], conv_w[l], w_alpha[l], b_alpha[l], rel_bias[l],
                         ml_norm_g[l], gla_norm_g[l], w_out[l])
    return rmsnorm(x, final_g)
```

```python
import contextlib
import numpy as np
import concourse.bass as bass
import concourse.mybir as mybir
from concourse.bass_utils import run_bass_kernel_spmd

F32 = mybir.dt.float32
BF16 = mybir.dt.bfloat16
ALU = mybir.AluOpType
AF = mybir.ActivationFunctionType
AX = mybir.AxisListType

D = 1024
D_IN = 4252
C_AQ, C_AK, C_AV, C_AG = 0, 384, 768, 1152
C_MQ, C_MK, C_MV, C_MO, C_MI, C_MF, C_MG = 1536, 1920, 2304, 2688, 3072, 3078, 3084
C_GQ, C_GK, C_GV, C_GA, C_GG = 3468, 3596, 3724, 3980, 3996
EPS = 1e-6
LN8 = float(np.log(0.125))
FRONT_EVERY = 1
B1_EVERY = 1
K_ID = 0
K_TN1, K_TP1, K_SU1 = 128, 256, 384
K_TN16, K_TP16, K_SU16 = 512, 640, 768
K_CAUS = 896
K_BDM = 1408
K_BDG = 1538
K_HM = 1794
NCONST = 1798
P_G, P_CW, P_BG, P_BA, P_MLG, P_GLG, P_CB = 0, 8, 32, 44, 172, 556, 812
NLP = 818


class Buf:
    __slots__ = ("name", "w", "rs", "excl")

    def __init__(self, name, excl=False):
        self.name = name
        self.w = None
        self.rs = {}
        self.excl = excl


class Sched:
    ENGS = ("tensor", "vector", "scalar", "gpsimd", "sync")

    def __init__(self, nc, stack, n_dma_sems=20):
        self.nc = nc
        self.prog = {e: [] for e in self.ENGS}
        self.sems = {}
        self.cnt = {}
        for e in self.ENGS:
            self.sems[e] = stack.enter_context(nc.semaphore("sem_" + e))
            self.cnt[e] = 0
        self.dma_sems = []
        for i in range(n_dma_sems):
            key = "d%d" % i
            self.sems[key] = stack.enter_context(nc.semaphore("semd%d" % i))
            self.cnt[key] = 0
            self.dma_sems.append(key)
        self.dma_rr = 0
        self.waited = {e: {} for e in self.ENGS}

    def _need(self, eng, deps, key, val):
        if self.waited[eng].get(key, 0) >= val:
            return
        if deps.get(key, 0) < val:
            deps[key] = val

    def _deps(self, eng, reads, writes):
        deps = {}
        for b in reads:
            if b.w is not None:
                self._need(eng, deps, b.w[0], b.w[1])
        for b in writes:
            if b.w is not None:
                self._need(eng, deps, b.w[0], b.w[1])
            for k, v in b.rs.items():
                self._need(eng, deps, k, v)
        return deps

    def _emit_waits(self, eng, deps):
        for key, val in deps.items():
            sem = self.sems[key]
            self.prog[eng].append(lambda e, sem=sem, val=val: e.wait_ge(sem, val))
            self.waited[eng][key] = val

    def op(self, eng, fn, reads=(), writes=(), same_ok=False):
        xr = [b for b in reads if b.excl]
        if xr:
            reads = [b for b in reads if not b.excl]
            writes = list(writes) + [b for b in xr if b not in writes]
        deps = self._deps(eng, reads, writes)
        if same_ok:
            deps.pop(eng, None)
        self._emit_waits(eng, deps)
        self.cnt[eng] += 1
        val = self.cnt[eng]
        sem = self.sems[eng]
        self.prog[eng].append(lambda e, fn=fn, sem=sem: fn(e).then_inc(sem, 1))
        for b in reads:
            b.rs[eng] = val
        for b in writes:
            b.w = (eng, val)
            b.rs = {}

    def dma(self, eng, fn, reads=(), writes=()):
        key = self.dma_sems[self.dma_rr]
        self.dma_rr = (self.dma_rr + 1) % len(self.dma_sems)
        deps = self._deps(eng, reads, writes)
        if self.cnt[key] > 0:
            self._need(eng, deps, key, self.cnt[key])
        self._emit_waits(eng, deps)
        self.cnt[key] += 16
        val = self.cnt[key]
        sem = self.sems[key]
        self.prog[eng].append(lambda e, fn=fn, sem=sem: fn(e).then_inc(sem, 16))
        for b in reads:
            b.rs[key] = val
        for b in writes:
            b.w = (key, val)
            b.rs = {}

    def finish(self, eng):
        for k in self.dma_sems:
            if self.cnt[k] > 0:
                self.prog[eng].append(lambda e, sem=self.sems[k], val=self.cnt[k]: e.wait_ge(sem, val))

    def run(self):
        with self.nc.Block() as block:
            @block.sync
            def _(e):
                for t in self.prog["sync"]:
                    t(e)

            @block.scalar
            def _(e):
                for t in self.prog["scalar"]:
                    t(e)

            @block.vector
            def _(e):
                for t in self.prog["vector"]:
                    t(e)

            @block.gpsimd
            def _(e):
                for t in self.prog["gpsimd"]:
                    t(e)

            @block.tensor
            def _(e):
                for t in self.prog["tensor"]:
                    t(e)


def build(S_len, n_layers=2, dbg=None):
    NT = S_len // 128
    nc = bass.Bass("TRN2", target_bir_lowering=False)
    x_in = nc.dram_tensor("x", [S_len, D], F32, kind="ExternalInput").ap()
    w_in = nc.dram_tensor("w_in", [n_layers, D, D_IN], F32, kind="ExternalInput").ap()
    w_out = nc.dram_tensor("w_out", [n_layers, D, D], F32, kind="ExternalInput").ap()
    consts = nc.dram_tensor("consts", [128, NCONST], F32, kind="ExternalInput").ap()
    lp = nc.dram_tensor("lp", [n_layers, 128, NLP], F32, kind="ExternalInput").ap()
    walpha = nc.dram_tensor("walpha", [n_layers, 16, 128], F32, kind="ExternalInput").ap()
    relb = nc.dram_tensor("relb", [n_layers, 128, 6 * 256], F32, kind="ExternalInput").ap()
    fgbc = nc.dram_tensor("fgbc", [128, D], F32, kind="ExternalInput").ap()
    y_out = nc.dram_tensor("y", [S_len, D], F32, kind="ExternalOutput").ap()
    x1 = nc.dram_tensor("x1s", [S_len, D], F32, kind="Internal").ap()

    with contextlib.ExitStack() as st:
        S = Sched(nc, st)

        def sb(name, shape, dt):
            return st.enter_context(nc.sbuf_tensor(name, shape, dt)), Buf(name)

        def sb2(name, shape, dt, n=2):
            return [sb("%s_%d" % (name, i), shape, dt) for i in range(n)]

        def ps(name, shape, dt):
            return st.enter_context(nc.psum_tensor(name, shape, dt)), Buf(name, excl=True)

        NR = 6
        KC = K_CAUS
        W, bW = sb("W", [128, 8, D_IN], BF16)
        Wo, bWo = sb("Wo", [128, 8, D], BF16)
        SW = 532
        stg = [sb("stg%d" % i, [128, SW], F32) for i in range(2)]
        CT, bCT = sb("CT", [128, NCONST - KC], F32)
        idb, bidb = sb("idb", [128, 128], BF16)
        CTb, bCTb = sb("CTb", [128, 896], BF16)
        LFh, bLFh = sb("LFh", [128, 384], BF16)
        LFl, bLFl = sb("LFl", [128, 384], BF16)
        IGh, bIGh = sb("IGh", [128, 384], BF16)
        IGl, bIGl = sb("IGl", [128, 384], BF16)
        lghs = sb2("lgh", [128, 128], BF16)
        lgls = sb2("lgl", [128, 128], BF16)
        ktp, bktp = sb("ktp", [128, 128], BF16)
        WAh, bWAh = sb("WAh", [16, 128], BF16)
        WAl, bWAl = sb("WAl", [16, 128], BF16)
        gah, bgah = sb("gah", [16, 128], BF16)
        gal, bgal = sb("gal", [16, 128], BF16)
        LP, bLP = sb("LP", [128, NLP], F32)
        gsc, bgsc = sb("gsc", [128, 16], F32)
        bgc, bbgc = sb("bgc", [128, 12], F32)
        WA, bWA = sb("WA", [16, 128], F32)
        RBh, bRBh = sb("RBh", [128, 6, 256], BF16)
        RBl, bRBl = sb("RBl", [128, 6, 256], BF16)
        M0, bM0 = sb("M0", [128, 128], BF16)
        FG, bFG = sb("FG", [128, D], F32)
        xt = sb2("xt", [128, D], F32, 3)
        hb, bhb = sb("hb", [128, D], BF16)
        hT, bhT = sb("hT", [128, 8, 128], BF16)
        ss, bss = sb("ss", [128, 8], F32)
        ss2, bss2 = sb("ss2", [128, 8], F32)
        QT = sb2("QT", [128, 6, 128], BF16)
        KTr, _ = sb("KTr", [128, 3, NR * 128], BF16)
        bKT = [Buf("KT%d" % i) for i in range(NR)]
        Vr, _ = sb("Vr", [128, NR, 6, 65], BF16)
        bVr = [Buf("Vr%d" % i) for i in range(NR)]
        gA = sb2("gA", [128, 384], BF16)
        gB = sb2("gB", [128, 384], BF16, 3)
        gC = sb2("gC", [128, 256], BF16, 3)
        sg1, bsg1 = sb("sg1", [128, 384], F32)
        sg2, bsg2 = sb("sg2", [128, 384], F32)
        sg3, bsg3 = sb("sg3", [128, 384], F32)
        zraw, bzraw = sb("zraw", [128, 384], F32)
        Vm = sb2("Vm", [128, 6, 65], BF16, 3)
        ift = sb2("ift", [128, 16], F32)
        lfp = sb2("lfp", [128, 8], F32)
        cin = sb2("cin", [128, 6, 132], BF16)
        czb, bczb = sb("czb", [128, 768], BF16)
        DG, bDG = sb("DG", [128, 24, 128], BF16)
        csg, bcsg = sb("csg", [128, 6, 128], F32)
        gsp, bgsp = sb("gsp", [128, 4, 8], BF16)
        gqT = sb2("gqT", [128, 128], F32)
        gkT = sb2("gkT", [128, 128], F32)
        gkt = sb2("gkt", [128, 128], F32)
        gvb = sb2("gvb", [128, 256], BF16, 3)
        gaT = sb2("gaT", [16, 128], F32)
        lag, blag = sb("lag", [128, 128], F32)
        E3, bE3 = sb("E3", [128, 384], F32)
        ktl, bktl = sb("ktl", [128, 4, 128], BF16)
        Kp, bKp = sb("Kp", [128, 128], BF16)
        Sm, _ = sb("Sm", [128, 3, 130], F32)
        Smb, _ = sb("Smb", [128, 3, 130], BF16)
        bSm = [Buf("Sm%d" % i) for i in range(3)]
        bSmb = [Buf("Smb%d" % i) for i in range(3)]
        Sg, bSg = sb("Sg", [128, 256], F32)
        Sgb, bSgb = sb("Sgb", [128, 256], BF16)
        PT, bPT = sb("PT", [128, 640], BF16)
        SCA = dict(E3=(E3, bE3), ktl=(ktl, bktl), ktp=(ktp, bktp), Kp=(Kp, bKp))
        SCB = dict(E3=sb("E3b", [128, 256], F32), ktl=sb("ktlb", [128, 1, 128], BF16),
                   ktp=sb("ktpb", [128, 128], BF16), Kp=sb("Kpb", [128, 128], BF16))
        def handoff(tag, nptm, nts):
            return [dict(qtl=sb("hq%s%d" % (tag, i), [128, 128], BF16), PTm=sb("hp%s%d" % (tag, i), [128, nptm], BF16),
                         tS=sb("ht%s%d" % (tag, i), [128, nts], F32), eb=sb("he%s%d" % (tag, i), [128, 2], F32))
                    for i in range(2)]
        HO = [handoff("g", 512, 256)] + [handoff("m%d" % i, 256, 130) for i in range(3)]
        rec, brec = sb("rec", [128, 8], F32)
        reca, breca = sb("reca", [128, 8], F32)
        ya, bya = sb("ya", [128, 384], F32)
        hml, bhml = sb("hml", [128, 384], F32)
        hgl, bhgl = sb("hgl", [128, 256], F32)
        sq, bsq = sb("sq", [128, 384], F32)
        st6, bst6 = sb("st6", [128, 32], F32)
        ybfs = sb2("ybf", [128, D], BF16)
        yT, byT = sb("yT", [128, 8, 128], BF16)

        pT, bpT = ps("pT", [128, 8, 128], BF16)
        pA, bpA = ps("pA", [128, 512], F32)
        pB, bpB = ps("pB", [128, 512], F32)
        pS, bpS = ps("pS", [128, 1024], F32)
        pO, bpO = ps("pO", [128, 512], F32)
        pOa, bpOa = ps("pOa", [128, 512], F32)
        pR, bpR = ps("pR", [128, 512], F32)
        pD, bpD = pR, bpR

        bx1 = [Buf("x1_%d" % t) for t in range(NT)]
        by = [Buf("y_%d" % t) for t in range(NT)]

        V, A, G, PE = "vector", "scalar", "gpsimd", "tensor"

        S.dma("sync", lambda e: e.dma_start(out=CT[:], in_=consts[:, KC:NCONST]), writes=[bCT])
        S.dma("sync", lambda e: e.dma_start(out=FG[:], in_=fgbc[:, :]), writes=[bFG])
        for hf in range(2):
            stg_t, stg_b = stg[hf]
            S.dma("sync", lambda e, hf=hf, stg_t=stg_t: e.dma_start(out=stg_t[:, 0:448], in_=consts[:, hf * 448:(hf + 1) * 448]),
                  writes=[stg_b])
            S.op(V, lambda e, hf=hf, stg_t=stg_t: e.tensor_copy(out=CTb[:, hf * 448:(hf + 1) * 448], in_=stg_t[:, 0:448]),
                 reads=[stg_b], writes=[bCTb])
        S.op(V, lambda e: e.tensor_copy(out=idb[:], in_=CTb[:, K_ID:K_ID + 128]), reads=[bCTb], writes=[bidb])
        S.op(G, lambda e: e.memset(Vr[:], 1.0), writes=bVr)
        S.op(G, lambda e: e.memset(M0[:], 0.0), writes=[bM0])
        for i in range(2):
            S.op(G, lambda e, i=i: e.memset(QT[i][0][:], 0.0), writes=[QT[i][1]])
        S.op(G, lambda e: e.memset(M0[0:64, 64:128], -30000.0), writes=[bM0])
        for i in range(3):
            S.op(G, lambda e, i=i: e.memset(Vm[i][0][:], 1.0), writes=[Vm[i][1]])
        for i in range(2):
            S.op(G, lambda e, i=i: e.memset(ift[i][0][:], 0.0), writes=[ift[i][1]])
        S.op(G, lambda e: e.memset(st6[:], 1.0), writes=[bst6])

        def sigmoid_chain(src_ap, src_bufs, out_ap, out_bufs, n):
            S.op(A, lambda e: e.activation(out=sg1[:, 0:n], in_=src_ap, func=AF.Exp, scale=-1.0),
                 reads=src_bufs, writes=[bsg1])
            S.op(A, lambda e: e.activation(out=sg1[:, 0:n], in_=sg1[:, 0:n], func=AF.Ln, bias=1.0),
                 reads=[bsg1], writes=[bsg1])
            S.op(A, lambda e: e.activation(out=out_ap, in_=sg1[:, 0:n], func=AF.Exp, scale=-1.0),
                 reads=[bsg1], writes=out_bufs)

        def rsqrt_small(src_ap, dst_ap, buf, scale, n):
            S.op(V, lambda e: e.tensor_scalar(out=dst_ap, in0=src_ap, scalar1=scale, scalar2=EPS,
                                              op0=ALU.mult, op1=ALU.add), reads=[buf], writes=[buf])
            S.op(A, lambda e: e.activation(out=dst_ap, in_=dst_ap, func=AF.Ln), reads=[buf], writes=[buf])
            S.op(A, lambda e: e.activation(out=dst_ap, in_=dst_ap, func=AF.Exp, scale=-0.5), reads=[buf], writes=[buf])

        def mixer_prep(qT_ap, q_bufs, kT_ap, k_bufs, ktok, la_ap, la_bufs, ig_ap, ig_bufs, kc, Vap, V_bufs, nv,
                       heads, bdm_ap, sc, ho):
            TN, TP, SU = kc
            (E3, bE3), (ktl, bktl), (ktp, bktp), (Kp, bKp) = sc["E3"], sc["ktl"], sc["ktp"], sc["Kp"]
            (qtl, bqtl), (PTm, bPTm), (tS, btS), (eb, beb) = ho["qtl"], ho["PTm"], ho["tS"], ho["eb"]
            la_hi, la_lo = la_ap
            terms = [(la_hi, TN), (la_lo, TN)]
            for i, (lx, cc) in enumerate(terms):
                S.op(PE, lambda e, lx=lx, cc=cc, i=i: e.matmul(pR[:, 0:128], lhsT=lx, rhs=CTb[:, cc:cc + 128],
                                                             start=(i == 0), stop=(i == 1)),
                     reads=la_bufs + [bCTb], writes=[bpR], same_ok=(i > 0))
            terms = [(la_hi, TP), (la_lo, TP)]
            if ig_ap is not None:
                terms += [(ig_ap[0], K_ID), (ig_ap[1], K_ID)]
            for i, (lx, cc) in enumerate(terms):
                S.op(PE, lambda e, lx=lx, cc=cc, i=i, n=len(terms): e.matmul(
                    pR[:, 128:256], lhsT=lx, rhs=CTb[:, cc:cc + 128], start=(i == 0), stop=(i == n - 1)),
                    reads=la_bufs + ig_bufs + [bCTb], writes=[bpR], same_ok=True)
            ne = 256
            if ktok is not None:
                ne = 384
                for i, lx in enumerate((la_hi, la_lo)):
                    S.op(PE, lambda e, lx=lx, i=i: e.matmul(pR[:, 256:384], lhsT=CTb[:, SU:SU + 128], rhs=lx,
                                                            start=(i == 0), stop=(i == 1)),
                         reads=la_bufs + [bCTb], writes=[bpR], same_ok=True)
            S.op(A, lambda e: e.activation(out=E3[:, 0:ne], in_=pR[:, 0:ne], func=AF.Exp), reads=[bpR], writes=[bE3])
            yield
            S.op(V, lambda e: e.tensor_tensor(out=qtl[:], in0=qT_ap, in1=E3[:, 0:128], op=ALU.mult),
                 reads=q_bufs + [bE3], writes=[bqtl])
            nh = len(heads)
            if nh == 2:
                S.op(V, lambda e: e.tensor_tensor(out=ktl[:, 0, :], in0=kT_ap, in1=E3[:, 128:256], op=ALU.mult),
                     reads=k_bufs + [bE3], writes=[bktl])
            else:
                for hh in range(nh):
                    S.op(V, lambda e, hh=hh: e.scalar_tensor_tensor(
                        out=ktl[:, hh, :], in0=kT_ap, scalar=CT[:, K_HM - KC + hh:K_HM - KC + hh + 1], in1=E3[:, 128:256],
                        op0=ALU.mult, op1=ALU.mult), reads=k_bufs + [bE3, bCT], writes=[bktl])
            if ktok is None:
                S.op(V, lambda e: e.scalar_tensor_tensor(out=ktp[:], in0=kT_ap, scalar=E3[:, 127:128], in1=E3[:, 128:256],
                                                         op0=ALU.mult, op1=ALU.mult), reads=k_bufs + [bE3], writes=[bktp])
            else:
                S.op(V, lambda e: e.tensor_tensor(out=Kp[:], in0=ktok[0], in1=E3[:, 256:384], op=ALU.mult),
                     reads=ktok[1] + [bE3], writes=[bKp])
            for hh, (r0, nr, vc0, vcn) in enumerate(heads):
                if nh == 2:
                    S.op(PE, lambda e, hh=hh, r0=r0, nr=nr: e.matmul(
                        pS[:, hh * 512:hh * 512 + 128], lhsT=ktl[r0:r0 + nr, 0, :], rhs=qtl[r0:r0 + nr, :],
                        start=True, stop=True), reads=[bktl, bqtl], writes=[bpS], same_ok=(hh > 0))
                else:
                    S.op(PE, lambda e, hh=hh: e.matmul(
                        pS[:, hh * 128:(hh + 1) * 128], lhsT=ktl[:, hh, :], rhs=qtl[:, :],
                        start=True, stop=True), reads=[bktl, bqtl], writes=[bpS], same_ok=(hh > 0))
            if nh == 2:
                S.op(V, lambda e: e.tensor_tensor(
                    out=PTm[:, 0:256].rearrange("p (a b) -> p a b", a=2),
                    in0=pS[:, 0:1024].rearrange("p (a b) -> p a b", a=2)[:, :, 0:128],
                    in1=CT[:, K_CAUS - KC:K_CAUS - KC + 256].rearrange("p (a b) -> p a b", a=2), op=ALU.mult),
                    reads=[bpS, bCT], writes=[bPTm])
            else:
                S.op(V, lambda e: e.tensor_tensor(out=PTm[:, 0:nh * 128], in0=pS[:, 0:nh * 128],
                                                  in1=CT[:, K_CAUS - KC:K_CAUS - KC + nh * 128], op=ALU.mult),
                     reads=[bpS, bCT], writes=[bPTm])
            yield
            if ktok is None:
                S.op(PE, lambda e: e.transpose(out=pT[:, 0, :], in_=ktp[:], identity=idb[:]), reads=[bktp, bidb],
                     writes=[bpT])
                S.op(V, lambda e: e.tensor_copy(out=Kp[:], in_=pT[:, 0, :]), reads=[bpT], writes=[bKp])
            S.op(PE, lambda e: e.matmul(pD[:, 0:nv], lhsT=Kp[:, :], rhs=Vap, start=True, stop=True),
                 reads=[bKp] + V_bufs, writes=[bpD])
            S.op(V, lambda e: e.tensor_tensor(out=tS[:, 0:nv], in0=pD[:, 0:nv], in1=bdm_ap, op=ALU.mult),
                 reads=[bpD, bCT], writes=[btS])
            S.op(V, lambda e: e.tensor_copy(out=eb[:, 0:1], in_=E3[:, 127:128]), reads=[bE3], writes=[beb])
            yield

        def mixer_apply(Vap, V_bufs, nv, heads, Sf, bSf, Sbf, bSbf, o_ap, ho):
            (qtl, bqtl), (PTm, bPTm), (tS, btS), (eb, beb) = ho["qtl"], ho["PTm"], ho["tS"], ho["eb"]
            nh = len(heads)
            S.op(PE, lambda e: e.matmul(o_ap, lhsT=qtl[:, :], rhs=Sbf, start=True, stop=False),
                 reads=[bqtl, bSbf], writes=[bpO])
            for hh, (r0, nr, vc0, vcn) in enumerate(heads):
                S.op(PE, lambda e, hh=hh, vc0=vc0, vcn=vcn: e.matmul(
                    o_ap[:, vc0:vc0 + vcn], lhsT=PTm[:, hh * 128:(hh + 1) * 128], rhs=Vap[:, vc0:vc0 + vcn],
                    start=False, stop=(hh == nh - 1)), reads=[bPTm] + V_bufs, writes=[bpO], same_ok=True)
            S.op(V, lambda e: e.scalar_tensor_tensor(out=Sf, in0=Sf, scalar=eb[:, 0:1], in1=tS[:, 0:nv],
                                                     op0=ALU.mult, op1=ALU.add),
                 reads=[bSf, beb, btS], writes=[bSf])
            S.op(A, lambda e: e.activation(out=Sbf, in_=Sf, func=AF.Copy), reads=[bSf], writes=[bSbf])

        def front(l, t):
            d, d3 = t % 2, t % 3
            slot = t % NR
            xt_t, xt_b = xt[d3]
            QT_t, bQT = QT[d]
            gqT_t, bgqT = gqT[d]
            gkT_t, bgkT = gkT[d]
            gkt_t, bgkt = gkt[d]
            gvb_t, bgvb = gvb[d3]
            gaT_t, bgaT = gaT[d]
            lgh, blgh = lghs[d]
            lgl, blgl = lgls[d]
            cin_t, bcin = cin[d]
            cin_p, bcin_p = cin[1 - d]
            Vm_t, bVm = Vm[d3]
            ift_t, bift = ift[d]
            lfp_t, blfp = lfp[d]
            gA_t, bgA = gA[d]
            gB_t, bgB = gB[d3]
            gC_t, bgC = gC[d3]
            src = x_in if l == 0 else x1
            rd = [] if l == 0 else [bx1[t]]
            if t < 3:
                S.dma("sync", lambda e: e.dma_start(out=xt_t[:], in_=src[t * 128:(t + 1) * 128, :]), reads=rd, writes=[xt_b])
            S.op(V, lambda e: e.scalar_tensor_tensor(
                out=hb[:], in0=xt_t[:], scalar=1.0, in1=xt_t[:], op0=ALU.mult, op1=ALU.mult,
                accum_out=ss[:, 0:1]), reads=[xt_b], writes=[bhb, bss])
            rsqrt_small(ss[:, 0:1], ss[:, 1:2], bss, 1.0 / D, 1)
            S.op(A, lambda e: e.activation(out=hb[:], in_=xt_t[:], func=AF.Copy, scale=ss[:, 1:2]),
                 reads=[xt_b, bss], writes=[bhb])
            S.op(G, lambda e: e.tensor_copy(out=cin_t[:, :, 0:3], in_=cin_p[:, :, 128:131]), reads=[bcin_p], writes=[bcin])
            yield
            for k in range(8):
                S.op(PE, lambda e, k=k: e.transpose(out=pT[:, k, :], in_=hb[:, k * 128:(k + 1) * 128],
                                                    identity=idb[:]), reads=[bhb, bidb], writes=[bpT],
                     same_ok=(k > 0))
            S.op(V, lambda e: e.tensor_copy(out=hT[:], in_=pT[:]), reads=[bpT], writes=[bhT])
            yield

            def proj_fm(pX, bpX, cols):
                first = True
                for i, (c0, cn) in enumerate(cols):
                    for k in range(8):
                        S.op(PE, lambda e, i=i, c0=c0, cn=cn, k=k: e.matmul(
                            pX[0:cn, i * 128:(i + 1) * 128], lhsT=W[:, k, c0:c0 + cn], rhs=hT[:, k, :],
                            start=(k == 0), stop=(k == 7)), reads=[bW, bhT], writes=[bpX], same_ok=not first)
                        first = False

            def proj_tm(pX, bpX, c0, cn):
                for k in range(8):
                    S.op(PE, lambda e, k=k: e.matmul(
                        pX[:, 0:cn], lhsT=hT[:, k, :], rhs=W[:, k, c0:c0 + cn],
                        start=(k == 0), stop=(k == 7)), reads=[bW, bhT], writes=[bpX], same_ok=(k > 0))

            def ev_g1(pX, bpX):
                for half in range(2):
                    S.op(A, lambda e, half=half: e.activation(
                        out=QT_t[half * 64:(half + 1) * 64, :, :].rearrange("p (a two) b -> p a two b", two=2)[:, :, half, :],
                        in_=pX[half * 64:(half + 1) * 64, 0:384].rearrange("p (a b) -> p a b", a=3), func=AF.Copy),
                        reads=[bpX], writes=[bQT])
                S.op(A, lambda e: e.activation(out=gqT_t[:], in_=pX[:, 384:512], func=AF.Copy), reads=[bpX], writes=[bgqT])

            def ev_g2(pX, bpX):
                S.op(A, lambda e: e.activation(
                    out=KTr[:, :, slot * 128:(slot + 1) * 128], in_=pX[:, 0:384].rearrange("p (a b) -> p a b", a=3),
                    func=AF.Copy), reads=[bpX], writes=[bKT[slot]])
                S.op(A, lambda e: e.activation(out=gkT_t[:], in_=pX[:, 384:512], func=AF.Copy), reads=[bpX], writes=[bgkT])

            def ev_g3(pX, bpX):
                S.op(A, lambda e: e.activation(out=cin_t[:, 0:4, 3:131], in_=pX[:, 0:512].rearrange("p (a b) -> p a b", a=4),
                                               func=AF.Copy), reads=[bpX], writes=[bcin])

            def ev_g4(pX, bpX):
                S.op(A, lambda e: e.activation(out=cin_t[:, 4:6, 3:131], in_=pX[:, 0:256].rearrange("p (a b) -> p a b", a=2),
                                               func=AF.Copy), reads=[bpX], writes=[bcin])
                S.op(A, lambda e: e.activation(out=gaT_t[:], in_=pX[0:16, 256:384], func=AF.Copy), reads=[bpX], writes=[bgaT])

            def post_g4():
                S.op(G, lambda e: e.tensor_copy(out=gah[:], in_=gaT_t[:]), reads=[bgaT], writes=[bgah])
                S.op(G, lambda e: e.tensor_tensor(out=gal[:], in0=gaT_t[:], in1=gah[:], op=ALU.subtract), reads=[bgaT, bgah],
                     writes=[bgal])
                for i, (aa, ww) in enumerate(((gah, WAh), (gal, WAh), (gah, WAl))):
                    S.op(PE, lambda e, aa=aa, ww=ww, i=i: e.matmul(pR[:, 384:512], lhsT=aa[:, :], rhs=ww[:, :],
                                                                   start=(i == 0), stop=(i == 2)),
                         reads=[bgah, bgal, bWAh, bWAl], writes=[bpR], same_ok=(i > 0))
                S.op(V, lambda e: e.tensor_tensor(out=lag[:], in0=pR[:, 384:512], in1=LP[:, P_BA:P_BA + 128], op=ALU.add),
                     reads=[bpR, bLP], writes=[blag])
                S.op(A, lambda e: e.activation(out=lag[:], in_=lag[:], func=AF.Exp, scale=-1.0), reads=[blag], writes=[blag])
                S.op(A, lambda e: e.activation(out=lag[:], in_=lag[:], func=AF.Ln, bias=1.0), reads=[blag], writes=[blag])
                S.op(G, lambda e: e.tensor_copy(out=lgh[:], in_=lag[:]), reads=[blag], writes=[blgh])
                S.op(G, lambda e: e.tensor_tensor(out=lgl[:], in0=lag[:], in1=lgh[:], op=ALU.subtract), reads=[blag, blgh],
                     writes=[blgl])

            def ev_av(pX, bpX):
                S.op(A, lambda e: e.activation(
                    out=Vr[:, slot, :, 0:64], in_=pX[:, 0:384].rearrange("p (a b) -> p a b", a=6), func=AF.Copy),
                    reads=[bpX], writes=[bVr[slot]])

            def ev_ag(pX, bpX):
                S.op(A, lambda e: e.activation(out=zraw[:], in_=pX[:, 0:384], func=AF.Copy), reads=[bpX], writes=[bzraw])
                sigmoid_chain(pX[:, 0:384], [bpX], sg2[:], [bsg2], 384)

            def post_ag():
                S.op(G, lambda e: e.tensor_tensor(out=gA_t[:], in0=zraw[:], in1=sg2[:], op=ALU.mult),
                     reads=[bzraw, bsg2], writes=[bgA])

            def ev_mv(pX, bpX):
                S.op(A, lambda e: e.activation(out=Vm_t[:, :, 0:64], in_=pX[:, 0:384].rearrange("p (a b) -> p a b", a=6),
                                               func=AF.Copy), reads=[bpX], writes=[bVm])

            def ev_mo(pX, bpX):
                S.op(V, lambda e: e.tensor_tensor(out=ift_t[:].rearrange("p (a b) -> p a b", a=2)[:, :, 0:6],
                                                  in0=pX[:, 384:396].rearrange("p (a b) -> p a b", a=2),
                                                  in1=bgc[:].rearrange("p (a b) -> p a b", a=2), op=ALU.add),
                     reads=[bpX, bbgc], writes=[bift])
                S.op(A, lambda e: e.activation(out=lfp_t[:], in_=ift_t[:, 8:16], func=AF.Exp, scale=-1.0), reads=[bift],
                     writes=[blfp])
                S.op(A, lambda e: e.activation(out=lfp_t[:], in_=lfp_t[:], func=AF.Ln, bias=1.0), reads=[blfp], writes=[blfp])
                sigmoid_chain(pX[:, 0:384], [bpX], sg3[:], [bsg3], 384)

            def post_mo():
                S.op(G, lambda e: e.tensor_tensor(out=sg3[:], in0=sg3[:], in1=LP[:, P_MLG:P_MLG + 384], op=ALU.mult),
                     reads=[bsg3, bLP], writes=[bsg3])

            def ev_mg(pX, bpX):
                S.op(A, lambda e: e.activation(out=zraw[:], in_=pX[:, 0:384], func=AF.Copy), reads=[bpX], writes=[bzraw])
                sigmoid_chain(pX[:, 0:384], [bpX], sg2[:], [bsg2], 384)

            def post_mg():
                S.op(G, lambda e: e.tensor_tensor(out=sg2[:], in0=zraw[:], in1=sg2[:], op=ALU.mult),
                     reads=[bzraw, bsg2], writes=[bsg2])
                S.op(G, lambda e: e.tensor_tensor(out=gB_t[:], in0=sg3[:], in1=sg2[:], op=ALU.mult),
                     reads=[bsg3, bsg2], writes=[bgB])

            def ev_gkv(pX, bpX):
                S.op(A, lambda e: e.activation(out=gkt_t[:], in_=pX[:, 0:128], func=AF.Copy), reads=[bpX], writes=[bgkt])
                S.op(A, lambda e: e.activation(out=gvb_t[:], in_=pX[:, 128:384], func=AF.Copy), reads=[bpX], writes=[bgvb])

            def ev_gg(pX, bpX):
                S.op(A, lambda e: e.activation(out=zraw[:, 0:256], in_=pX[:, 0:256], func=AF.Copy), reads=[bpX],
                     writes=[bzraw])
                sigmoid_chain(pX[:, 0:256], [bpX], sg2[:, 0:256], [bsg2], 256)

            def post_gg():
                S.op(G, lambda e: e.tensor_tensor(out=sg2[:, 0:256], in0=zraw[:, 0:256], in1=sg2[:, 0:256], op=ALU.mult),
                     reads=[bzraw, bsg2], writes=[bsg2])
                S.op(G, lambda e: e.tensor_tensor(out=gC_t[:], in0=sg2[:, 0:256], in1=LP[:, P_GLG:P_GLG + 256], op=ALU.mult),
                     reads=[bsg2, bLP], writes=[bgC])

            FM = lambda cols: (lambda pX, bpX: proj_fm(pX, bpX, cols))
            TM = lambda c0, cn: (lambda pX, bpX: proj_tm(pX, bpX, c0, cn))
            groups = [
                (FM([(C_AQ, 128), (C_AQ + 128, 128), (C_AQ + 256, 128), (C_GQ, 128)]), ev_g1, None),
                (FM([(C_AK, 128), (C_AK + 128, 128), (C_AK + 256, 128), (C_GK, 128)]), ev_g2, None),
                (FM([(C_MQ, 128), (C_MQ + 128, 128), (C_MQ + 256, 128), (C_MK, 128)]), ev_g3, None),
                (FM([(C_MK + 128, 128), (C_MK + 256, 128), (C_GA, 16)]), ev_g4, post_g4),
                (TM(C_AV, 384), ev_av, None),
                (TM(C_AG, 384), ev_ag, post_ag),
                (TM(C_MV, 384), ev_mv, None),
                (TM(C_MO, 396), ev_mo, post_mo),
                (TM(C_MG, 384), ev_mg, post_mg),
                (TM(C_GK, 384), ev_gkv, None),
                (TM(C_GG, 256), ev_gg, post_gg),
            ]
            banks = ((pA, bpA), (pB, bpB))
            ng = len(groups)
            for k in range(ng + 2):
                if 0 <= k - 2 < ng and groups[k - 2][2] is not None:
                    groups[k - 2][2]()
                if 0 <= k - 1 < ng:
                    groups[k - 1][1](*banks[(k - 1) % 2])
                if k < ng:
                    groups[k][0](*banks[k % 2])
                yield

        def b1(l, t):
            d = t % 2
            QT_t, bQT = QT[d]
            lgh, blgh = lghs[d]
            lgl, blgl = lgls[d]
            cin_t, bcin = cin[d]
            ift_t, bift = ift[d]
            lfp_t, blfp = lfp[d]
            gA_t, bgA = gA[d]
            ybf, bybf = ybfs[d]
            gqT_t, bgqT = gqT[d]
            gkT_t, bgkT = gkT[d]
            gkt_t, bgkt = gkt[d]
            gvb_t, bgvb = gvb[t % 3]
            Vm_t, bVm = Vm[t % 3]
            pO, bpO = pOa, bpOa
            rec, brec = reca, breca
            j0 = max(0, 4 - t)
            pO3 = pO[:, 0:390].rearrange("p (h c) -> p h c", h=6)
            qkT, bqkT = csg, bcsg

            def att():
                for h in range(6):
                    p, half = h // 2, h % 2
                    r0 = half * 64
                    first = True
                    for j in range(j0, 5):
                        sj = (t - 4 + j) % NR
                        extra = j >= 3 or j == 0
                        S.op(PE, lambda e, j=j, sj=sj, p=p, h=h, extra=extra: e.matmul(
                            pS[:, j * 128:(j + 1) * 128], lhsT=KTr[:, p, sj * 128:(sj + 1) * 128],
                            rhs=QT_t[:, h, :], start=True, stop=not extra),
                            reads=[bKT[sj], bQT], writes=[bpS], same_ok=not first)
                        first = False
                        if j == 0:
                            S.op(PE, lambda e: e.matmul(pS[:, 0:128], lhsT=idb[:], rhs=M0[:], start=False, stop=True),
                                 reads=[bidb, bM0], writes=[bpS], same_ok=True)
                        if j >= 3:
                            S.op(PE, lambda e, j=j, h=h: e.matmul(
                                pS[:, j * 128:(j + 1) * 128], lhsT=idb[:], rhs=RBh[:, h, (j - 3) * 128:(j - 2) * 128],
                                start=False, stop=False), reads=[bidb, bRBh], writes=[bpS], same_ok=True)
                            S.op(PE, lambda e, j=j, h=h: e.matmul(
                                pS[:, j * 128:(j + 1) * 128], lhsT=idb[:], rhs=RBl[:, h, (j - 3) * 128:(j - 2) * 128],
                                start=False, stop=True), reads=[bidb, bRBl], writes=[bpS], same_ok=True)
                    S.op(A, lambda e, h=h: e.activation(
                        out=PT[:, j0 * 128:640], in_=pS[:, j0 * 128:640], func=AF.Exp,
                        bias=LP[:, P_CB + h:P_CB + h + 1]), reads=[bpS, bLP], writes=[bPT])
                    yield
                    for j in range(j0, 5):
                        sj = (t - 4 + j) % NR
                        S.op(PE, lambda e, j=j, sj=sj, h=h: e.matmul(
                            pO[:, h * 65:(h + 1) * 65], lhsT=PT[:, j * 128:(j + 1) * 128], rhs=Vr[:, sj, h, :],
                            start=(j == j0), stop=(j == 4)), reads=[bPT, bVr[sj]], writes=[bpO], same_ok=(j > j0))
                S.op(V, lambda e: e.reciprocal(out=rec[:, 0:6].unsqueeze(2), in_=pO3[:, :, 64:65]), reads=[bpO], writes=[brec])
                S.op(V, lambda e: e.tensor_tensor(
                    out=ya[:].rearrange("p (h c) -> p h c", h=6), in0=pO3[:, :, 0:64],
                    in1=rec[:, 0:6].unsqueeze(2).to_broadcast([128, 6, 64]), op=ALU.mult),
                    reads=[bpO, brec], writes=[bya])
                S.op(G, lambda e: e.tensor_tensor(out=ybf[:, 0:384], in0=ya[:], in1=gA_t[:], op=ALU.mult),
                     reads=[bya, bgA], writes=[bybf])
                yield


            def mprep():
                pG = mixer_prep(gqT_t[:, :], [bgqT], gkT_t[:, :], [bgkT], (gkt_t[:, :], [bgkt]),
                                (lgh[:, :], lgl[:, :]), [blgh, blgl], None, [], (K_TN16, K_TP16, K_SU16),
                                gvb_t[:, :], [bgvb], 256, [(32 * i, 32, 64 * i, 64) for i in range(4)],
                                CT[:, K_BDG - KC:K_BDG - KC + 256], SCA, HO[0][d])

                first = True
                for ct in range(6):
                    for jj in range(4):
                        S.op(PE, lambda e, ct=ct, jj=jj: e.matmul(
                            pS[:, ct * 128:(ct + 1) * 128], lhsT=DG[:, ct * 4 + jj, :], rhs=cin_t[:, ct, jj:jj + 128],
                            start=(jj == 0), stop=(jj == 3)), reads=[bDG, bcin], writes=[bpS], same_ok=not first)
                        first = False
                sflat = csg[:].rearrange("p a b -> p (a b)")
                S.op(A, lambda e: e.activation(out=czb[:], in_=pS[:, 0:768], func=AF.Copy), reads=[bpS], writes=[bczb])
                S.op(A, lambda e: e.activation(out=sflat, in_=pS[:, 0:768], func=AF.Exp, scale=-1.0), reads=[bpS], writes=[bcsg])
                next(pG)
                yield
                S.op(A, lambda e: e.activation(out=sflat, in_=sflat, func=AF.Ln, bias=1.0), reads=[bcsg], writes=[bcsg])
                S.op(A, lambda e: e.activation(out=sflat, in_=sflat, func=AF.Exp, scale=-1.0), reads=[bcsg], writes=[bcsg])
                next(pG)
                yield
                for hf in range(2):
                    S.op(G, lambda e, hf=hf: e.tensor_tensor(out=sflat[:, hf * 384:(hf + 1) * 384],
                                                             in0=czb[:, hf * 384:(hf + 1) * 384],
                                                             in1=sflat[:, hf * 384:(hf + 1) * 384], op=ALU.mult),
                         reads=[bczb, bcsg], writes=[bcsg])
                for gi, (src_ap, src_b) in enumerate(((lfp_t[:, 0:6], blfp), (ift_t[:, 0:6], bift))):
                    S.op(V, lambda e, gi=gi, src_ap=src_ap: e.tensor_copy(out=gsp[:, 2 * gi, 0:6], in_=src_ap),
                         reads=[src_b], writes=[bgsp])
                    S.op(V, lambda e, gi=gi, src_ap=src_ap: e.tensor_tensor(out=gsp[:, 2 * gi + 1, 0:6], in0=src_ap,
                                                                           in1=gsp[:, 2 * gi, 0:6], op=ALU.subtract),
                         reads=[src_b, bgsp], writes=[bgsp])
                for gi, (dst_t, dst_b) in enumerate(((LFh, bLFh), (LFl, bLFl), (IGh, bIGh), (IGl, bIGl))):
                    S.op(V, lambda e, gi=gi, dst_t=dst_t: e.tensor_copy(
                        out=dst_t[:].rearrange("p (h c) -> p h c", h=6),
                        in_=gsp[:, gi, 0:6].unsqueeze(2).to_broadcast([128, 6, 64])), reads=[bgsp], writes=[dst_b])
                next(pG)
                yield
                Vmf = Vm_t[:].rearrange("p h c -> p (h c)")
                def mk_pair(pr, sc):
                    return mixer_prep(qkT[:, pr, :], [bqkT], qkT[:, 3 + pr, :], [bqkT], None,
                                      (LFh[:, pr * 128:(pr + 1) * 128], LFl[:, pr * 128:(pr + 1) * 128]), [bLFh, bLFl],
                                      (IGh[:, pr * 128:(pr + 1) * 128], IGl[:, pr * 128:(pr + 1) * 128]), [bIGh, bIGl],
                                      (K_TN1, K_TP1, K_SU1), Vmf[:, pr * 130:(pr + 1) * 130], [bVm], 130,
                                      [(0, 64, 0, 65), (64, 64, 65, 65)], CT[:, K_BDM - KC:K_BDM - KC + 130], sc,
                                      HO[1 + pr][d])

                p0, p1, p2 = mk_pair(0, SCB), mk_pair(1, SCA), mk_pair(2, SCB)
                for g in (p0, p0, p1, p0, p1, p2, p1, p2, p2):
                    next(g)
                    yield

            ga, gm = att(), mprep()
            while ga is not None or gm is not None:
                if ga is not None:
                    try:
                        next(ga)
                    except StopIteration:
                        ga = None
                if gm is not None:
                    try:
                        next(gm)
                    except StopIteration:
                        gm = None
                yield
        def b2(l, t):
            d, d3 = t % 2, t % 3
            xt_t, xt_b = xt[d3]
            gvb_t, bgvb = gvb[d3]
            Vm_t, bVm = Vm[d3]
            gB_t, bgB = gB[d3]
            gC_t, bgC = gC[d3]
            ybf, bybf = ybfs[d]
            pO3 = pO[:, 0:390].rearrange("p (h c) -> p h c", h=6)
            Vmf = Vm_t[:].rearrange("p h c -> p (h c)")
            def gpost():
                S.op(G, lambda e: e.tensor_tensor(out=sq[:, 0:256], in0=hgl[:], in1=hgl[:], op=ALU.mult),
                     reads=[bhgl], writes=[bsq])
                yield
                S.op(V, lambda e: e.tensor_reduce(out=st6[:, 16:20], in_=sq[:, 0:256].rearrange("p (h c) -> p h c", h=4),
                                                  axis=AX.X, op=ALU.add), reads=[bsq], writes=[bst6])
                yield
                rsqrt_small(st6[:, 16:24], st6[:, 24:32], bst6, 1.0 / 64, 8)
                yield
                S.op(V, lambda e: e.tensor_tensor(
                    out=hgl[:].rearrange("p (h c) -> p h c", h=4), in0=hgl[:].rearrange("p (h c) -> p h c", h=4),
                    in1=st6[:, 24:28].unsqueeze(2).to_broadcast([128, 4, 64]), op=ALU.mult),
                    reads=[bhgl, bst6], writes=[bhgl])
                yield
                S.op(G, lambda e: e.tensor_tensor(out=ybf[:, 768:1024], in0=hgl[:], in1=gC_t[:], op=ALU.mult),
                     reads=[bhgl, bgC], writes=[bybf])
                yield

            gq = gpost()

            def adv(g):
                try:
                    next(g)
                except StopIteration:
                    pass

            mixer_apply(gvb_t[:, :], [bgvb], 256, [(32 * i, 32, 64 * i, 64) for i in range(4)], Sg[:, :], bSg,
                        Sgb[:, :], bSgb, pO[:, 0:256], HO[0][d])
            S.op(A, lambda e: e.activation(out=hgl[:], in_=pO[:, 0:256], func=AF.Copy), reads=[bpO], writes=[bhgl])
            yield
            for pr in range(3):
                mixer_apply(Vmf[:, pr * 130:(pr + 1) * 130], [bVm], 130, [(0, 64, 0, 65), (64, 64, 65, 65)],
                            Sm[:, pr, :], bSm[pr], Smb[:, pr, :], bSmb[pr], pO[:, pr * 130:(pr + 1) * 130], HO[1 + pr][d])
                adv(gq)
                adv(gq)
                yield
            for _ in range(6):
                adv(gq)
            S.op(V, lambda e: e.tensor_copy(out=rec[:, 0:6].unsqueeze(2), in_=pO3[:, :, 64:65]), reads=[bpO], writes=[brec])
            S.op(V, lambda e: e.scalar_tensor_tensor(out=rec[:, 0:6], in0=rec[:, 0:6], scalar=-1.0, in1=rec[:, 0:6],
                                                     op0=ALU.mult, op1=ALU.max), reads=[brec], writes=[brec])
            S.op(V, lambda e: e.tensor_scalar(out=rec[:, 0:6], in0=rec[:, 0:6], scalar1=1.0, scalar2=None,
                                              op0=ALU.max), reads=[brec], writes=[brec])
            S.op(V, lambda e: e.reciprocal(out=rec[:, 0:6], in_=rec[:, 0:6]), reads=[brec], writes=[brec])
            S.op(V, lambda e: e.tensor_tensor(
                out=hml[:].rearrange("p (h c) -> p h c", h=6), in0=pO3[:, :, 0:64],
                in1=rec[:, 0:6].unsqueeze(2).to_broadcast([128, 6, 64]), op=ALU.mult),
                reads=[bpO, brec], writes=[bhml])
            S.op(G, lambda e: e.tensor_tensor(out=sq[:], in0=hml[:], in1=hml[:], op=ALU.mult), reads=[bhml], writes=[bsq])
            S.op(V, lambda e: e.tensor_reduce(out=st6[:, 0:6], in_=sq[:].rearrange("p (h c) -> p h c", h=6),
                                              axis=AX.X, op=ALU.add), reads=[bsq], writes=[bst6])
            rsqrt_small(st6[:, 0:8], st6[:, 8:16], bst6, 1.0 / 64, 8)
            S.op(V, lambda e: e.tensor_tensor(
                out=hml[:].rearrange("p (h c) -> p h c", h=6), in0=hml[:].rearrange("p (h c) -> p h c", h=6),
                in1=st6[:, 8:14].unsqueeze(2).to_broadcast([128, 6, 64]), op=ALU.mult),
                reads=[bhml, bst6], writes=[bhml])
            S.op(G, lambda e: e.tensor_tensor(out=ybf[:, 384:768], in0=hml[:], in1=gB_t[:], op=ALU.mult),
                 reads=[bhml, bgB], writes=[bybf])
            yield

            for k in range(8):
                S.op(PE, lambda e, k=k: e.transpose(out=pT[:, k, :], in_=ybf[:, k * 128:(k + 1) * 128],
                                                    identity=idb[:]), reads=[bybf, bidb], writes=[bpT],
                     same_ok=(k > 0))
            S.op(V, lambda e: e.tensor_copy(out=yT[:], in_=pT[:]), reads=[bpT], writes=[byT])
            yield
            for half, (pX, bpX) in enumerate(((pO, bpO), (pR, bpR))):
                for k in range(8):
                    S.op(PE, lambda e, k=k, half=half, pX=pX: e.matmul(
                        pX[:, 0:512], lhsT=yT[:, k, :], rhs=Wo[:, k, half * 512:(half + 1) * 512],
                        start=(k == 0), stop=(k == 7)), reads=[byT, bWo], writes=[bpX], same_ok=(k > 0))
                S.op(V, lambda e, half=half, pX=pX: e.tensor_tensor(
                    out=xt_t[:, half * 512:(half + 1) * 512], in0=pX[:, 0:512],
                    in1=xt_t[:, half * 512:(half + 1) * 512], op=ALU.add), reads=[bpX, xt_b], writes=[xt_b])
                yield
            if l < n_layers - 1:
                S.dma("sync", lambda e: e.dma_start(out=x1[t * 128:(t + 1) * 128, :], in_=xt_t[:]),
                      reads=[xt_b], writes=[bx1[t]])
            else:
                S.op(V, lambda e: e.scalar_tensor_tensor(
                    out=ybf[:], in0=xt_t[:], scalar=1.0, in1=xt_t[:], op0=ALU.mult, op1=ALU.mult,
                    accum_out=ss2[:, 0:1]), reads=[xt_b], writes=[bybf, bss2])
                rsqrt_small(ss2[:, 0:1], ss2[:, 1:2], bss2, 1.0 / D, 1)
                S.op(V, lambda e: e.scalar_tensor_tensor(
                    out=xt_t[:], in0=xt_t[:], scalar=ss2[:, 1:2], in1=FG[:], op0=ALU.mult, op1=ALU.mult),
                    reads=[xt_b, bss2, bFG], writes=[xt_b])
                S.dma("sync", lambda e: e.dma_start(out=y_out[t * 128:(t + 1) * 128, :], in_=xt_t[:]),
                      reads=[xt_b], writes=[by[t]])
            if t + 3 < NT:
                srcp = x_in if l == 0 else x1
                rdp = [] if l == 0 else [bx1[t + 3]]
                S.dma("sync", lambda e: e.dma_start(out=xt_t[:], in_=srcp[(t + 3) * 128:(t + 4) * 128, :]), reads=rdp,
                      writes=[xt_b])
            yield

        def drive(*gens):
            gens = [[g, ev] for (g, ev) in gens if g is not None]
            i = 0
            while gens:
                for ent in list(gens):
                    if i % ent[1] == 0 or len(gens) == 1:
                        try:
                            next(ent[0])
                        except StopIteration:
                            gens.remove(ent)
                i += 1

        for l in range(n_layers):
            S.dma("sync", lambda e, l=l: e.dma_start(out=LP[:], in_=lp[l, :, :]), writes=[bLP])
            S.dma("sync", lambda e, l=l: e.dma_start(out=WA[:], in_=walpha[l, :, :]), writes=[bWA])
            for ch in range(3):
                stg_t, stg_b = stg[ch % 2]
                S.dma("sync", lambda e, l=l, ch=ch, stg_t=stg_t: e.dma_start(out=stg_t[:, 0:512], in_=relb[l, :, ch * 512:(ch + 1) * 512]),
                      writes=[stg_b])
                for hh in range(2):
                    h = 2 * ch + hh
                    seg = stg_t[:, hh * 256:(hh + 1) * 256]
                    S.op(V, lambda e, h=h, seg=seg: e.tensor_scalar(out=seg, in0=seg, scalar1=LP[:, P_CB + h:P_CB + h + 1],
                                                                  scalar2=None, op0=ALU.subtract), reads=[stg_b, bLP], writes=[stg_b])
                    S.op(V, lambda e, h=h, seg=seg: e.tensor_copy(out=RBh[:, h, :], in_=seg), reads=[stg_b], writes=[bRBh])
                    S.op(V, lambda e, h=h, seg=seg: e.tensor_tensor(out=RBl[:, h, :], in0=seg, in1=RBh[:, h, :], op=ALU.subtract),
                         reads=[stg_b, bRBh], writes=[bRBl])
            S.op(G, lambda e: e.memset(RBh[64:128, :, 128:192], -30000.0), writes=[bRBh])
            S.op(G, lambda e: e.memset(RBl[64:128, :, 128:192], 0.0), writes=[bRBl])
            S.op(V, lambda e: e.tensor_scalar(out=gsc[:, 0:8], in0=LP[:, P_G:P_G + 8], scalar1=0.125, scalar2=None,
                                              op0=ALU.mult), reads=[bLP], writes=[bgsc])
            S.op(V, lambda e: e.tensor_scalar(out=gsc[:, 8:16], in0=LP[:, P_G:P_G + 8], scalar1=float(32 ** -0.5),
                                              scalar2=None, op0=ALU.mult), reads=[bLP], writes=[bgsc])
            S.op(V, lambda e: e.tensor_copy(out=bgc[:], in_=LP[:, P_BG:P_BG + 12]), reads=[bLP], writes=[bbgc])
            S.op(V, lambda e: e.tensor_copy(out=WAh[:], in_=WA[:]), reads=[bWA], writes=[bWAh])
            S.op(V, lambda e: e.tensor_tensor(out=WAl[:], in0=WA[:], in1=WAh[:], op=ALU.subtract), reads=[bWA, bWAh],
                 writes=[bWAl])
            S.op(V, lambda e: e.tensor_scalar(out=bgc[:, 0:6], in0=bgc[:, 0:6], scalar1=LN8, scalar2=None,
                                              op0=ALU.add), reads=[bbgc], writes=[bbgc])
            for idx in range(24):
                S.op(V, lambda e, idx=idx: e.tensor_scalar(out=DG[:, idx, :], in0=idb[:], scalar1=LP[:, P_CW + idx:P_CW + idx + 1],
                                                           scalar2=None, op0=ALU.mult), reads=[bidb, bLP], writes=[bDG])
            stgs = [stg[0], stg[1], xt[0], xt[1], xt[2]]
            si = 0
            for k in range(8):
                for c0 in range(0, D_IN, SW):
                    cn = min(SW, D_IN - c0)
                    stg_t, stg_b = stgs[si % 5]
                    si += 1
                    S.dma("sync", lambda e, l=l, k=k, c0=c0, cn=cn, stg_t=stg_t: e.dma_start(
                        out=stg_t[:, 0:cn], in_=w_in[l, k * 128:(k + 1) * 128, c0:c0 + cn]), writes=[stg_b])
                    for (a, b_, sc) in ((0, 384, 0), (384, C_GQ, 1), (C_GQ, C_GK, 2), (C_GK, D_IN, 1)):
                        lo, hi = max(a, c0), min(b_, c0 + cn)
                        if lo >= hi:
                            continue
                        if sc == 0:
                            scl = gsc[:, k:k + 1]
                        elif sc == 2:
                            scl = gsc[:, 8 + k:9 + k]
                        else:
                            scl = LP[:, P_G + k:P_G + k + 1]
                        if si % 2 == 0:
                            S.op(A, lambda e, lo=lo, hi=hi, k=k, c0=c0, scl=scl, stg_t=stg_t: e.activation(
                                out=W[:, k, lo:hi], in_=stg_t[:, lo - c0:hi - c0], func=AF.Copy, scale=scl),
                                reads=[stg_b, bgsc, bLP], writes=[bW])
                        else:
                            S.op(V, lambda e, lo=lo, hi=hi, k=k, c0=c0, scl=scl, stg_t=stg_t: e.tensor_scalar(
                                out=W[:, k, lo:hi], in0=stg_t[:, lo - c0:hi - c0], scalar1=scl, scalar2=None, op0=ALU.mult),
                                reads=[stg_b, bgsc, bLP], writes=[bW])
            for k in range(8):
                for hf in range(2):
                    stg_t, stg_b = stgs[si % 5]
                    si += 1
                    S.dma("sync", lambda e, l=l, k=k, hf=hf, stg_t=stg_t: e.dma_start(
                        out=stg_t[:, 0:512], in_=w_out[l, k * 128:(k + 1) * 128, hf * 512:(hf + 1) * 512]), writes=[stg_b])
                    if si % 2 == 0:
                        S.op(A, lambda e, k=k, hf=hf, stg_t=stg_t: e.activation(
                            out=Wo[:, k, hf * 512:(hf + 1) * 512], in_=stg_t[:, 0:512], func=AF.Copy),
                            reads=[stg_b], writes=[bWo])
                    else:
                        S.op(V, lambda e, k=k, hf=hf, stg_t=stg_t: e.tensor_copy(
                            out=Wo[:, k, hf * 512:(hf + 1) * 512], in_=stg_t[:, 0:512]),
                            reads=[stg_b], writes=[bWo])
            S.op(G, lambda e: e.memset(Sm[:], 0.0), writes=bSm)
            S.op(G, lambda e: e.memset(Smb[:], 0.0), writes=bSmb)
            S.op(G, lambda e: e.memset(Sg[:], 0.0), writes=[bSg])
            S.op(G, lambda e: e.memset(Sgb[:], 0.0), writes=[bSgb])
            for i in range(2):
                S.op(G, lambda e, i=i: e.memset(cin[i][0][:], 0.0), writes=[cin[i][1]])

            drive((front(l, 0), 1))
            drive((b1(l, 0), 1), (front(l, 1) if NT > 1 else None, 1))
            for t in range(NT):
                drive((b2(l, t), 1),
                      (front(l, t + 2) if t + 2 < NT else None, FRONT_EVERY),
                      (b1(l, t + 1) if t + 1 < NT else None, B1_EVERY))
        S.finish("sync")
        S.run()
    return nc


def _consts():
    c = np.zeros((128, NCONST), np.float32)
    j = np.arange(128)[:, None]
    t = np.arange(128)[None, :]
    c[:, K_ID:K_ID + 128] = (j == t)
    le = (j <= t).astype(np.float32)
    gt = (j > t).astype(np.float32)
    c[:, K_TN1:K_TN1 + 128] = -le
    c[:, K_TP1:K_TP1 + 128] = le
    c[:, K_SU1:K_SU1 + 128] = -gt
    c[:, K_TN16:K_TN16 + 128] = -le / 16.0
    c[:, K_TP16:K_TP16 + 128] = le / 16.0
    c[:, K_SU16:K_SU16 + 128] = -gt / 16.0
    c[:, K_CAUS:K_CAUS + 512] = np.tile(le, (1, 4))
    c[0:64, K_BDM:K_BDM + 65] = 1.0
    c[64:128, K_BDM + 65:K_BDM + 130] = 1.0
    for i in range(4):
        c[32 * i:32 * i + 32, K_BDG + 64 * i:K_BDG + 64 * i + 64] = 1.0
        c[32 * i:32 * i + 32, K_HM + i] = 1.0
    return c


def _host_layout(norm_g, b_gates, conv_w, w_alpha, b_alpha, rel_bias, ml_norm_g, gla_norm_g, final_g):
    L = norm_g.shape[0]
    lp = np.zeros((L, 128, NLP), np.float32)
    relb = np.zeros((L, 128, 6, 256), np.float32)
    k = np.arange(128)[:, None]
    q = np.arange(128)[None, :]
    for l in range(L):
        lp[l, :, P_G:P_G + 8] = norm_g[l].reshape(8, 128).T
        lp[l, :, P_CW:P_CW + 24] = conv_w[l].reshape(4, 6, 128).transpose(2, 1, 0).reshape(128, 24)
        lp[l, :, P_BG:P_BG + 12] = b_gates[l][None, :]
        lp[l, :, P_BA:P_BA + 128] = b_alpha[l][None, :]
        lp[l, :, P_MLG:P_MLG + 384] = ml_norm_g[l][None, :]
        lp[l, :, P_GLG:P_GLG + 256] = gla_norm_g[l][None, :]
        lp[l, :, P_CB:P_CB + 6] = rel_bias[l][:, 256][None, :]
        for jj, dlt in ((0, 128), (1, 0)):
            idx = np.clip(q - k + dlt, -128, 128) + 128
            relb[l, :, :, jj * 128:(jj + 1) * 128] = rel_bias[l][:, idx].transpose(1, 0, 2)
    fg = np.broadcast_to(final_g[None, :], (128, D)).astype(np.float32).copy()
    return lp, relb.reshape(L, 128, 6 * 256), fg


_NC_CACHE = {}


def kernel(x, norm_g, w_in, b_gates, conv_w, w_alpha, b_alpha, rel_bias, ml_norm_g, gla_norm_g, w_out, final_g,
           _dbg=None, _layers=None):
    x = np.asarray(x, np.float32)
    B, S_len, _ = x.shape
    L = int(_layers) if _layers else int(np.asarray(norm_g).shape[0])
    f = lambda a: np.ascontiguousarray(np.asarray(a, np.float32))
    lp, relb, fg = _host_layout(f(norm_g), f(b_gates), f(conv_w), f(w_alpha), f(b_alpha), f(rel_bias),
                                f(ml_norm_g), f(gla_norm_g), f(final_g))
    key = (S_len, L, _dbg)
    if key not in _NC_CACHE:
        _NC_CACHE[key] = build(S_len, L, _dbg)
    nc = _NC_CACHE[key]
    common = {"w_in": f(w_in)[:L], "w_out": f(w_out)[:L], "consts": _consts(), "lp": lp[:L], "walpha": f(w_alpha)[:L],
              "relb": relb[:L], "fgbc": fg}
    in_maps = [dict(common, x=np.ascontiguousarray(x[b])) for b in range(B)]
    res = run_bass_kernel_spmd(nc, in_maps, core_ids=list(range(B)))
    out = np.stack([np.asarray(res.results[b]["y"], np.float32) for b in range(B)], axis=0)
    if _dbg:
        return out, [np.asarray(res.results[b]["dbg"]) for b in range(B)]
    return out
```

```python
import contextlib
import numpy as np
import concourse.bass as bass
import concourse.mybir as mybir
from concourse.bass_utils import run_bass_kernel_spmd

F32 = mybir.dt.float32
BF16 = mybir.dt.bfloat16
ALU = mybir.AluOpType
AF = mybir.ActivationFunctionType
AX = mybir.AxisListType

D = 1024
D_IN = 4252
C_AQ, C_AK, C_AV, C_AG = 0, 384, 768, 1152
C_MQ, C_MK, C_MV, C_MO, C_MI, C_MF, C_MG = 1536, 1920, 2304, 2688, 3072, 3078, 3084
C_GQ, C_GK, C_GV, C_GA, C_GG = 3468, 3596, 3724, 3980, 3996
EPS = 1e-6
LN8 = float(np.log(0.125))
FRONT_EVERY = 1
B1_EVERY = 1
K_ID = 0
K_TN1, K_TP1, K_SU1 = 128, 256, 384
K_TN16, K_TP16, K_SU16 = 512, 640, 768
K_CAUS = 896
K_BDM = 1408
K_BDG = 1538
K_HM = 1794
NCONST = 1798
P_G, P_CW, P_BG, P_BA, P_MLG, P_GLG, P_CB = 0, 8, 32, 44, 172, 556, 812
NLP = 818


class Buf:
    __slots__ = ("name", "w", "rs", "excl")

    def __init__(self, name, excl=False):
        self.name = name
        self.w = None
        self.rs = {}
        self.excl = excl


class Sched:
    ENGS = ("tensor", "vector", "scalar", "gpsimd", "sync")

    def __init__(self, nc, stack, n_dma_sems=20):
        self.nc = nc
        self.prog = {e: [] for e in self.ENGS}
        self.sems = {}
        self.cnt = {}
        for e in self.ENGS:
            self.sems[e] = stack.enter_context(nc.semaphore("sem_" + e))
            self.cnt[e] = 0
        self.dma_sems = []
        for i in range(n_dma_sems):
            key = "d%d" % i
            self.sems[key] = stack.enter_context(nc.semaphore("semd%d" % i))
            self.cnt[key] = 0
            self.dma_sems.append(key)
        self.dma_rr = 0
        self.waited = {e: {} for e in self.ENGS}

    def _need(self, eng, deps, key, val):
        if self.waited[eng].get(key, 0) >= val:
            return
        if deps.get(key, 0) < val:
            deps[key] = val

    def _deps(self, eng, reads, writes):
        deps = {}
        for b in reads:
            if b.w is not None:
                self._need(eng, deps, b.w[0], b.w[1])
        for b in writes:
            if b.w is not None:
                self._need(eng, deps, b.w[0], b.w[1])
            for k, v in b.rs.items():
                self._need(eng, deps, k, v)
        return deps

    def _emit_waits(self, eng, deps):
        for key, val in deps.items():
            sem = self.sems[key]
            self.prog[eng].append(lambda e, sem=sem, val=val: e.wait_ge(sem, val))
            self.waited[eng][key] = val

    def op(self, eng, fn, reads=(), writes=(), same_ok=False):
        xr = [b for b in reads if b.excl]
        if xr:
            reads = [b for b in reads if not b.excl]
            writes = list(writes) + [b for b in xr if b not in writes]
        deps = self._deps(eng, reads, writes)
        if same_ok:
            deps.pop(eng, None)
        self._emit_waits(eng, deps)
        self.cnt[eng] += 1
        val = self.cnt[eng]
        sem = self.sems[eng]
        self.prog[eng].append(lambda e, fn=fn, sem=sem: fn(e).then_inc(sem, 1))
        for b in reads:
            b.rs[eng] = val
        for b in writes:
            b.w = (eng, val)
            b.rs = {}

    def dma(self, eng, fn, reads=(), writes=()):
        key = self.dma_sems[self.dma_rr]
        self.dma_rr = (self.dma_rr + 1) % len(self.dma_sems)
        deps = self._deps(eng, reads, writes)
        if self.cnt[key] > 0:
            self._need(eng, deps, key, self.cnt[key])
        self._emit_waits(eng, deps)
        self.cnt[key] += 16
        val = self.cnt[key]
        sem = self.sems[key]
        self.prog[eng].append(lambda e, fn=fn, sem=sem: fn(e).then_inc(sem, 16))
        for b in reads:
            b.rs[key] = val
        for b in writes:
            b.w = (key, val)
            b.rs = {}

    def finish(self, eng):
        for k in self.dma_sems:
            if self.cnt[k] > 0:
                self.prog[eng].append(lambda e, sem=self.sems[k], val=self.cnt[k]: e.wait_ge(sem, val))

    def run(self):
        with self.nc.Block() as block:
            @block.sync
            def _(e):
                for t in self.prog["sync"]:
                    t(e)

            @block.scalar
            def _(e):
                for t in self.prog["scalar"]:
                    t(e)

            @block.vector
            def _(e):
                for t in self.prog["vector"]:
                    t(e)

            @block.gpsimd
            def _(e):
                for t in self.prog["gpsimd"]:
                    t(e)

            @block.tensor
            def _(e):
                for t in self.prog["tensor"]:
                    t(e)


def build(S_len, n_layers=2, dbg=None):
    NT = S_len // 128
    nc = bass.Bass("TRN2", target_bir_lowering=False)
    x_in = nc.dram_tensor("x", [S_len, D], F32, kind="ExternalInput").ap()
    w_in = nc.dram_tensor("w_in", [n_layers, D, D_IN], F32, kind="ExternalInput").ap()
    w_out = nc.dram_tensor("w_out", [n_layers, D, D], F32, kind="ExternalInput").ap()
    consts = nc.dram_tensor("consts", [128, NCONST], F32, kind="ExternalInput").ap()
    lp = nc.dram_tensor("lp", [n_layers, 128, NLP], F32, kind="ExternalInput").ap()
    walpha = nc.dram_tensor("walpha", [n_layers, 16, 128], F32, kind="ExternalInput").ap()
    relb = nc.dram_tensor("relb", [n_layers, 128, 6 * 256], F32, kind="ExternalInput").ap()
    fgbc = nc.dram_tensor("fgbc", [128, D], F32, kind="ExternalInput").ap()
    y_out = nc.dram_tensor("y", [S_len, D], F32, kind="ExternalOutput").ap()
    x1 = nc.dram_tensor("x1s", [S_len, D], F32, kind="Internal").ap()

    with contextlib.ExitStack() as st:
        S = Sched(nc, st)

        def sb(name, shape, dt):
            return st.enter_context(nc.sbuf_tensor(name, shape, dt)), Buf(name)

        def sb2(name, shape, dt, n=2):
            return [sb("%s_%d" % (name, i), shape, dt) for i in range(n)]

        def ps(name, shape, dt):
            return st.enter_context(nc.psum_tensor(name, shape, dt)), Buf(name, excl=True)

        NR = 6
        KC = K_CAUS
        W, bW = sb("W", [128, 8, D_IN], BF16)
        Wo, bWo = sb("Wo", [128, 8, D], BF16)
        SW = 532
        stg = [sb("stg%d" % i, [128, SW], F32) for i in range(2)]
        CT, bCT = sb("CT", [128, NCONST - KC], F32)
        idb, bidb = sb("idb", [128, 128], BF16)
        CTb, bCTb = sb("CTb", [128, 896], BF16)
        LFh, bLFh = sb("LFh", [128, 384], BF16)
        LFl, bLFl = sb("LFl", [128, 384], BF16)
        IGh, bIGh = sb("IGh", [128, 384], BF16)
        IGl, bIGl = sb("IGl", [128, 384], BF16)
        lghs = sb2("lgh", [128, 128], BF16)
        lgls = sb2("lgl", [128, 128], BF16)
        ktp, bktp = sb("ktp", [128, 128], BF16)
        WAh, bWAh = sb("WAh", [16, 128], BF16)
        WAl, bWAl = sb("WAl", [16, 128], BF16)
        gah, bgah = sb("gah", [16, 128], BF16)
        gal, bgal = sb("gal", [16, 128], BF16)
        LP, bLP = sb("LP", [128, NLP], F32)
        gsc, bgsc = sb("gsc", [128, 16], F32)
        bgc, bbgc = sb("bgc", [128, 12], F32)
        WA, bWA = sb("WA", [16, 128], F32)
        RBh, bRBh = sb("RBh", [128, 6, 256], BF16)
        RBl, bRBl = sb("RBl", [128, 6, 256], BF16)
        M0, bM0 = sb("M0", [128, 128], BF16)
        FG, bFG = sb("FG", [128, D], F32)
        xt = sb2("xt", [128, D], F32, 3)
        hb, bhb = sb("hb", [128, D], BF16)
        hT, bhT = sb("hT", [128, 8, 128], BF16)
        ss, bss = sb("ss", [128, 8], F32)
        ss2, bss2 = sb("ss2", [128, 8], F32)
        QT = sb2("QT", [128, 6, 128], BF16)
        KTr, _ = sb("KTr", [128, 3, NR * 128], BF16)
        bKT = [Buf("KT%d" % i) for i in range(NR)]
        Vr, _ = sb("Vr", [128, NR, 6, 65], BF16)
        bVr = [Buf("Vr%d" % i) for i in range(NR)]
        gA = sb2("gA", [128, 384], BF16)
        gB = sb2("gB", [128, 384], BF16, 3)
        gC = sb2("gC", [128, 256], BF16, 3)
        sg1, bsg1 = sb("sg1", [128, 384], F32)
        sg2, bsg2 = sb("sg2", [128, 384], F32)
        sg3, bsg3 = sb("sg3", [128, 384], F32)
        zraw, bzraw = sb("zraw", [128, 384], F32)
        Vm = sb2("Vm", [128, 6, 65], BF16, 3)
        ift = sb2("ift", [128, 16], F32)
        lfp = sb2("lfp", [128, 8], F32)
        cin = sb2("cin", [128, 6, 132], BF16)
        czb, bczb = sb("czb", [128, 768], BF16)
        DG, bDG = sb("DG", [128, 24, 128], BF16)
        csg, bcsg = sb("csg", [128, 6, 128], F32)
        gsps = sb2("gsp", [128, 4, 8], BF16)
        gqT = sb2("gqT", [128, 128], F32)
        gkT = sb2("gkT", [128, 128], F32)
        gkt = sb2("gkt", [128, 128], F32)
        gvb = sb2("gvb", [128, 256], BF16, 3)
        gaT = sb2("gaT", [16, 128], F32)
        lag, blag = sb("lag", [128, 128], F32)
        E3, bE3 = sb("E3", [128, 384], F32)
        ktl, bktl = sb("ktl", [128, 4, 128], BF16)
        Kp, bKp = sb("Kp", [128, 128], BF16)
        Sm, _ = sb("Sm", [128, 3, 130], F32)
        Smb, _ = sb("Smb", [128, 3, 130], BF16)
        bSm = [Buf("Sm%d" % i) for i in range(3)]
        bSmb = [Buf("Smb%d" % i) for i in range(3)]
        Sg, bSg = sb("Sg", [128, 256], F32)
        Sgb, bSgb = sb("Sgb", [128, 256], BF16)
        PT, bPT = sb("PT", [128, 640], BF16)
        SCA = dict(E3=(E3, bE3), ktl=(ktl, bktl), ktp=(ktp, bktp), Kp=(Kp, bKp))
        SCB = dict(E3=sb("E3b", [128, 256], F32), ktl=sb("ktlb", [128, 1, 128], BF16),
                   ktp=sb("ktpb", [128, 128], BF16), Kp=sb("Kpb", [128, 128], BF16))
        def handoff(tag, nptm, nts):
            return [dict(qtl=sb("hq%s%d" % (tag, i), [128, 128], BF16), PTm=sb("hp%s%d" % (tag, i), [128, nptm], BF16),
                         tS=sb("ht%s%d" % (tag, i), [128, nts], F32), eb=sb("he%s%d" % (tag, i), [128, 2], F32))
                    for i in range(2)]
        HO = [handoff("g", 512, 256)] + [handoff("m%d" % i, 256, 130) for i in range(3)]
        rec, brec = sb("rec", [128, 8], F32)
        reca, breca = sb("reca", [128, 8], F32)
        ya, bya = sb("ya", [128, 384], F32)
        hml, bhml = sb("hml", [128, 384], F32)
        hgl, bhgl = sb("hgl", [128, 256], F32)
        sq, bsq = sb("sq", [128, 384], F32)
        st6, bst6 = sb("st6", [128, 32], F32)
        ybfs = sb2("ybf", [128, D], BF16)
        yT, byT = sb("yT", [128, 8, 128], BF16)

        pT, bpT = ps("pT", [128, 8, 128], BF16)
        pA, bpA = ps("pA", [128, 512], F32)
        pB, bpB = ps("pB", [128, 512], F32)
        pS, bpS = ps("pS", [128, 1024], F32)
        pO, bpO = ps("pO", [128, 512], F32)
        pOa, bpOa = ps("pOa", [128, 512], F32)
        pR, bpR = ps("pR", [128, 512], F32)
        pD, bpD = pR, bpR

        bx1 = [Buf("x1_%d" % t) for t in range(NT)]
        by = [Buf("y_%d" % t) for t in range(NT)]

        V, A, G, PE = "vector", "scalar", "gpsimd", "tensor"

        S.dma("sync", lambda e: e.dma_start(out=CT[:], in_=consts[:, KC:NCONST]), writes=[bCT])
        S.dma("sync", lambda e: e.dma_start(out=FG[:], in_=fgbc[:, :]), writes=[bFG])
        for hf in range(2):
            stg_t, stg_b = stg[hf]
            S.dma("sync", lambda e, hf=hf, stg_t=stg_t: e.dma_start(out=stg_t[:, 0:448], in_=consts[:, hf * 448:(hf + 1) * 448]),
                  writes=[stg_b])
            S.op(V, lambda e, hf=hf, stg_t=stg_t: e.tensor_copy(out=CTb[:, hf * 448:(hf + 1) * 448], in_=stg_t[:, 0:448]),
                 reads=[stg_b], writes=[bCTb])
        S.op(V, lambda e: e.tensor_copy(out=idb[:], in_=CTb[:, K_ID:K_ID + 128]), reads=[bCTb], writes=[bidb])
        S.op(G, lambda e: e.memset(Vr[:], 1.0), writes=bVr)
        S.op(G, lambda e: e.memset(M0[:], 0.0), writes=[bM0])
        for i in range(2):
            S.op(G, lambda e, i=i: e.memset(QT[i][0][:], 0.0), writes=[QT[i][1]])
        S.op(G, lambda e: e.memset(M0[0:64, 64:128], -30000.0), writes=[bM0])
        for i in range(3):
            S.op(G, lambda e, i=i: e.memset(Vm[i][0][:], 1.0), writes=[Vm[i][1]])
        for i in range(2):
            S.op(G, lambda e, i=i: e.memset(ift[i][0][:], 0.0), writes=[ift[i][1]])
        S.op(G, lambda e: e.memset(st6[:], 1.0), writes=[bst6])

        def sigmoid_chain(src_ap, src_bufs, out_ap, out_bufs, n):
            S.op(A, lambda e: e.activation(out=sg1[:, 0:n], in_=src_ap, func=AF.Exp, scale=-1.0),
                 reads=src_bufs, writes=[bsg1])
            S.op(A, lambda e: e.activation(out=sg1[:, 0:n], in_=sg1[:, 0:n], func=AF.Ln, bias=1.0),
                 reads=[bsg1], writes=[bsg1])
            S.op(A, lambda e: e.activation(out=out_ap, in_=sg1[:, 0:n], func=AF.Exp, scale=-1.0),
                 reads=[bsg1], writes=out_bufs)

        def rsqrt_small(src_ap, dst_ap, buf, scale, n):
            S.op(V, lambda e: e.tensor_scalar(out=dst_ap, in0=src_ap, scalar1=scale, scalar2=EPS,
                                              op0=ALU.mult, op1=ALU.add), reads=[buf], writes=[buf])
            S.op(A, lambda e: e.activation(out=dst_ap, in_=dst_ap, func=AF.Ln), reads=[buf], writes=[buf])
            S.op(A, lambda e: e.activation(out=dst_ap, in_=dst_ap, func=AF.Exp, scale=-0.5), reads=[buf], writes=[buf])

        def mixer_prep(qT_ap, q_bufs, kT_ap, k_bufs, ktok, la_ap, la_bufs, ig_ap, ig_bufs, kc, Vap, V_bufs, nv,
                       heads, bdm_ap, sc, ho):
            TN, TP, SU = kc
            (E3, bE3), (ktl, bktl), (ktp, bktp), (Kp, bKp) = sc["E3"], sc["ktl"], sc["ktp"], sc["Kp"]
            (qtl, bqtl), (PTm, bPTm), (tS, btS), (eb, beb) = ho["qtl"], ho["PTm"], ho["tS"], ho["eb"]
            la_hi, la_lo = la_ap
            terms = [(la_hi, TN), (la_lo, TN)]
            for i, (lx, cc) in enumerate(terms):
                S.op(PE, lambda e, lx=lx, cc=cc, i=i: e.matmul(pR[:, 0:128], lhsT=lx, rhs=CTb[:, cc:cc + 128],
                                                             start=(i == 0), stop=(i == 1)),
                     reads=la_bufs + [bCTb], writes=[bpR], same_ok=(i > 0))
            terms = [(la_hi, TP), (la_lo, TP)]
            if ig_ap is not None:
                terms += [(ig_ap[0], K_ID), (ig_ap[1], K_ID)]
            for i, (lx, cc) in enumerate(terms):
                S.op(PE, lambda e, lx=lx, cc=cc, i=i, n=len(terms): e.matmul(
                    pR[:, 128:256], lhsT=lx, rhs=CTb[:, cc:cc + 128], start=(i == 0), stop=(i == n - 1)),
                    reads=la_bufs + ig_bufs + [bCTb], writes=[bpR], same_ok=True)
            ne = 256
            if ktok is not None:
                ne = 384
                for i, lx in enumerate((la_hi, la_lo)):
                    S.op(PE, lambda e, lx=lx, i=i: e.matmul(pR[:, 256:384], lhsT=CTb[:, SU:SU + 128], rhs=lx,
                                                            start=(i == 0), stop=(i == 1)),
                         reads=la_bufs + [bCTb], writes=[bpR], same_ok=True)
            S.op(A, lambda e: e.activation(out=E3[:, 0:ne], in_=pR[:, 0:ne], func=AF.Exp), reads=[bpR], writes=[bE3])
            yield
            S.op(V, lambda e: e.tensor_tensor(out=qtl[:], in0=qT_ap, in1=E3[:, 0:128], op=ALU.mult),
                 reads=q_bufs + [bE3], writes=[bqtl])
            nh = len(heads)
            if nh == 2:
                S.op(V, lambda e: e.tensor_tensor(out=ktl[:, 0, :], in0=kT_ap, in1=E3[:, 128:256], op=ALU.mult),
                     reads=k_bufs + [bE3], writes=[bktl])
            else:
                for hh in range(nh):
                    S.op(V, lambda e, hh=hh: e.scalar_tensor_tensor(
                        out=ktl[:, hh, :], in0=kT_ap, scalar=CT[:, K_HM - KC + hh:K_HM - KC + hh + 1], in1=E3[:, 128:256],
                        op0=ALU.mult, op1=ALU.mult), reads=k_bufs + [bE3, bCT], writes=[bktl])
            if ktok is None:
                S.op(V, lambda e: e.scalar_tensor_tensor(out=ktp[:], in0=kT_ap, scalar=E3[:, 127:128], in1=E3[:, 128:256],
                                                         op0=ALU.mult, op1=ALU.mult), reads=k_bufs + [bE3], writes=[bktp])
            else:
                S.op(V, lambda e: e.tensor_tensor(out=Kp[:], in0=ktok[0], in1=E3[:, 256:384], op=ALU.mult),
                     reads=ktok[1] + [bE3], writes=[bKp])
            for hh, (r0, nr, vc0, vcn) in enumerate(heads):
                if nh == 2:
                    S.op(PE, lambda e, hh=hh, r0=r0, nr=nr: e.matmul(
                        pS[:, hh * 512:hh * 512 + 128], lhsT=ktl[r0:r0 + nr, 0, :], rhs=qtl[r0:r0 + nr, :],
                        start=True, stop=True), reads=[bktl, bqtl], writes=[bpS], same_ok=(hh > 0))
                else:
                    S.op(PE, lambda e, hh=hh: e.matmul(
                        pS[:, hh * 128:(hh + 1) * 128], lhsT=ktl[:, hh, :], rhs=qtl[:, :],
                        start=True, stop=True), reads=[bktl, bqtl], writes=[bpS], same_ok=(hh > 0))
            if nh == 2:
                S.op(V, lambda e: e.tensor_tensor(
                    out=PTm[:, 0:256].rearrange("p (a b) -> p a b", a=2),
                    in0=pS[:, 0:1024].rearrange("p (a b) -> p a b", a=2)[:, :, 0:128],
                    in1=CT[:, K_CAUS - KC:K_CAUS - KC + 256].rearrange("p (a b) -> p a b", a=2), op=ALU.mult),
                    reads=[bpS, bCT], writes=[bPTm])
            else:
                S.op(V, lambda e: e.tensor_tensor(out=PTm[:, 0:nh * 128], in0=pS[:, 0:nh * 128],
                                                  in1=CT[:, K_CAUS - KC:K_CAUS - KC + nh * 128], op=ALU.mult),
                     reads=[bpS, bCT], writes=[bPTm])
            yield
            if ktok is None:
                S.op(PE, lambda e: e.transpose(out=pT[:, 0, :], in_=ktp[:], identity=idb[:]), reads=[bktp, bidb],
                     writes=[bpT])
                S.op(V, lambda e: e.tensor_copy(out=Kp[:], in_=pT[:, 0, :]), reads=[bpT], writes=[bKp])
            S.op(PE, lambda e: e.matmul(pD[:, 0:nv], lhsT=Kp[:, :], rhs=Vap, start=True, stop=True),
                 reads=[bKp] + V_bufs, writes=[bpD])
            S.op(V, lambda e: e.tensor_tensor(out=tS[:, 0:nv], in0=pD[:, 0:nv], in1=bdm_ap, op=ALU.mult),
                 reads=[bpD, bCT], writes=[btS])
            S.op(V, lambda e: e.tensor_copy(out=eb[:, 0:1], in_=E3[:, 127:128]), reads=[bE3], writes=[beb])
            yield

        def mixer_apply(Vap, V_bufs, nv, heads, Sf, bSf, Sbf, bSbf, o_ap, ho):
            (qtl, bqtl), (PTm, bPTm), (tS, btS), (eb, beb) = ho["qtl"], ho["PTm"], ho["tS"], ho["eb"]
            nh = len(heads)
            S.op(PE, lambda e: e.matmul(o_ap, lhsT=qtl[:, :], rhs=Sbf, start=True, stop=False),
                 reads=[bqtl, bSbf], writes=[bpO])
            for hh, (r0, nr, vc0, vcn) in enumerate(heads):
                S.op(PE, lambda e, hh=hh, vc0=vc0, vcn=vcn: e.matmul(
                    o_ap[:, vc0:vc0 + vcn], lhsT=PTm[:, hh * 128:(hh + 1) * 128], rhs=Vap[:, vc0:vc0 + vcn],
                    start=False, stop=(hh == nh - 1)), reads=[bPTm] + V_bufs, writes=[bpO], same_ok=True)
            S.op(V, lambda e: e.scalar_tensor_tensor(out=Sf, in0=Sf, scalar=eb[:, 0:1], in1=tS[:, 0:nv],
                                                     op0=ALU.mult, op1=ALU.add),
                 reads=[bSf, beb, btS], writes=[bSf])
            S.op(A, lambda e: e.activation(out=Sbf, in_=Sf, func=AF.Copy), reads=[bSf], writes=[bSbf])

        def front(l, t):
            d, d3 = t % 2, t % 3
            slot = t % NR
            xt_t, xt_b = xt[d3]
            QT_t, bQT = QT[d]
            gqT_t, bgqT = gqT[d]
            gkT_t, bgkT = gkT[d]
            gkt_t, bgkt = gkt[d]
            gvb_t, bgvb = gvb[d3]
            gaT_t, bgaT = gaT[d]
            lgh, blgh = lghs[d]
            lgl, blgl = lgls[d]
            cin_t, bcin = cin[d]
            cin_p, bcin_p = cin[1 - d]
            Vm_t, bVm = Vm[d3]
            ift_t, bift = ift[d]
            lfp_t, blfp = lfp[d]
            gsp, bgsp = gsps[d]
            gA_t, bgA = gA[d]
            gB_t, bgB = gB[d3]
            gC_t, bgC = gC[d3]
            src = x_in if l == 0 else x1
            rd = [] if l == 0 else [bx1[t]]
            if t < 3:
                S.dma("sync", lambda e: e.dma_start(out=xt_t[:], in_=src[t * 128:(t + 1) * 128, :]), reads=rd, writes=[xt_b])
            S.op(V, lambda e: e.scalar_tensor_tensor(
                out=hb[:], in0=xt_t[:], scalar=1.0, in1=xt_t[:], op0=ALU.mult, op1=ALU.mult,
                accum_out=ss[:, 0:1]), reads=[xt_b], writes=[bhb, bss])
            rsqrt_small(ss[:, 0:1], ss[:, 1:2], bss, 1.0 / D, 1)
            S.op(A, lambda e: e.activation(out=hb[:], in_=xt_t[:], func=AF.Copy, scale=ss[:, 1:2]),
                 reads=[xt_b, bss], writes=[bhb])
            S.op(G, lambda e: e.tensor_copy(out=cin_t[:, :, 0:3], in_=cin_p[:, :, 128:131]), reads=[bcin_p], writes=[bcin])
            yield
            for k in range(8):
                S.op(PE, lambda e, k=k: e.transpose(out=pT[:, k, :], in_=hb[:, k * 128:(k + 1) * 128],
                                                    identity=idb[:]), reads=[bhb, bidb], writes=[bpT],
                     same_ok=(k > 0))
            S.op(V, lambda e: e.tensor_copy(out=hT[:], in_=pT[:]), reads=[bpT], writes=[bhT])
            yield

            def proj_fm(pX, bpX, cols):
                first = True
                for i, (c0, cn) in enumerate(cols):
                    for k in range(8):
                        S.op(PE, lambda e, i=i, c0=c0, cn=cn, k=k: e.matmul(
                            pX[0:cn, i * 128:(i + 1) * 128], lhsT=W[:, k, c0:c0 + cn], rhs=hT[:, k, :],
                            start=(k == 0), stop=(k == 7)), reads=[bW, bhT], writes=[bpX], same_ok=not first)
                        first = False

            def proj_tm(pX, bpX, c0, cn):
                for k in range(8):
                    S.op(PE, lambda e, k=k: e.matmul(
                        pX[:, 0:cn], lhsT=hT[:, k, :], rhs=W[:, k, c0:c0 + cn],
                        start=(k == 0), stop=(k == 7)), reads=[bW, bhT], writes=[bpX], same_ok=(k > 0))

            def ev_g1(pX, bpX):
                for half in range(2):
                    S.op(A, lambda e, half=half: e.activation(
                        out=QT_t[half * 64:(half + 1) * 64, :, :].rearrange("p (a two) b -> p a two b", two=2)[:, :, half, :],
                        in_=pX[half * 64:(half + 1) * 64, 0:384].rearrange("p (a b) -> p a b", a=3), func=AF.Copy),
                        reads=[bpX], writes=[bQT])
                S.op(A, lambda e: e.activation(out=gqT_t[:], in_=pX[:, 384:512], func=AF.Copy), reads=[bpX], writes=[bgqT])

            def ev_g2(pX, bpX):
                S.op(A, lambda e: e.activation(
                    out=KTr[:, :, slot * 128:(slot + 1) * 128], in_=pX[:, 0:384].rearrange("p (a b) -> p a b", a=3),
                    func=AF.Copy), reads=[bpX], writes=[bKT[slot]])
                S.op(A, lambda e: e.activation(out=gkT_t[:], in_=pX[:, 384:512], func=AF.Copy), reads=[bpX], writes=[bgkT])

            def ev_g3(pX, bpX):
                S.op(A, lambda e: e.activation(out=cin_t[:, 0:4, 3:131], in_=pX[:, 0:512].rearrange("p (a b) -> p a b", a=4),
                                               func=AF.Copy), reads=[bpX], writes=[bcin])

            def ev_g4(pX, bpX):
                S.op(A, lambda e: e.activation(out=cin_t[:, 4:6, 3:131], in_=pX[:, 0:256].rearrange("p (a b) -> p a b", a=2),
                                               func=AF.Copy), reads=[bpX], writes=[bcin])
                S.op(A, lambda e: e.activation(out=gaT_t[:], in_=pX[0:16, 256:384], func=AF.Copy), reads=[bpX], writes=[bgaT])

            def post_g4():
                S.op(G, lambda e: e.tensor_copy(out=gah[:], in_=gaT_t[:]), reads=[bgaT], writes=[bgah])
                S.op(G, lambda e: e.tensor_tensor(out=gal[:], in0=gaT_t[:], in1=gah[:], op=ALU.subtract), reads=[bgaT, bgah],
                     writes=[bgal])
                for i, (aa, ww) in enumerate(((gah, WAh), (gal, WAh), (gah, WAl))):
                    S.op(PE, lambda e, aa=aa, ww=ww, i=i: e.matmul(pR[:, 384:512], lhsT=aa[:, :], rhs=ww[:, :],
                                                                   start=(i == 0), stop=(i == 2)),
                         reads=[bgah, bgal, bWAh, bWAl], writes=[bpR], same_ok=(i > 0))
                S.op(V, lambda e: e.tensor_tensor(out=lag[:], in0=pR[:, 384:512], in1=LP[:, P_BA:P_BA + 128], op=ALU.add),
                     reads=[bpR, bLP], writes=[blag])
                S.op(A, lambda e: e.activation(out=lag[:], in_=lag[:], func=AF.Exp, scale=-1.0), reads=[blag], writes=[blag])
                S.op(A, lambda e: e.activation(out=lag[:], in_=lag[:], func=AF.Ln, bias=1.0), reads=[blag], writes=[blag])
                S.op(G, lambda e: e.tensor_copy(out=lgh[:], in_=lag[:]), reads=[blag], writes=[blgh])
                S.op(G, lambda e: e.tensor_tensor(out=lgl[:], in0=lag[:], in1=lgh[:], op=ALU.subtract), reads=[blag, blgh],
                     writes=[blgl])

            def ev_av(pX, bpX):
                S.op(A, lambda e: e.activation(
                    out=Vr[:, slot, :, 0:64], in_=pX[:, 0:384].rearrange("p (a b) -> p a b", a=6), func=AF.Copy),
                    reads=[bpX], writes=[bVr[slot]])

            def ev_ag(pX, bpX):
                S.op(A, lambda e: e.activation(out=zraw[:], in_=pX[:, 0:384], func=AF.Copy), reads=[bpX], writes=[bzraw])
                sigmoid_chain(pX[:, 0:384], [bpX], sg2[:], [bsg2], 384)

            def post_ag():
                S.op(G, lambda e: e.tensor_tensor(out=gA_t[:], in0=zraw[:], in1=sg2[:], op=ALU.mult),
                     reads=[bzraw, bsg2], writes=[bgA])

            def ev_mv(pX, bpX):
                S.op(A, lambda e: e.activation(out=Vm_t[:, :, 0:64], in_=pX[:, 0:384].rearrange("p (a b) -> p a b", a=6),
                                               func=AF.Copy), reads=[bpX], writes=[bVm])

            def ev_mo(pX, bpX):
                S.op(V, lambda e: e.tensor_tensor(out=ift_t[:].rearrange("p (a b) -> p a b", a=2)[:, :, 0:6],
                                                  in0=pX[:, 384:396].rearrange("p (a b) -> p a b", a=2),
                                                  in1=bgc[:].rearrange("p (a b) -> p a b", a=2), op=ALU.add),
                     reads=[bpX, bbgc], writes=[bift])
                S.op(A, lambda e: e.activation(out=lfp_t[:], in_=ift_t[:, 8:16], func=AF.Exp, scale=-1.0), reads=[bift],
                     writes=[blfp])
                S.op(A, lambda e: e.activation(out=lfp_t[:], in_=lfp_t[:], func=AF.Ln, bias=1.0), reads=[blfp], writes=[blfp])
                sigmoid_chain(pX[:, 0:384], [bpX], sg3[:], [bsg3], 384)

            def post_mo():
                for gi, (src_ap, src_b) in enumerate(((lfp_t[:, 0:6], blfp), (ift_t[:, 0:6], bift))):
                    S.op(V, lambda e, gi=gi, src_ap=src_ap: e.tensor_copy(out=gsp[:, 2 * gi, 0:6], in_=src_ap),
                         reads=[src_b], writes=[bgsp])
                    S.op(V, lambda e, gi=gi, src_ap=src_ap: e.tensor_tensor(out=gsp[:, 2 * gi + 1, 0:6], in0=src_ap,
                                                                           in1=gsp[:, 2 * gi, 0:6], op=ALU.subtract),
                         reads=[src_b, bgsp], writes=[bgsp])
                S.op(G, lambda e: e.tensor_tensor(out=sg3[:], in0=sg3[:], in1=LP[:, P_MLG:P_MLG + 384], op=ALU.mult),
                     reads=[bsg3, bLP], writes=[bsg3])

            def ev_mg(pX, bpX):
                S.op(A, lambda e: e.activation(out=zraw[:], in_=pX[:, 0:384], func=AF.Copy), reads=[bpX], writes=[bzraw])
                sigmoid_chain(pX[:, 0:384], [bpX], sg2[:], [bsg2], 384)

            def post_mg():
                S.op(G, lambda e: e.tensor_tensor(out=sg2[:], in0=zraw[:], in1=sg2[:], op=ALU.mult),
                     reads=[bzraw, bsg2], writes=[bsg2])
                S.op(G, lambda e: e.tensor_tensor(out=gB_t[:], in0=sg3[:], in1=sg2[:], op=ALU.mult),
                     reads=[bsg3, bsg2], writes=[bgB])

            def ev_gkv(pX, bpX):
                S.op(A, lambda e: e.activation(out=gkt_t[:], in_=pX[:, 0:128], func=AF.Copy), reads=[bpX], writes=[bgkt])
                S.op(A, lambda e: e.activation(out=gvb_t[:], in_=pX[:, 128:384], func=AF.Copy), reads=[bpX], writes=[bgvb])

            def ev_gg(pX, bpX):
                S.op(A, lambda e: e.activation(out=zraw[:, 0:256], in_=pX[:, 0:256], func=AF.Copy), reads=[bpX],
                     writes=[bzraw])
                sigmoid_chain(pX[:, 0:256], [bpX], sg2[:, 0:256], [bsg2], 256)

            def post_gg():
                S.op(G, lambda e: e.tensor_tensor(out=sg2[:, 0:256], in0=zraw[:, 0:256], in1=sg2[:, 0:256], op=ALU.mult),
                     reads=[bzraw, bsg2], writes=[bsg2])
                S.op(G, lambda e: e.tensor_tensor(out=gC_t[:], in0=sg2[:, 0:256], in1=LP[:, P_GLG:P_GLG + 256], op=ALU.mult),
                     reads=[bsg2, bLP], writes=[bgC])

            FM = lambda cols: (lambda pX, bpX: proj_fm(pX, bpX, cols))
            TM = lambda c0, cn: (lambda pX, bpX: proj_tm(pX, bpX, c0, cn))
            groups = [
                (FM([(C_AQ, 128), (C_AQ + 128, 128), (C_AQ + 256, 128), (C_GQ, 128)]), ev_g1, None),
                (FM([(C_AK, 128), (C_AK + 128, 128), (C_AK + 256, 128), (C_GK, 128)]), ev_g2, None),
                (FM([(C_MQ, 128), (C_MQ + 128, 128), (C_MQ + 256, 128), (C_MK, 128)]), ev_g3, None),
                (FM([(C_MK + 128, 128), (C_MK + 256, 128), (C_GA, 16)]), ev_g4, post_g4),
                (TM(C_AV, 384), ev_av, None),
                (TM(C_AG, 384), ev_ag, post_ag),
                (TM(C_MV, 384), ev_mv, None),
                (TM(C_MO, 396), ev_mo, post_mo),
                (TM(C_MG, 384), ev_mg, post_mg),
                (TM(C_GK, 384), ev_gkv, None),
                (TM(C_GG, 256), ev_gg, post_gg),
            ]
            banks = ((pA, bpA), (pB, bpB))
            ng = len(groups)
            for k in range(ng + 2):
                if 0 <= k - 2 < ng and groups[k - 2][2] is not None:
                    groups[k - 2][2]()
                if 0 <= k - 1 < ng:
                    groups[k - 1][1](*banks[(k - 1) % 2])
                if k < ng:
                    groups[k][0](*banks[k % 2])
                yield

        def b1(l, t):
            d = t % 2
            QT_t, bQT = QT[d]
            lgh, blgh = lghs[d]
            lgl, blgl = lgls[d]
            cin_t, bcin = cin[d]
            ift_t, bift = ift[d]
            lfp_t, blfp = lfp[d]
            gsp, bgsp = gsps[d]
            gA_t, bgA = gA[d]
            ybf, bybf = ybfs[d]
            gqT_t, bgqT = gqT[d]
            gkT_t, bgkT = gkT[d]
            gkt_t, bgkt = gkt[d]
            gvb_t, bgvb = gvb[t % 3]
            Vm_t, bVm = Vm[t % 3]
            pO, bpO = pOa, bpOa
            rec, brec = reca, breca
            j0 = max(0, 4 - t)
            pO3 = pO[:, 0:390].rearrange("p (h c) -> p h c", h=6)
            qkT, bqkT = csg, bcsg

            def att():
                for h in range(6):
                    p, half = h // 2, h % 2
                    r0 = half * 64
                    first = True
                    for j in range(j0, 5):
                        sj = (t - 4 + j) % NR
                        extra = j >= 3 or j == 0
                        S.op(PE, lambda e, j=j, sj=sj, p=p, h=h, extra=extra: e.matmul(
                            pS[:, j * 128:(j + 1) * 128], lhsT=KTr[:, p, sj * 128:(sj + 1) * 128],
                            rhs=QT_t[:, h, :], start=True, stop=not extra),
                            reads=[bKT[sj], bQT], writes=[bpS], same_ok=not first)
                        first = False
                        if j == 0:
                            S.op(PE, lambda e: e.matmul(pS[:, 0:128], lhsT=idb[:], rhs=M0[:], start=False, stop=True),
                                 reads=[bidb, bM0], writes=[bpS], same_ok=True)
                        if j >= 3:
                            S.op(PE, lambda e, j=j, h=h: e.matmul(
                                pS[:, j * 128:(j + 1) * 128], lhsT=idb[:], rhs=RBh[:, h, (j - 3) * 128:(j - 2) * 128],
                                start=False, stop=False), reads=[bidb, bRBh], writes=[bpS], same_ok=True)
                            S.op(PE, lambda e, j=j, h=h: e.matmul(
                                pS[:, j * 128:(j + 1) * 128], lhsT=idb[:], rhs=RBl[:, h, (j - 3) * 128:(j - 2) * 128],
                                start=False, stop=True), reads=[bidb, bRBl], writes=[bpS], same_ok=True)
                    S.op(A, lambda e, h=h: e.activation(
                        out=PT[:, j0 * 128:640], in_=pS[:, j0 * 128:640], func=AF.Exp,
                        bias=LP[:, P_CB + h:P_CB + h + 1]), reads=[bpS, bLP], writes=[bPT])
                    yield
                    for j in range(j0, 5):
                        sj = (t - 4 + j) % NR
                        S.op(PE, lambda e, j=j, sj=sj, h=h: e.matmul(
                            pO[:, h * 65:(h + 1) * 65], lhsT=PT[:, j * 128:(j + 1) * 128], rhs=Vr[:, sj, h, :],
                            start=(j == j0), stop=(j == 4)), reads=[bPT, bVr[sj]], writes=[bpO], same_ok=(j > j0))
                S.op(V, lambda e: e.reciprocal(out=rec[:, 0:6].unsqueeze(2), in_=pO3[:, :, 64:65]), reads=[bpO], writes=[brec])
                S.op(V, lambda e: e.tensor_tensor(
                    out=ya[:].rearrange("p (h c) -> p h c", h=6), in0=pO3[:, :, 0:64],
                    in1=rec[:, 0:6].unsqueeze(2).to_broadcast([128, 6, 64]), op=ALU.mult),
                    reads=[bpO, brec], writes=[bya])
                S.op(G, lambda e: e.tensor_tensor(out=ybf[:, 0:384], in0=ya[:], in1=gA_t[:], op=ALU.mult),
                     reads=[bya, bgA], writes=[bybf])
                yield


            def mprep():
                pG = mixer_prep(gqT_t[:, :], [bgqT], gkT_t[:, :], [bgkT], (gkt_t[:, :], [bgkt]),
                                (lgh[:, :], lgl[:, :]), [blgh, blgl], None, [], (K_TN16, K_TP16, K_SU16),
                                gvb_t[:, :], [bgvb], 256, [(32 * i, 32, 64 * i, 64) for i in range(4)],
                                CT[:, K_BDG - KC:K_BDG - KC + 256], SCA, HO[0][d])

                first = True
                for ct in range(6):
                    for jj in range(4):
                        S.op(PE, lambda e, ct=ct, jj=jj: e.matmul(
                            pS[:, ct * 128:(ct + 1) * 128], lhsT=DG[:, ct * 4 + jj, :], rhs=cin_t[:, ct, jj:jj + 128],
                            start=(jj == 0), stop=(jj == 3)), reads=[bDG, bcin], writes=[bpS], same_ok=not first)
                        first = False
                sflat = csg[:].rearrange("p a b -> p (a b)")
                S.op(A, lambda e: e.activation(out=czb[:], in_=pS[:, 0:768], func=AF.Copy), reads=[bpS], writes=[bczb])
                S.op(A, lambda e: e.activation(out=sflat, in_=pS[:, 0:768], func=AF.Exp, scale=-1.0), reads=[bpS], writes=[bcsg])
                next(pG)
                yield
                S.op(A, lambda e: e.activation(out=sflat, in_=sflat, func=AF.Ln, bias=1.0), reads=[bcsg], writes=[bcsg])
                S.op(A, lambda e: e.activation(out=sflat, in_=sflat, func=AF.Exp, scale=-1.0), reads=[bcsg], writes=[bcsg])
                next(pG)
                yield
                for hf in range(2):
                    S.op(G, lambda e, hf=hf: e.tensor_tensor(out=sflat[:, hf * 384:(hf + 1) * 384],
                                                             in0=czb[:, hf * 384:(hf + 1) * 384],
                                                             in1=sflat[:, hf * 384:(hf + 1) * 384], op=ALU.mult),
                         reads=[bczb, bcsg], writes=[bcsg])
                for gi, (dst_t, dst_b) in enumerate(((LFh, bLFh), (LFl, bLFl), (IGh, bIGh), (IGl, bIGl))):
                    S.op(V, lambda e, gi=gi, dst_t=dst_t: e.tensor_copy(
                        out=dst_t[:].rearrange("p (h c) -> p h c", h=6),
                        in_=gsp[:, gi, 0:6].unsqueeze(2).to_broadcast([128, 6, 64])), reads=[bgsp], writes=[dst_b])
                next(pG)
                yield
                Vmf = Vm_t[:].rearrange("p h c -> p (h c)")
                def mk_pair(pr, sc):
                    return mixer_prep(qkT[:, pr, :], [bqkT], qkT[:, 3 + pr, :], [bqkT], None,
                                      (LFh[:, pr * 128:(pr + 1) * 128], LFl[:, pr * 128:(pr + 1) * 128]), [bLFh, bLFl],
                                      (IGh[:, pr * 128:(pr + 1) * 128], IGl[:, pr * 128:(pr + 1) * 128]), [bIGh, bIGl],
                                      (K_TN1, K_TP1, K_SU1), Vmf[:, pr * 130:(pr + 1) * 130], [bVm], 130,
                                      [(0, 64, 0, 65), (64, 64, 65, 65)], CT[:, K_BDM - KC:K_BDM - KC + 130], sc,
                                      HO[1 + pr][d])

                p0, p1, p2 = mk_pair(0, SCB), mk_pair(1, SCA), mk_pair(2, SCB)
                for g in (p0, p0, p1, p0, p1, p2, p1, p2, p2):
                    next(g)
                    yield

            ga, gm = att(), mprep()
            while ga is not None or gm is not None:
                if ga is not None:
                    try:
                        next(ga)
                    except StopIteration:
                        ga = None
                if gm is not None:
                    try:
                        next(gm)
                    except StopIteration:
                        gm = None
                yield
        def b2(l, t):
            d, d3 = t % 2, t % 3
            xt_t, xt_b = xt[d3]
            gvb_t, bgvb = gvb[d3]
            Vm_t, bVm = Vm[d3]
            gB_t, bgB = gB[d3]
            gC_t, bgC = gC[d3]
            ybf, bybf = ybfs[d]
            pO3 = pO[:, 0:390].rearrange("p (h c) -> p h c", h=6)
            Vmf = Vm_t[:].rearrange("p h c -> p (h c)")
            def gpost():
                S.op(G, lambda e: e.tensor_tensor(out=sq[:, 0:256], in0=hgl[:], in1=hgl[:], op=ALU.mult),
                     reads=[bhgl], writes=[bsq])
                yield
                S.op(V, lambda e: e.tensor_reduce(out=st6[:, 16:20], in_=sq[:, 0:256].rearrange("p (h c) -> p h c", h=4),
                                                  axis=AX.X, op=ALU.add), reads=[bsq], writes=[bst6])
                yield
                rsqrt_small(st6[:, 16:24], st6[:, 24:32], bst6, 1.0 / 64, 8)
                yield
                S.op(V, lambda e: e.tensor_tensor(
                    out=hgl[:].rearrange("p (h c) -> p h c", h=4), in0=hgl[:].rearrange("p (h c) -> p h c", h=4),
                    in1=st6[:, 24:28].unsqueeze(2).to_broadcast([128, 4, 64]), op=ALU.mult),
                    reads=[bhgl, bst6], writes=[bhgl])
                yield
                S.op(G, lambda e: e.tensor_tensor(out=ybf[:, 768:1024], in0=hgl[:], in1=gC_t[:], op=ALU.mult),
                     reads=[bhgl, bgC], writes=[bybf])
                yield

            gq = gpost()

            def adv(g):
                try:
                    next(g)
                except StopIteration:
                    pass

            mixer_apply(gvb_t[:, :], [bgvb], 256, [(32 * i, 32, 64 * i, 64) for i in range(4)], Sg[:, :], bSg,
                        Sgb[:, :], bSgb, pO[:, 0:256], HO[0][d])
            S.op(A, lambda e: e.activation(out=hgl[:], in_=pO[:, 0:256], func=AF.Copy), reads=[bpO], writes=[bhgl])
            yield
            for pr in range(3):
                mixer_apply(Vmf[:, pr * 130:(pr + 1) * 130], [bVm], 130, [(0, 64, 0, 65), (64, 64, 65, 65)],
                            Sm[:, pr, :], bSm[pr], Smb[:, pr, :], bSmb[pr], pO[:, pr * 130:(pr + 1) * 130], HO[1 + pr][d])
                adv(gq)
                adv(gq)
                yield
            for _ in range(6):
                adv(gq)
            S.op(V, lambda e: e.tensor_copy(out=rec[:, 0:6].unsqueeze(2), in_=pO3[:, :, 64:65]), reads=[bpO], writes=[brec])
            S.op(V, lambda e: e.scalar_tensor_tensor(out=rec[:, 0:6], in0=rec[:, 0:6], scalar=-1.0, in1=rec[:, 0:6],
                                                     op0=ALU.mult, op1=ALU.max), reads=[brec], writes=[brec])
            S.op(V, lambda e: e.tensor_scalar(out=rec[:, 0:6], in0=rec[:, 0:6], scalar1=1.0, scalar2=None,
                                              op0=ALU.max), reads=[brec], writes=[brec])
            S.op(V, lambda e: e.reciprocal(out=rec[:, 0:6], in_=rec[:, 0:6]), reads=[brec], writes=[brec])
            S.op(V, lambda e: e.tensor_tensor(
                out=hml[:].rearrange("p (h c) -> p h c", h=6), in0=pO3[:, :, 0:64],
                in1=rec[:, 0:6].unsqueeze(2).to_broadcast([128, 6, 64]), op=ALU.mult),
                reads=[bpO, brec], writes=[bhml])
            S.op(G, lambda e: e.tensor_tensor(out=sq[:], in0=hml[:], in1=hml[:], op=ALU.mult), reads=[bhml], writes=[bsq])
            S.op(V, lambda e: e.tensor_reduce(out=st6[:, 0:6], in_=sq[:].rearrange("p (h c) -> p h c", h=6),
                                              axis=AX.X, op=ALU.add), reads=[bsq], writes=[bst6])
            rsqrt_small(st6[:, 0:8], st6[:, 8:16], bst6, 1.0 / 64, 8)
            S.op(V, lambda e: e.tensor_tensor(
                out=hml[:].rearrange("p (h c) -> p h c", h=6), in0=hml[:].rearrange("p (h c) -> p h c", h=6),
                in1=st6[:, 8:14].unsqueeze(2).to_broadcast([128, 6, 64]), op=ALU.mult),
                reads=[bhml, bst6], writes=[bhml])
            S.op(G, lambda e: e.tensor_tensor(out=ybf[:, 384:768], in0=hml[:], in1=gB_t[:], op=ALU.mult),
                 reads=[bhml, bgB], writes=[bybf])
            yield

            for k in range(8):
                S.op(PE, lambda e, k=k: e.transpose(out=pT[:, k, :], in_=ybf[:, k * 128:(k + 1) * 128],
                                                    identity=idb[:]), reads=[bybf, bidb], writes=[bpT],
                     same_ok=(k > 0))
            S.op(V, lambda e: e.tensor_copy(out=yT[:], in_=pT[:]), reads=[bpT], writes=[byT])
            yield
            for half, (pX, bpX) in enumerate(((pO, bpO), (pR, bpR))):
                for k in range(8):
                    S.op(PE, lambda e, k=k, half=half, pX=pX: e.matmul(
                        pX[:, 0:512], lhsT=yT[:, k, :], rhs=Wo[:, k, half * 512:(half + 1) * 512],
                        start=(k == 0), stop=(k == 7)), reads=[byT, bWo], writes=[bpX], same_ok=(k > 0))
                S.op(V, lambda e, half=half, pX=pX: e.tensor_tensor(
                    out=xt_t[:, half * 512:(half + 1) * 512], in0=pX[:, 0:512],
                    in1=xt_t[:, half * 512:(half + 1) * 512], op=ALU.add), reads=[bpX, xt_b], writes=[xt_b])
                yield
            if l < n_layers - 1:
                S.dma("sync", lambda e: e.dma_start(out=x1[t * 128:(t + 1) * 128, :], in_=xt_t[:]),
                      reads=[xt_b], writes=[bx1[t]])
            else:
                S.op(V, lambda e: e.scalar_tensor_tensor(
                    out=ybf[:], in0=xt_t[:], scalar=1.0, in1=xt_t[:], op0=ALU.mult, op1=ALU.mult,
                    accum_out=ss2[:, 0:1]), reads=[xt_b], writes=[bybf, bss2])
                rsqrt_small(ss2[:, 0:1], ss2[:, 1:2], bss2, 1.0 / D, 1)
                S.op(V, lambda e: e.scalar_tensor_tensor(
                    out=xt_t[:], in0=xt_t[:], scalar=ss2[:, 1:2], in1=FG[:], op0=ALU.mult, op1=ALU.mult),
                    reads=[xt_b, bss2, bFG], writes=[xt_b])
                S.dma("sync", lambda e: e.dma_start(out=y_out[t * 128:(t + 1) * 128, :], in_=xt_t[:]),
                      reads=[xt_b], writes=[by[t]])
            if t + 3 < NT:
                srcp = x_in if l == 0 else x1
                rdp = [] if l == 0 else [bx1[t + 3]]
                S.dma("sync", lambda e: e.dma_start(out=xt_t[:], in_=srcp[(t + 3) * 128:(t + 4) * 128, :]), reads=rdp,
                      writes=[xt_b])
            yield

        def drive(*gens):
            gens = [[g, ev] for (g, ev) in gens if g is not None]
            i = 0
            while gens:
                for ent in list(gens):
                    if i % ent[1] == 0 or len(gens) == 1:
                        try:
                            next(ent[0])
                        except StopIteration:
                            gens.remove(ent)
                i += 1

        for l in range(n_layers):
            S.dma("sync", lambda e, l=l: e.dma_start(out=LP[:], in_=lp[l, :, :]), writes=[bLP])
            S.dma("sync", lambda e, l=l: e.dma_start(out=WA[:], in_=walpha[l, :, :]), writes=[bWA])
            for ch in range(3):
                stg_t, stg_b = stg[ch % 2]
                S.dma("sync", lambda e, l=l, ch=ch, stg_t=stg_t: e.dma_start(out=stg_t[:, 0:512], in_=relb[l, :, ch * 512:(ch + 1) * 512]),
                      writes=[stg_b])
                for hh in range(2):
                    h = 2 * ch + hh
                    seg = stg_t[:, hh * 256:(hh + 1) * 256]
                    S.op(V, lambda e, h=h, seg=seg: e.tensor_scalar(out=seg, in0=seg, scalar1=LP[:, P_CB + h:P_CB + h + 1],
                                                                  scalar2=None, op0=ALU.subtract), reads=[stg_b, bLP], writes=[stg_b])
                    S.op(V, lambda e, h=h, seg=seg: e.tensor_copy(out=RBh[:, h, :], in_=seg), reads=[stg_b], writes=[bRBh])
                    S.op(V, lambda e, h=h, seg=seg: e.tensor_tensor(out=RBl[:, h, :], in0=seg, in1=RBh[:, h, :], op=ALU.subtract),
                         reads=[stg_b, bRBh], writes=[bRBl])
            S.op(G, lambda e: e.memset(RBh[64:128, :, 128:192], -30000.0), writes=[bRBh])
            S.op(G, lambda e: e.memset(RBl[64:128, :, 128:192], 0.0), writes=[bRBl])
            S.op(V, lambda e: e.tensor_scalar(out=gsc[:, 0:8], in0=LP[:, P_G:P_G + 8], scalar1=0.125, scalar2=None,
                                              op0=ALU.mult), reads=[bLP], writes=[bgsc])
            S.op(V, lambda e: e.tensor_scalar(out=gsc[:, 8:16], in0=LP[:, P_G:P_G + 8], scalar1=float(32 ** -0.5),
                                              scalar2=None, op0=ALU.mult), reads=[bLP], writes=[bgsc])
            S.op(V, lambda e: e.tensor_copy(out=bgc[:], in_=LP[:, P_BG:P_BG + 12]), reads=[bLP], writes=[bbgc])
            S.op(V, lambda e: e.tensor_copy(out=WAh[:], in_=WA[:]), reads=[bWA], writes=[bWAh])
            S.op(V, lambda e: e.tensor_tensor(out=WAl[:], in0=WA[:], in1=WAh[:], op=ALU.subtract), reads=[bWA, bWAh],
                 writes=[bWAl])
            S.op(V, lambda e: e.tensor_scalar(out=bgc[:, 0:6], in0=bgc[:, 0:6], scalar1=LN8, scalar2=None,
                                              op0=ALU.add), reads=[bbgc], writes=[bbgc])
            for idx in range(24):
                S.op(V, lambda e, idx=idx: e.tensor_scalar(out=DG[:, idx, :], in0=idb[:], scalar1=LP[:, P_CW + idx:P_CW + idx + 1],
                                                           scalar2=None, op0=ALU.mult), reads=[bidb, bLP], writes=[bDG])
            stgs = [stg[0], stg[1], xt[0], xt[1], xt[2]]
            si = 0
            for k in range(8):
                for c0 in range(0, D_IN, SW):
                    cn = min(SW, D_IN - c0)
                    stg_t, stg_b = stgs[si % 5]
                    si += 1
                    S.dma("sync", lambda e, l=l, k=k, c0=c0, cn=cn, stg_t=stg_t: e.dma_start(
                        out=stg_t[:, 0:cn], in_=w_in[l, k * 128:(k + 1) * 128, c0:c0 + cn]), writes=[stg_b])
                    for (a, b_, sc) in ((0, 384, 0), (384, C_GQ, 1), (C_GQ, C_GK, 2), (C_GK, D_IN, 1)):
                        lo, hi = max(a, c0), min(b_, c0 + cn)
                        if lo >= hi:
                            continue
                        if sc == 0:
                            scl = gsc[:, k:k + 1]
                        elif sc == 2:
                            scl = gsc[:, 8 + k:9 + k]
                        else:
                            scl = LP[:, P_G + k:P_G + k + 1]
                        if si % 2 == 0:
                            S.op(A, lambda e, lo=lo, hi=hi, k=k, c0=c0, scl=scl, stg_t=stg_t: e.activation(
                                out=W[:, k, lo:hi], in_=stg_t[:, lo - c0:hi - c0], func=AF.Copy, scale=scl),
                                reads=[stg_b, bgsc, bLP], writes=[bW])
                        else:
                            S.op(V, lambda e, lo=lo, hi=hi, k=k, c0=c0, scl=scl, stg_t=stg_t: e.tensor_scalar(
                                out=W[:, k, lo:hi], in0=stg_t[:, lo - c0:hi - c0], scalar1=scl, scalar2=None, op0=ALU.mult),
                                reads=[stg_b, bgsc, bLP], writes=[bW])
            for k in range(8):
                for hf in range(2):
                    stg_t, stg_b = stgs[si % 5]
                    si += 1
                    S.dma("sync", lambda e, l=l, k=k, hf=hf, stg_t=stg_t: e.dma_start(
                        out=stg_t[:, 0:512], in_=w_out[l, k * 128:(k + 1) * 128, hf * 512:(hf + 1) * 512]), writes=[stg_b])
                    if si % 2 == 0:
                        S.op(A, lambda e, k=k, hf=hf, stg_t=stg_t: e.activation(
                            out=Wo[:, k, hf * 512:(hf + 1) * 512], in_=stg_t[:, 0:512], func=AF.Copy),
                            reads=[stg_b], writes=[bWo])
                    else:
                        S.op(V, lambda e, k=k, hf=hf, stg_t=stg_t: e.tensor_copy(
                            out=Wo[:, k, hf * 512:(hf + 1) * 512], in_=stg_t[:, 0:512]),
                            reads=[stg_b], writes=[bWo])
            S.op(G, lambda e: e.memset(Sm[:], 0.0), writes=bSm)
            S.op(G, lambda e: e.memset(Smb[:], 0.0), writes=bSmb)
            S.op(G, lambda e: e.memset(Sg[:], 0.0), writes=[bSg])
            S.op(G, lambda e: e.memset(Sgb[:], 0.0), writes=[bSgb])
            for i in range(2):
                S.op(G, lambda e, i=i: e.memset(cin[i][0][:], 0.0), writes=[cin[i][1]])

            drive((front(l, 0), 1))
            drive((b1(l, 0), 1), (front(l, 1) if NT > 1 else None, 1))
            for t in range(NT):
                drive((b2(l, t), 1),
                      (b1(l, t + 1) if t + 1 < NT else None, B1_EVERY),
                      (front(l, t + 2) if t + 2 < NT else None, FRONT_EVERY))
        S.finish("sync")
        S.run()
    return nc


def _consts():
    c = np.zeros((128, NCONST), np.float32)
    j = np.arange(128)[:, None]
    t = np.arange(128)[None, :]
    c[:, K_ID:K_ID + 128] = (j == t)
    le = (j <= t).astype(np.float32)
    gt = (j > t).astype(np.float32)
    c[:, K_TN1:K_TN1 + 128] = -le
    c[:, K_TP1:K_TP1 + 128] = le
    c[:, K_SU1:K_SU1 + 128] = -gt
    c[:, K_TN16:K_TN16 + 128] = -le / 16.0
    c[:, K_TP16:K_TP16 + 128] = le / 16.0
    c[:, K_SU16:K_SU16 + 128] = -gt / 16.0
    c[:, K_CAUS:K_CAUS + 512] = np.tile(le, (1, 4))
    c[0:64, K_BDM:K_BDM + 65] = 1.0
    c[64:128, K_BDM + 65:K_BDM + 130] = 1.0
    for i in range(4):
        c[32 * i:32 * i + 32, K_BDG + 64 * i:K_BDG + 64 * i + 64] = 1.0
        c[32 * i:32 * i + 32, K_HM + i] = 1.0
    return c


def _host_layout(norm_g, b_gates, conv_w, w_alpha, b_alpha, rel_bias, ml_norm_g, gla_norm_g, final_g):
    L = norm_g.shape[0]
    lp = np.zeros((L, 128, NLP), np.float32)
    relb = np.zeros((L, 128, 6, 256), np.float32)
    k = np.arange(128)[:, None]
    q = np.arange(128)[None, :]
    for l in range(L):
        lp[l, :, P_G:P_G + 8] = norm_g[l].reshape(8, 128).T
        lp[l, :, P_CW:P_CW + 24] = conv_w[l].reshape(4, 6, 128).transpose(2, 1, 0).reshape(128, 24)
        lp[l, :, P_BG:P_BG + 12] = b_gates[l][None, :]
        lp[l, :, P_BA:P_BA + 128] = b_alpha[l][None, :]
        lp[l, :, P_MLG:P_MLG + 384] = ml_norm_g[l][None, :]
        lp[l, :, P_GLG:P_GLG + 256] = gla_norm_g[l][None, :]
        lp[l, :, P_CB:P_CB + 6] = rel_bias[l][:, 256][None, :]
        for jj, dlt in ((0, 128), (1, 0)):
            idx = np.clip(q - k + dlt, -128, 128) + 128
            relb[l, :, :, jj * 128:(jj + 1) * 128] = rel_bias[l][:, idx].transpose(1, 0, 2)
    fg = np.broadcast_to(final_g[None, :], (128, D)).astype(np.float32).copy()
    return lp, relb.reshape(L, 128, 6 * 256), fg


_NC_CACHE = {}


def kernel(x, norm_g, w_in, b_gates, conv_w, w_alpha, b_alpha, rel_bias, ml_norm_g, gla_norm_g, w_out, final_g,
           _dbg=None, _layers=None):
    x = np.asarray(x, np.float32)
    B, S_len, _ = x.shape
    L = int(_layers) if _layers else int(np.asarray(norm_g).shape[0])
    f = lambda a: np.ascontiguousarray(np.asarray(a, np.float32))
    lp, relb, fg = _host_layout(f(norm_g), f(b_gates), f(conv_w), f(w_alpha), f(b_alpha), f(rel_bias),
                                f(ml_norm_g), f(gla_norm_g), f(final_g))
    key = (S_len, L, _dbg)
    if key not in _NC_CACHE:
        _NC_CACHE[key] = build(S_len, L, _dbg)
    nc = _NC_CACHE[key]
    common = {"w_in": f(w_in)[:L], "w_out": f(w_out)[:L], "consts": _consts(), "lp": lp[:L], "walpha": f(w_alpha)[:L],
              "relb": relb[:L], "fgbc": fg}
    in_maps = [dict(common, x=np.ascontiguousarray(x[b])) for b in range(B)]
    res = run_bass_kernel_spmd(nc, in_maps, core_ids=list(range(B)))
    out = np.stack([np.asarray(res.results[b]["y"], np.float32) for b in range(B)], axis=0)
    if _dbg:
        return out, [np.asarray(res.results[b]["dbg"]) for b in range(B)]
    return out
```

```python
import contextlib
import numpy as np
import concourse.bass as bass
import concourse.mybir as mybir
from concourse.bass_utils import run_bass_kernel_spmd

F32 = mybir.dt.float32
BF16 = mybir.dt.bfloat16
ALU = mybir.AluOpType
AF = mybir.ActivationFunctionType
AX = mybir.AxisListType

D = 1024
D_IN = 4252
C_AQ, C_AK, C_AV, C_AG = 0, 384, 768, 1152
C_MQ, C_MK, C_MV, C_MO, C_MI, C_MF, C_MG = 1536, 1920, 2304, 2688, 3072, 3078, 3084
C_GQ, C_GK, C_GV, C_GA, C_GG = 3468, 3596, 3724, 3980, 3996
EPS = 1e-6
LN8 = float(np.log(0.125))
FRONT_EVERY = 1
B1_EVERY = 1
K_ID = 0
K_TN1, K_TP1, K_SU1 = 128, 256, 384
K_TN16, K_TP16, K_SU16 = 512, 640, 768
K_CAUS = 896
K_BDM = 1408
K_BDG = 1538
K_HM = 1794
NCONST = 1798
P_G, P_CW, P_BG, P_BA, P_MLG, P_GLG, P_CB = 0, 8, 32, 44, 172, 556, 812
NLP = 818


class Buf:
    __slots__ = ("name", "w", "rs", "excl")

    def __init__(self, name, excl=False):
        self.name = name
        self.w = None
        self.rs = {}
        self.excl = excl


class Sched:
    ENGS = ("tensor", "vector", "scalar", "gpsimd", "sync")

    def __init__(self, nc, stack, n_dma_sems=20):
        self.nc = nc
        self.prog = {e: [] for e in self.ENGS}
        self.sems = {}
        self.cnt = {}
        for e in self.ENGS:
            self.sems[e] = stack.enter_context(nc.semaphore("sem_" + e))
            self.cnt[e] = 0
        self.dma_sems = []
        for i in range(n_dma_sems):
            key = "d%d" % i
            self.sems[key] = stack.enter_context(nc.semaphore("semd%d" % i))
            self.cnt[key] = 0
            self.dma_sems.append(key)
        self.dma_rr = 0
        self.waited = {e: {} for e in self.ENGS}

    def _need(self, eng, deps, key, val):
        if self.waited[eng].get(key, 0) >= val:
            return
        if deps.get(key, 0) < val:
            deps[key] = val

    def _deps(self, eng, reads, writes):
        deps = {}
        for b in reads:
            if b.w is not None:
                self._need(eng, deps, b.w[0], b.w[1])
        for b in writes:
            if b.w is not None:
                self._need(eng, deps, b.w[0], b.w[1])
            for k, v in b.rs.items():
                self._need(eng, deps, k, v)
        return deps

    def _emit_waits(self, eng, deps):
        for key, val in deps.items():
            sem = self.sems[key]
            self.prog[eng].append(lambda e, sem=sem, val=val: e.wait_ge(sem, val))
            self.waited[eng][key] = val

    def op(self, eng, fn, reads=(), writes=(), same_ok=False):
        xr = [b for b in reads if b.excl]
        if xr:
            reads = [b for b in reads if not b.excl]
            writes = list(writes) + [b for b in xr if b not in writes]
        deps = self._deps(eng, reads, writes)
        if same_ok:
            deps.pop(eng, None)
        self._emit_waits(eng, deps)
        self.cnt[eng] += 1
        val = self.cnt[eng]
        sem = self.sems[eng]
        self.prog[eng].append(lambda e, fn=fn, sem=sem: fn(e).then_inc(sem, 1))
        for b in reads:
            b.rs[eng] = val
        for b in writes:
            b.w = (eng, val)
            b.rs = {}

    def dma(self, eng, fn, reads=(), writes=()):
        key = self.dma_sems[self.dma_rr]
        self.dma_rr = (self.dma_rr + 1) % len(self.dma_sems)
        deps = self._deps(eng, reads, writes)
        if self.cnt[key] > 0:
            self._need(eng, deps, key, self.cnt[key])
        self._emit_waits(eng, deps)
        self.cnt[key] += 16
        val = self.cnt[key]
        sem = self.sems[key]
        self.prog[eng].append(lambda e, fn=fn, sem=sem: fn(e).then_inc(sem, 16))
        for b in reads:
            b.rs[key] = val
        for b in writes:
            b.w = (key, val)
            b.rs = {}

    def finish(self, eng):
        for k in self.dma_sems:
            if self.cnt[k] > 0:
                self.prog[eng].append(lambda e, sem=self.sems[k], val=self.cnt[k]: e.wait_ge(sem, val))

    def run(self):
        with self.nc.Block() as block:
            @block.sync
            def _(e):
                for t in self.prog["sync"]:
                    t(e)

            @block.scalar
            def _(e):
                for t in self.prog["scalar"]:
                    t(e)

            @block.vector
            def _(e):
                for t in self.prog["vector"]:
                    t(e)

            @block.gpsimd
            def _(e):
                for t in self.prog["gpsimd"]:
                    t(e)

            @block.tensor
            def _(e):
                for t in self.prog["tensor"]:
                    t(e)


def build(S_len, n_layers=2, dbg=None):
    NT = S_len // 128
    nc = bass.Bass("TRN2", target_bir_lowering=False)
    x_in = nc.dram_tensor("x", [S_len, D], F32, kind="ExternalInput").ap()
    w_in = nc.dram_tensor("w_in", [n_layers, D, D_IN], F32, kind="ExternalInput").ap()
    w_out = nc.dram_tensor("w_out", [n_layers, D, D], F32, kind="ExternalInput").ap()
    consts = nc.dram_tensor("consts", [128, NCONST], F32, kind="ExternalInput").ap()
    lp = nc.dram_tensor("lp", [n_layers, 128, NLP], F32, kind="ExternalInput").ap()
    walpha = nc.dram_tensor("walpha", [n_layers, 16, 128], F32, kind="ExternalInput").ap()
    relb = nc.dram_tensor("relb", [n_layers, 128, 6 * 256], F32, kind="ExternalInput").ap()
    fgbc = nc.dram_tensor("fgbc", [128, D], F32, kind="ExternalInput").ap()
    y_out = nc.dram_tensor("y", [S_len, D], F32, kind="ExternalOutput").ap()
    x1 = nc.dram_tensor("x1s", [S_len, D], F32, kind="Internal").ap()

    with contextlib.ExitStack() as st:
        S = Sched(nc, st)

        def sb(name, shape, dt):
            return st.enter_context(nc.sbuf_tensor(name, shape, dt)), Buf(name)

        def sb2(name, shape, dt, n=2):
            return [sb("%s_%d" % (name, i), shape, dt) for i in range(n)]

        def ps(name, shape, dt):
            return st.enter_context(nc.psum_tensor(name, shape, dt)), Buf(name, excl=True)

        NR = 6
        KC = K_CAUS
        W, bW = sb("W", [128, 8, D_IN], BF16)
        Wo, bWo = sb("Wo", [128, 8, D], BF16)
        SW = 532
        stg = [sb("stg%d" % i, [128, SW], F32) for i in range(2)]
        CT, bCT = sb("CT", [128, NCONST - KC], F32)
        idb, bidb = sb("idb", [128, 128], BF16)
        CTb, bCTb = sb("CTb", [128, 896], BF16)
        LFhs = sb2("LFh", [128, 384], BF16)
        LFls = sb2("LFl", [128, 384], BF16)
        IGhs = sb2("IGh", [128, 384], BF16)
        IGls = sb2("IGl", [128, 384], BF16)
        lghs = sb2("lgh", [128, 128], BF16)
        lgls = sb2("lgl", [128, 128], BF16)
        ktp, bktp = sb("ktp", [128, 128], BF16)
        WAh, bWAh = sb("WAh", [16, 128], BF16)
        WAl, bWAl = sb("WAl", [16, 128], BF16)
        gah, bgah = sb("gah", [16, 128], BF16)
        gal, bgal = sb("gal", [16, 128], BF16)
        LP, bLP = sb("LP", [128, NLP], F32)
        gsc, bgsc = sb("gsc", [128, 16], F32)
        bgc, bbgc = sb("bgc", [128, 12], F32)
        WA, bWA = sb("WA", [16, 128], F32)
        RBh, bRBh = sb("RBh", [128, 6, 256], BF16)
        RBl, bRBl = sb("RBl", [128, 6, 256], BF16)
        M0, bM0 = sb("M0", [128, 128], BF16)
        FG, bFG = sb("FG", [128, D], F32)
        xt = sb2("xt", [128, D], F32, 3)
        hb, bhb = sb("hb", [128, D], BF16)
        hT, bhT = sb("hT", [128, 8, 128], BF16)
        ss, bss = sb("ss", [128, 8], F32)
        ss2, bss2 = sb("ss2", [128, 8], F32)
        QT = sb2("QT", [128, 6, 128], BF16)
        KTr, _ = sb("KTr", [128, 3, NR * 128], BF16)
        bKT = [Buf("KT%d" % i) for i in range(NR)]
        Vr, _ = sb("Vr", [128, NR, 6, 65], BF16)
        bVr = [Buf("Vr%d" % i) for i in range(NR)]
        gA = sb2("gA", [128, 384], BF16)
        gB = sb2("gB", [128, 384], BF16, 3)
        gC = sb2("gC", [128, 256], BF16, 3)
        sg1, bsg1 = sb("sg1", [128, 384], F32)
        sg2, bsg2 = sb("sg2", [128, 384], F32)
        zraw, bzraw = sb("zraw", [128, 384], BF16)
        Vm = sb2("Vm", [128, 6, 65], BF16, 3)
        ift = sb2("ift", [128, 16], F32)
        lfp = sb2("lfp", [128, 8], F32)
        cin = sb2("cin", [128, 6, 132], BF16)
        czb, bczb = sb("czb", [128, 768], BF16)
        DG, bDG = sb("DG", [128, 24, 128], BF16)
        csg, bcsg = sb("csg", [128, 6, 128], F32)
        gsps = sb2("gsp", [128, 4, 8], BF16)
        gqT = sb2("gqT", [128, 128], F32)
        gkT = sb2("gkT", [128, 128], F32)
        gkt = sb2("gkt", [128, 128], F32)
        gvb = sb2("gvb", [128, 256], BF16, 3)
        gaT = sb2("gaT", [16, 128], F32)
        lag, blag = sb("lag", [128, 128], F32)
        E3, bE3 = sb("E3", [128, 384], F32)
        ktl, bktl = sb("ktl", [128, 4, 128], BF16)
        Kp, bKp = sb("Kp", [128, 128], BF16)
        Sm, _ = sb("Sm", [128, 3, 130], F32)
        Smb, _ = sb("Smb", [128, 3, 130], BF16)
        bSm = [Buf("Sm%d" % i) for i in range(3)]
        bSmb = [Buf("Smb%d" % i) for i in range(3)]
        Sg, bSg = sb("Sg", [128, 256], F32)
        Sgb, bSgb = sb("Sgb", [128, 256], BF16)
        PT, bPT = sb("PT", [128, 640], BF16)
        SCA = dict(E3=(E3, bE3), ktl=(ktl, bktl), ktp=(ktp, bktp), Kp=(Kp, bKp))
        SCB = dict(E3=sb("E3b", [128, 256], F32), ktl=sb("ktlb", [128, 1, 128], BF16),
                   ktp=sb("ktpb", [128, 128], BF16), Kp=sb("Kpb", [128, 128], BF16))
        def handoff(tag, nptm, nts):
            return [dict(qtl=sb("hq%s%d" % (tag, i), [128, 128], BF16), PTm=sb("hp%s%d" % (tag, i), [128, nptm], BF16),
                         tS=sb("ht%s%d" % (tag, i), [128, nts], F32), eb=sb("he%s%d" % (tag, i), [128, 2], F32))
                    for i in range(2)]
        HO = [handoff("g", 512, 256)] + [handoff("m%d" % i, 256, 130) for i in range(3)]
        rec, brec = sb("rec", [128, 8], F32)
        reca, breca = sb("reca", [128, 8], F32)
        ya, bya = sb("ya", [128, 384], F32)
        hml, bhml = sb("hml", [128, 384], F32)
        hgl, bhgl = sb("hgl", [128, 256], F32)
        sq, bsq = sb("sq", [128, 384], F32)
        st6, bst6 = sb("st6", [128, 32], F32)
        ybfs = sb2("ybf", [128, D], BF16)
        yT, byT = sb("yT", [128, 8, 128], BF16)

        pT, bpT = ps("pT", [128, 8, 128], BF16)
        pA, bpA = ps("pA", [128, 512], F32)
        pB, bpB = ps("pB", [128, 512], F32)
        pS, bpS = ps("pS", [128, 1024], F32)
        pO, bpO = ps("pO", [128, 512], F32)
        pOa, bpOa = ps("pOa", [128, 512], F32)
        pR, bpR = ps("pR", [128, 512], F32)
        pD, bpD = pR, bpR

        bx1 = [Buf("x1_%d" % t) for t in range(NT)]
        by = [Buf("y_%d" % t) for t in range(NT)]

        V, A, G, PE = "vector", "scalar", "gpsimd", "tensor"

        S.dma("sync", lambda e: e.dma_start(out=CT[:], in_=consts[:, KC:NCONST]), writes=[bCT])
        S.dma("sync", lambda e: e.dma_start(out=FG[:], in_=fgbc[:, :]), writes=[bFG])
        for hf in range(2):
            stg_t, stg_b = stg[hf]
            S.dma("sync", lambda e, hf=hf, stg_t=stg_t: e.dma_start(out=stg_t[:, 0:448], in_=consts[:, hf * 448:(hf + 1) * 448]),
                  writes=[stg_b])
            S.op(V, lambda e, hf=hf, stg_t=stg_t: e.tensor_copy(out=CTb[:, hf * 448:(hf + 1) * 448], in_=stg_t[:, 0:448]),
                 reads=[stg_b], writes=[bCTb])
        S.op(V, lambda e: e.tensor_copy(out=idb[:], in_=CTb[:, K_ID:K_ID + 128]), reads=[bCTb], writes=[bidb])
        S.op(G, lambda e: e.memset(Vr[:], 1.0), writes=bVr)
        S.op(G, lambda e: e.memset(M0[:], 0.0), writes=[bM0])
        for i in range(2):
            S.op(G, lambda e, i=i: e.memset(QT[i][0][:], 0.0), writes=[QT[i][1]])
        S.op(G, lambda e: e.memset(M0[0:64, 64:128], -30000.0), writes=[bM0])
        for i in range(3):
            S.op(G, lambda e, i=i: e.memset(Vm[i][0][:], 1.0), writes=[Vm[i][1]])
        for i in range(2):
            S.op(G, lambda e, i=i: e.memset(ift[i][0][:], 0.0), writes=[ift[i][1]])
        S.op(G, lambda e: e.memset(st6[:], 1.0), writes=[bst6])

        def sigmoid_chain(src_ap, src_bufs, out_ap, out_bufs, n):
            S.op(A, lambda e: e.activation(out=sg1[:, 0:n], in_=src_ap, func=AF.Exp, scale=-1.0),
                 reads=src_bufs, writes=[bsg1])
            S.op(A, lambda e: e.activation(out=sg1[:, 0:n], in_=sg1[:, 0:n], func=AF.Ln, bias=1.0),
                 reads=[bsg1], writes=[bsg1])
            S.op(A, lambda e: e.activation(out=out_ap, in_=sg1[:, 0:n], func=AF.Exp, scale=-1.0),
                 reads=[bsg1], writes=out_bufs)

        def rsqrt_small(src_ap, dst_ap, buf, scale, n):
            S.op(V, lambda e: e.tensor_scalar(out=dst_ap, in0=src_ap, scalar1=scale, scalar2=EPS,
                                              op0=ALU.mult, op1=ALU.add), reads=[buf], writes=[buf])
            S.op(A, lambda e: e.activation(out=dst_ap, in_=dst_ap, func=AF.Ln), reads=[buf], writes=[buf])
            S.op(A, lambda e: e.activation(out=dst_ap, in_=dst_ap, func=AF.Exp, scale=-0.5), reads=[buf], writes=[buf])

        def mixer_prep(qT_ap, q_bufs, kT_ap, k_bufs, ktok, la_ap, la_bufs, ig_ap, ig_bufs, kc, Vap, V_bufs, nv,
                       heads, bdm_ap, sc, ho):
            TN, TP, SU = kc
            (E3, bE3), (ktl, bktl), (ktp, bktp), (Kp, bKp) = sc["E3"], sc["ktl"], sc["ktp"], sc["Kp"]
            (qtl, bqtl), (PTm, bPTm), (tS, btS), (eb, beb) = ho["qtl"], ho["PTm"], ho["tS"], ho["eb"]
            la_hi, la_lo = la_ap
            terms = [(la_hi, TN), (la_lo, TN)]
            for i, (lx, cc) in enumerate(terms):
                S.op(PE, lambda e, lx=lx, cc=cc, i=i: e.matmul(pR[:, 0:128], lhsT=lx, rhs=CTb[:, cc:cc + 128],
                                                             start=(i == 0), stop=(i == 1)),
                     reads=la_bufs + [bCTb], writes=[bpR], same_ok=(i > 0))
            terms = [(la_hi, TP), (la_lo, TP)]
            if ig_ap is not None:
                terms += [(ig_ap[0], K_ID), (ig_ap[1], K_ID)]
            for i, (lx, cc) in enumerate(terms):
                S.op(PE, lambda e, lx=lx, cc=cc, i=i, n=len(terms): e.matmul(
                    pR[:, 128:256], lhsT=lx, rhs=CTb[:, cc:cc + 128], start=(i == 0), stop=(i == n - 1)),
                    reads=la_bufs + ig_bufs + [bCTb], writes=[bpR], same_ok=True)
            ne = 256
            if ktok is not None:
                ne = 384
                for i, lx in enumerate((la_hi, la_lo)):
                    S.op(PE, lambda e, lx=lx, i=i: e.matmul(pR[:, 256:384], lhsT=CTb[:, SU:SU + 128], rhs=lx,
                                                            start=(i == 0), stop=(i == 1)),
                         reads=la_bufs + [bCTb], writes=[bpR], same_ok=True)
            S.op(A, lambda e: e.activation(out=E3[:, 0:ne], in_=pR[:, 0:ne], func=AF.Exp), reads=[bpR], writes=[bE3])
            yield
            S.op(V, lambda e: e.tensor_tensor(out=qtl[:], in0=qT_ap, in1=E3[:, 0:128], op=ALU.mult),
                 reads=q_bufs + [bE3], writes=[bqtl])
            nh = len(heads)
            if nh == 2:
                S.op(V, lambda e: e.tensor_tensor(out=ktl[:, 0, :], in0=kT_ap, in1=E3[:, 128:256], op=ALU.mult),
                     reads=k_bufs + [bE3], writes=[bktl])
            else:
                for hh in range(nh):
                    S.op(V, lambda e, hh=hh: e.scalar_tensor_tensor(
                        out=ktl[:, hh, :], in0=kT_ap, scalar=CT[:, K_HM - KC + hh:K_HM - KC + hh + 1], in1=E3[:, 128:256],
                        op0=ALU.mult, op1=ALU.mult), reads=k_bufs + [bE3, bCT], writes=[bktl])
            if ktok is None:
                S.op(V, lambda e: e.scalar_tensor_tensor(out=ktp[:], in0=kT_ap, scalar=E3[:, 127:128], in1=E3[:, 128:256],
                                                         op0=ALU.mult, op1=ALU.mult), reads=k_bufs + [bE3], writes=[bktp])
            else:
                S.op(V, lambda e: e.tensor_tensor(out=Kp[:], in0=ktok[0], in1=E3[:, 256:384], op=ALU.mult),
                     reads=ktok[1] + [bE3], writes=[bKp])
            for hh, (r0, nr, vc0, vcn) in enumerate(heads):
                if nh == 2:
                    S.op(PE, lambda e, hh=hh, r0=r0, nr=nr: e.matmul(
                        pS[:, hh * 512:hh * 512 + 128], lhsT=ktl[r0:r0 + nr, 0, :], rhs=qtl[r0:r0 + nr, :],
                        start=True, stop=True), reads=[bktl, bqtl], writes=[bpS], same_ok=(hh > 0))
                else:
                    S.op(PE, lambda e, hh=hh: e.matmul(
                        pS[:, hh * 128:(hh + 1) * 128], lhsT=ktl[:, hh, :], rhs=qtl[:, :],
                        start=True, stop=True), reads=[bktl, bqtl], writes=[bpS], same_ok=(hh > 0))
            if nh == 2:
                S.op(V, lambda e: e.tensor_tensor(
                    out=PTm[:, 0:256].rearrange("p (a b) -> p a b", a=2),
                    in0=pS[:, 0:1024].rearrange("p (a b) -> p a b", a=2)[:, :, 0:128],
                    in1=CT[:, K_CAUS - KC:K_CAUS - KC + 256].rearrange("p (a b) -> p a b", a=2), op=ALU.mult),
                    reads=[bpS, bCT], writes=[bPTm])
            else:
                S.op(V, lambda e: e.tensor_tensor(out=PTm[:, 0:nh * 128], in0=pS[:, 0:nh * 128],
                                                  in1=CT[:, K_CAUS - KC:K_CAUS - KC + nh * 128], op=ALU.mult),
                     reads=[bpS, bCT], writes=[bPTm])
            yield
            if ktok is None:
                S.op(PE, lambda e: e.transpose(out=pT[:, 0, :], in_=ktp[:], identity=idb[:]), reads=[bktp, bidb],
                     writes=[bpT])
                S.op(V, lambda e: e.tensor_copy(out=Kp[:], in_=pT[:, 0, :]), reads=[bpT], writes=[bKp])
            S.op(PE, lambda e: e.matmul(pD[:, 0:nv], lhsT=Kp[:, :], rhs=Vap, start=True, stop=True),
                 reads=[bKp] + V_bufs, writes=[bpD])
            S.op(V, lambda e: e.tensor_tensor(out=tS[:, 0:nv], in0=pD[:, 0:nv], in1=bdm_ap, op=ALU.mult),
                 reads=[bpD, bCT], writes=[btS])
            S.op(V, lambda e: e.tensor_copy(out=eb[:, 0:1], in_=E3[:, 127:128]), reads=[bE3], writes=[beb])
            yield

        def mixer_apply(Vap, V_bufs, nv, heads, Sf, bSf, Sbf, bSbf, o_ap, ho):
            (qtl, bqtl), (PTm, bPTm), (tS, btS), (eb, beb) = ho["qtl"], ho["PTm"], ho["tS"], ho["eb"]
            nh = len(heads)
            S.op(PE, lambda e: e.matmul(o_ap, lhsT=qtl[:, :], rhs=Sbf, start=True, stop=False),
                 reads=[bqtl, bSbf], writes=[bpO])
            for hh, (r0, nr, vc0, vcn) in enumerate(heads):
                S.op(PE, lambda e, hh=hh, vc0=vc0, vcn=vcn: e.matmul(
                    o_ap[:, vc0:vc0 + vcn], lhsT=PTm[:, hh * 128:(hh + 1) * 128], rhs=Vap[:, vc0:vc0 + vcn],
                    start=False, stop=(hh == nh - 1)), reads=[bPTm] + V_bufs, writes=[bpO], same_ok=True)
            S.op(V, lambda e: e.scalar_tensor_tensor(out=Sf, in0=Sf, scalar=eb[:, 0:1], in1=tS[:, 0:nv],
                                                     op0=ALU.mult, op1=ALU.add),
                 reads=[bSf, beb, btS], writes=[bSf])
            S.op(A, lambda e: e.activation(out=Sbf, in_=Sf, func=AF.Copy), reads=[bSf], writes=[bSbf])

        def front(l, t):
            d, d3 = t % 2, t % 3
            slot = t % NR
            xt_t, xt_b = xt[d3]
            QT_t, bQT = QT[d]
            gqT_t, bgqT = gqT[d]
            gkT_t, bgkT = gkT[d]
            gkt_t, bgkt = gkt[d]
            gvb_t, bgvb = gvb[d3]
            gaT_t, bgaT = gaT[d]
            lgh, blgh = lghs[d]
            lgl, blgl = lgls[d]
            cin_t, bcin = cin[d]
            cin_p, bcin_p = cin[1 - d]
            Vm_t, bVm = Vm[d3]
            ift_t, bift = ift[d]
            lfp_t, blfp = lfp[d]
            gsp, bgsp = gsps[d]
            LFh, bLFh = LFhs[d]
            LFl, bLFl = LFls[d]
            IGh, bIGh = IGhs[d]
            IGl, bIGl = IGls[d]
            gA_t, bgA = gA[d]
            gB_t, bgB = gB[d3]
            gC_t, bgC = gC[d3]
            src = x_in if l == 0 else x1
            rd = [] if l == 0 else [bx1[t]]
            if t < 3:
                S.dma("sync", lambda e: e.dma_start(out=xt_t[:], in_=src[t * 128:(t + 1) * 128, :]), reads=rd, writes=[xt_b])
            S.op(V, lambda e: e.scalar_tensor_tensor(
                out=hb[:], in0=xt_t[:], scalar=1.0, in1=xt_t[:], op0=ALU.mult, op1=ALU.mult,
                accum_out=ss[:, 0:1]), reads=[xt_b], writes=[bhb, bss])
            rsqrt_small(ss[:, 0:1], ss[:, 1:2], bss, 1.0 / D, 1)
            S.op(A, lambda e: e.activation(out=hb[:], in_=xt_t[:], func=AF.Copy, scale=ss[:, 1:2]),
                 reads=[xt_b, bss], writes=[bhb])
            S.op(G, lambda e: e.tensor_copy(out=cin_t[:, :, 0:3], in_=cin_p[:, :, 128:131]), reads=[bcin_p], writes=[bcin])
            yield
            for k in range(8):
                S.op(PE, lambda e, k=k: e.transpose(out=pT[:, k, :], in_=hb[:, k * 128:(k + 1) * 128],
                                                    identity=idb[:]), reads=[bhb, bidb], writes=[bpT],
                     same_ok=(k > 0))
            S.op(V, lambda e: e.tensor_copy(out=hT[:], in_=pT[:]), reads=[bpT], writes=[bhT])
            yield

            def proj_fm(pX, bpX, cols):
                first = True
                for i, (c0, cn) in enumerate(cols):
                    for k in range(8):
                        S.op(PE, lambda e, i=i, c0=c0, cn=cn, k=k: e.matmul(
                            pX[0:cn, i * 128:(i + 1) * 128], lhsT=W[:, k, c0:c0 + cn], rhs=hT[:, k, :],
                            start=(k == 0), stop=(k == 7)), reads=[bW, bhT], writes=[bpX], same_ok=not first)
                        first = False

            def proj_tm(pX, bpX, c0, cn):
                for k in range(8):
                    S.op(PE, lambda e, k=k: e.matmul(
                        pX[:, 0:cn], lhsT=hT[:, k, :], rhs=W[:, k, c0:c0 + cn],
                        start=(k == 0), stop=(k == 7)), reads=[bW, bhT], writes=[bpX], same_ok=(k > 0))

            def ev_g1(pX, bpX):
                for half in range(2):
                    S.op(A, lambda e, half=half: e.activation(
                        out=QT_t[half * 64:(half + 1) * 64, :, :].rearrange("p (a two) b -> p a two b", two=2)[:, :, half, :],
                        in_=pX[half * 64:(half + 1) * 64, 0:384].rearrange("p (a b) -> p a b", a=3), func=AF.Copy),
                        reads=[bpX], writes=[bQT])
                S.op(A, lambda e: e.activation(out=gqT_t[:], in_=pX[:, 384:512], func=AF.Copy), reads=[bpX], writes=[bgqT])

            def ev_g2(pX, bpX):
                S.op(A, lambda e: e.activation(
                    out=KTr[:, :, slot * 128:(slot + 1) * 128], in_=pX[:, 0:384].rearrange("p (a b) -> p a b", a=3),
                    func=AF.Copy), reads=[bpX], writes=[bKT[slot]])
                S.op(A, lambda e: e.activation(out=gkT_t[:], in_=pX[:, 384:512], func=AF.Copy), reads=[bpX], writes=[bgkT])

            def ev_g3(pX, bpX):
                S.op(A, lambda e: e.activation(out=cin_t[:, 0:4, 3:131], in_=pX[:, 0:512].rearrange("p (a b) -> p a b", a=4),
                                               func=AF.Copy), reads=[bpX], writes=[bcin])

            def ev_g4(pX, bpX):
                S.op(A, lambda e: e.activation(out=cin_t[:, 4:6, 3:131], in_=pX[:, 0:256].rearrange("p (a b) -> p a b", a=2),
                                               func=AF.Copy), reads=[bpX], writes=[bcin])
                S.op(A, lambda e: e.activation(out=gaT_t[:], in_=pX[0:16, 256:384], func=AF.Copy), reads=[bpX], writes=[bgaT])

            def post_g4():
                S.op(G, lambda e: e.tensor_copy(out=gah[:], in_=gaT_t[:]), reads=[bgaT], writes=[bgah])
                S.op(G, lambda e: e.tensor_tensor(out=gal[:], in0=gaT_t[:], in1=gah[:], op=ALU.subtract), reads=[bgaT, bgah],
                     writes=[bgal])
                for i, (aa, ww) in enumerate(((gah, WAh), (gal, WAh), (gah, WAl))):
                    S.op(PE, lambda e, aa=aa, ww=ww, i=i: e.matmul(pR[:, 384:512], lhsT=aa[:, :], rhs=ww[:, :],
                                                                   start=(i == 0), stop=(i == 2)),
                         reads=[bgah, bgal, bWAh, bWAl], writes=[bpR], same_ok=(i > 0))
                S.op(V, lambda e: e.tensor_tensor(out=lag[:], in0=pR[:, 384:512], in1=LP[:, P_BA:P_BA + 128], op=ALU.add),
                     reads=[bpR, bLP], writes=[blag])
                S.op(A, lambda e: e.activation(out=lag[:], in_=lag[:], func=AF.Exp, scale=-1.0), reads=[blag], writes=[blag])
                S.op(A, lambda e: e.activation(out=lag[:], in_=lag[:], func=AF.Ln, bias=1.0), reads=[blag], writes=[blag])
                S.op(G, lambda e: e.tensor_copy(out=lgh[:], in_=lag[:]), reads=[blag], writes=[blgh])
                S.op(G, lambda e: e.tensor_tensor(out=lgl[:], in0=lag[:], in1=lgh[:], op=ALU.subtract), reads=[blag, blgh],
                     writes=[blgl])

            def ev_av(pX, bpX):
                S.op(A, lambda e: e.activation(
                    out=Vr[:, slot, :, 0:64], in_=pX[:, 0:384].rearrange("p (a b) -> p a b", a=6), func=AF.Copy),
                    reads=[bpX], writes=[bVr[slot]])

            def ev_ag(pX, bpX):
                S.op(A, lambda e: e.activation(out=zraw[:], in_=pX[:, 0:384], func=AF.Copy), reads=[bpX], writes=[bzraw])
                sigmoid_chain(pX[:, 0:384], [bpX], sg2[:], [bsg2], 384)

            def post_ag():
                S.op(G, lambda e: e.tensor_tensor(out=gA_t[:], in0=zraw[:], in1=sg2[:], op=ALU.mult),
                     reads=[bzraw, bsg2], writes=[bgA])

            def ev_mv(pX, bpX):
                S.op(A, lambda e: e.activation(out=Vm_t[:, :, 0:64], in_=pX[:, 0:384].rearrange("p (a b) -> p a b", a=6),
                                               func=AF.Copy), reads=[bpX], writes=[bVm])

            def ev_mo(pX, bpX):
                S.op(V, lambda e: e.tensor_tensor(out=ift_t[:].rearrange("p (a b) -> p a b", a=2)[:, :, 0:6],
                                                  in0=pX[:, 384:396].rearrange("p (a b) -> p a b", a=2),
                                                  in1=bgc[:].rearrange("p (a b) -> p a b", a=2), op=ALU.add),
                     reads=[bpX, bbgc], writes=[bift])
                S.op(A, lambda e: e.activation(out=lfp_t[:], in_=ift_t[:, 8:16], func=AF.Exp, scale=-1.0), reads=[bift],
                     writes=[blfp])
                S.op(A, lambda e: e.activation(out=lfp_t[:], in_=lfp_t[:], func=AF.Ln, bias=1.0), reads=[blfp], writes=[blfp])
                sigmoid_chain(pX[:, 0:384], [bpX], sg2[:], [bsg2], 384)

            def post_mo():
                for gi, (src_ap, src_b) in enumerate(((lfp_t[:, 0:6], blfp), (ift_t[:, 0:6], bift))):
                    S.op(V, lambda e, gi=gi, src_ap=src_ap: e.tensor_copy(out=gsp[:, 2 * gi, 0:6], in_=src_ap),
                         reads=[src_b], writes=[bgsp])
                    S.op(V, lambda e, gi=gi, src_ap=src_ap: e.tensor_tensor(out=gsp[:, 2 * gi + 1, 0:6], in0=src_ap,
                                                                           in1=gsp[:, 2 * gi, 0:6], op=ALU.subtract),
                         reads=[src_b, bgsp], writes=[bgsp])
                S.op(G, lambda e: e.tensor_tensor(out=gB_t[:], in0=sg2[:], in1=LP[:, P_MLG:P_MLG + 384], op=ALU.mult),
                     reads=[bsg2, bLP], writes=[bgB])

            def ev_mg(pX, bpX):
                S.op(A, lambda e: e.activation(out=zraw[:], in_=pX[:, 0:384], func=AF.Copy), reads=[bpX], writes=[bzraw])
                sigmoid_chain(pX[:, 0:384], [bpX], sg2[:], [bsg2], 384)

            def post_mg():
                for gi, (dst_t, dst_b) in enumerate(((LFh, bLFh), (LFl, bLFl), (IGh, bIGh), (IGl, bIGl))):
                    S.op(V, lambda e, gi=gi, dst_t=dst_t: e.tensor_copy(
                        out=dst_t[:].rearrange("p (h c) -> p h c", h=6),
                        in_=gsp[:, gi, 0:6].unsqueeze(2).to_broadcast([128, 6, 64])), reads=[bgsp], writes=[dst_b])
                S.op(G, lambda e: e.tensor_tensor(out=sg2[:], in0=zraw[:], in1=sg2[:], op=ALU.mult),
                     reads=[bzraw, bsg2], writes=[bsg2])
                S.op(G, lambda e: e.tensor_tensor(out=gB_t[:], in0=gB_t[:], in1=sg2[:], op=ALU.mult),
                     reads=[bgB, bsg2], writes=[bgB])

            def ev_gkv(pX, bpX):
                S.op(A, lambda e: e.activation(out=gkt_t[:], in_=pX[:, 0:128], func=AF.Copy), reads=[bpX], writes=[bgkt])
                S.op(A, lambda e: e.activation(out=gvb_t[:], in_=pX[:, 128:384], func=AF.Copy), reads=[bpX], writes=[bgvb])

            def ev_gg(pX, bpX):
                S.op(A, lambda e: e.activation(out=zraw[:, 0:256], in_=pX[:, 0:256], func=AF.Copy), reads=[bpX],
                     writes=[bzraw])
                sigmoid_chain(pX[:, 0:256], [bpX], sg2[:, 0:256], [bsg2], 256)

            def post_gg():
                S.op(G, lambda e: e.tensor_tensor(out=sg2[:, 0:256], in0=zraw[:, 0:256], in1=sg2[:, 0:256], op=ALU.mult),
                     reads=[bzraw, bsg2], writes=[bsg2])
                S.op(G, lambda e: e.tensor_tensor(out=gC_t[:], in0=sg2[:, 0:256], in1=LP[:, P_GLG:P_GLG + 256], op=ALU.mult),
                     reads=[bsg2, bLP], writes=[bgC])

            FM = lambda cols: (lambda pX, bpX: proj_fm(pX, bpX, cols))
            TM = lambda c0, cn: (lambda pX, bpX: proj_tm(pX, bpX, c0, cn))
            groups = [
                (FM([(C_AQ, 128), (C_AQ + 128, 128), (C_AQ + 256, 128), (C_GQ, 128)]), ev_g1, None),
                (FM([(C_AK, 128), (C_AK + 128, 128), (C_AK + 256, 128), (C_GK, 128)]), ev_g2, None),
                (FM([(C_MQ, 128), (C_MQ + 128, 128), (C_MQ + 256, 128), (C_MK, 128)]), ev_g3, None),
                (FM([(C_MK + 128, 128), (C_MK + 256, 128), (C_GA, 16)]), ev_g4, post_g4),
                (TM(C_AV, 384), ev_av, None),
                (TM(C_AG, 384), ev_ag, post_ag),
                (TM(C_MV, 384), ev_mv, None),
                (TM(C_MO, 396), ev_mo, post_mo),
                (TM(C_MG, 384), ev_mg, post_mg),
                (TM(C_GK, 384), ev_gkv, None),
                (TM(C_GG, 256), ev_gg, post_gg),
            ]
            banks = ((pA, bpA), (pB, bpB))
            ng = len(groups)
            for k in range(ng + 2):
                if 0 <= k - 2 < ng and groups[k - 2][2] is not None:
                    groups[k - 2][2]()
                if 0 <= k - 1 < ng:
                    groups[k - 1][1](*banks[(k - 1) % 2])
                if k < ng:
                    groups[k][0](*banks[k % 2])
                yield

        def b1(l, t):
            d = t % 2
            QT_t, bQT = QT[d]
            lgh, blgh = lghs[d]
            lgl, blgl = lgls[d]
            cin_t, bcin = cin[d]
            ift_t, bift = ift[d]
            lfp_t, blfp = lfp[d]
            LFh, bLFh = LFhs[d]
            LFl, bLFl = LFls[d]
            IGh, bIGh = IGhs[d]
            IGl, bIGl = IGls[d]
            gA_t, bgA = gA[d]
            ybf, bybf = ybfs[d]
            gqT_t, bgqT = gqT[d]
            gkT_t, bgkT = gkT[d]
            gkt_t, bgkt = gkt[d]
            gvb_t, bgvb = gvb[t % 3]
            Vm_t, bVm = Vm[t % 3]
            pO, bpO = pOa, bpOa
            rec, brec = reca, breca
            j0 = max(0, 4 - t)
            pO3 = pO[:, 0:390].rearrange("p (h c) -> p h c", h=6)
            qkT, bqkT = csg, bcsg

            def att():
                for h in range(6):
                    p, half = h // 2, h % 2
                    r0 = half * 64
                    first = True
                    for j in range(j0, 5):
                        sj = (t - 4 + j) % NR
                        extra = j >= 3 or j == 0
                        S.op(PE, lambda e, j=j, sj=sj, p=p, h=h, extra=extra: e.matmul(
                            pS[:, j * 128:(j + 1) * 128], lhsT=KTr[:, p, sj * 128:(sj + 1) * 128],
                            rhs=QT_t[:, h, :], start=True, stop=not extra),
                            reads=[bKT[sj], bQT], writes=[bpS], same_ok=not first)
                        first = False
                        if j == 0:
                            S.op(PE, lambda e: e.matmul(pS[:, 0:128], lhsT=idb[:], rhs=M0[:], start=False, stop=True),
                                 reads=[bidb, bM0], writes=[bpS], same_ok=True)
                        if j >= 3:
                            S.op(PE, lambda e, j=j, h=h: e.matmul(
                                pS[:, j * 128:(j + 1) * 128], lhsT=idb[:], rhs=RBh[:, h, (j - 3) * 128:(j - 2) * 128],
                                start=False, stop=False), reads=[bidb, bRBh], writes=[bpS], same_ok=True)
                            S.op(PE, lambda e, j=j, h=h: e.matmul(
                                pS[:, j * 128:(j + 1) * 128], lhsT=idb[:], rhs=RBl[:, h, (j - 3) * 128:(j - 2) * 128],
                                start=False, stop=True), reads=[bidb, bRBl], writes=[bpS], same_ok=True)
                    S.op(A, lambda e, h=h: e.activation(
                        out=PT[:, j0 * 128:640], in_=pS[:, j0 * 128:640], func=AF.Exp,
                        bias=LP[:, P_CB + h:P_CB + h + 1]), reads=[bpS, bLP], writes=[bPT])
                    yield
                    for j in range(j0, 5):
                        sj = (t - 4 + j) % NR
                        S.op(PE, lambda e, j=j, sj=sj, h=h: e.matmul(
                            pO[:, h * 65:(h + 1) * 65], lhsT=PT[:, j * 128:(j + 1) * 128], rhs=Vr[:, sj, h, :],
                            start=(j == j0), stop=(j == 4)), reads=[bPT, bVr[sj]], writes=[bpO], same_ok=(j > j0))
                S.op(V, lambda e: e.reciprocal(out=rec[:, 0:6].unsqueeze(2), in_=pO3[:, :, 64:65]), reads=[bpO], writes=[brec])
                S.op(V, lambda e: e.tensor_tensor(
                    out=ya[:].rearrange("p (h c) -> p h c", h=6), in0=pO3[:, :, 0:64],
                    in1=rec[:, 0:6].unsqueeze(2).to_broadcast([128, 6, 64]), op=ALU.mult),
                    reads=[bpO, brec], writes=[bya])
                S.op(G, lambda e: e.tensor_tensor(out=ybf[:, 0:384], in0=ya[:], in1=gA_t[:], op=ALU.mult),
                     reads=[bya, bgA], writes=[bybf])
                yield


            def mprep():
                pG = mixer_prep(gqT_t[:, :], [bgqT], gkT_t[:, :], [bgkT], (gkt_t[:, :], [bgkt]),
                                (lgh[:, :], lgl[:, :]), [blgh, blgl], None, [], (K_TN16, K_TP16, K_SU16),
                                gvb_t[:, :], [bgvb], 256, [(32 * i, 32, 64 * i, 64) for i in range(4)],
                                CT[:, K_BDG - KC:K_BDG - KC + 256], SCA, HO[0][d])

                first = True
                for ct in range(6):
                    for jj in range(4):
                        S.op(PE, lambda e, ct=ct, jj=jj: e.matmul(
                            pS[:, ct * 128:(ct + 1) * 128], lhsT=DG[:, ct * 4 + jj, :], rhs=cin_t[:, ct, jj:jj + 128],
                            start=(jj == 0), stop=(jj == 3)), reads=[bDG, bcin], writes=[bpS], same_ok=not first)
                        first = False
                sflat = csg[:].rearrange("p a b -> p (a b)")
                S.op(A, lambda e: e.activation(out=czb[:], in_=pS[:, 0:768], func=AF.Copy), reads=[bpS], writes=[bczb])
                S.op(A, lambda e: e.activation(out=sflat, in_=pS[:, 0:768], func=AF.Exp, scale=-1.0), reads=[bpS], writes=[bcsg])
                next(pG)
                yield
                S.op(A, lambda e: e.activation(out=sflat, in_=sflat, func=AF.Ln, bias=1.0), reads=[bcsg], writes=[bcsg])
                S.op(A, lambda e: e.activation(out=sflat, in_=sflat, func=AF.Exp, scale=-1.0), reads=[bcsg], writes=[bcsg])
                next(pG)
                yield
                for hf in range(2):
                    S.op(G, lambda e, hf=hf: e.tensor_tensor(out=sflat[:, hf * 384:(hf + 1) * 384],
                                                             in0=czb[:, hf * 384:(hf + 1) * 384],
                                                             in1=sflat[:, hf * 384:(hf + 1) * 384], op=ALU.mult),
                         reads=[bczb, bcsg], writes=[bcsg])
                next(pG)
                yield
                Vmf = Vm_t[:].rearrange("p h c -> p (h c)")
                def mk_pair(pr, sc):
                    return mixer_prep(qkT[:, pr, :], [bqkT], qkT[:, 3 + pr, :], [bqkT], None,
                                      (LFh[:, pr * 128:(pr + 1) * 128], LFl[:, pr * 128:(pr + 1) * 128]), [bLFh, bLFl],
                                      (IGh[:, pr * 128:(pr + 1) * 128], IGl[:, pr * 128:(pr + 1) * 128]), [bIGh, bIGl],
                                      (K_TN1, K_TP1, K_SU1), Vmf[:, pr * 130:(pr + 1) * 130], [bVm], 130,
                                      [(0, 64, 0, 65), (64, 64, 65, 65)], CT[:, K_BDM - KC:K_BDM - KC + 130], sc,
                                      HO[1 + pr][d])

                p0, p1, p2 = mk_pair(0, SCB), mk_pair(1, SCA), mk_pair(2, SCB)
                for g in (p0, p0, p1, p0, p1, p2, p1, p2, p2):
                    next(g)
                    yield

            ga, gm = att(), mprep()
            while ga is not None or gm is not None:
                if ga is not None:
                    try:
                        next(ga)
                    except StopIteration:
                        ga = None
                if gm is not None:
                    try:
                        next(gm)
                    except StopIteration:
                        gm = None
                yield
        def b2(l, t):
            d, d3 = t % 2, t % 3
            xt_t, xt_b = xt[d3]
            gvb_t, bgvb = gvb[d3]
            Vm_t, bVm = Vm[d3]
            gB_t, bgB = gB[d3]
            gC_t, bgC = gC[d3]
            ybf, bybf = ybfs[d]
            pO3 = pO[:, 0:390].rearrange("p (h c) -> p h c", h=6)
            Vmf = Vm_t[:].rearrange("p h c -> p (h c)")
            def gpost():
                S.op(G, lambda e: e.tensor_tensor(out=sq[:, 0:256], in0=hgl[:], in1=hgl[:], op=ALU.mult),
                     reads=[bhgl], writes=[bsq])
                yield
                S.op(V, lambda e: e.tensor_reduce(out=st6[:, 16:20], in_=sq[:, 0:256].rearrange("p (h c) -> p h c", h=4),
                                                  axis=AX.X, op=ALU.add), reads=[bsq], writes=[bst6])
                yield
                rsqrt_small(st6[:, 16:24], st6[:, 24:32], bst6, 1.0 / 64, 8)
                yield
                S.op(V, lambda e: e.tensor_tensor(
                    out=hgl[:].rearrange("p (h c) -> p h c", h=4), in0=hgl[:].rearrange("p (h c) -> p h c", h=4),
                    in1=st6[:, 24:28].unsqueeze(2).to_broadcast([128, 4, 64]), op=ALU.mult),
                    reads=[bhgl, bst6], writes=[bhgl])
                yield
                S.op(G, lambda e: e.tensor_tensor(out=ybf[:, 768:1024], in0=hgl[:], in1=gC_t[:], op=ALU.mult),
                     reads=[bhgl, bgC], writes=[bybf])
                yield

            gq = gpost()

            def adv(g):
                try:
                    next(g)
                except StopIteration:
                    pass

            mixer_apply(gvb_t[:, :], [bgvb], 256, [(32 * i, 32, 64 * i, 64) for i in range(4)], Sg[:, :], bSg,
                        Sgb[:, :], bSgb, pO[:, 0:256], HO[0][d])
            S.op(A, lambda e: e.activation(out=hgl[:], in_=pO[:, 0:256], func=AF.Copy), reads=[bpO], writes=[bhgl])
            yield
            for pr in range(3):
                mixer_apply(Vmf[:, pr * 130:(pr + 1) * 130], [bVm], 130, [(0, 64, 0, 65), (64, 64, 65, 65)],
                            Sm[:, pr, :], bSm[pr], Smb[:, pr, :], bSmb[pr], pO[:, pr * 130:(pr + 1) * 130], HO[1 + pr][d])
                adv(gq)
                adv(gq)
                yield
            for _ in range(6):
                adv(gq)
            S.op(V, lambda e: e.tensor_copy(out=rec[:, 0:6].unsqueeze(2), in_=pO3[:, :, 64:65]), reads=[bpO], writes=[brec])
            S.op(V, lambda e: e.scalar_tensor_tensor(out=rec[:, 0:6], in0=rec[:, 0:6], scalar=-1.0, in1=rec[:, 0:6],
                                                     op0=ALU.mult, op1=ALU.max), reads=[brec], writes=[brec])
            S.op(V, lambda e: e.tensor_scalar(out=rec[:, 0:6], in0=rec[:, 0:6], scalar1=1.0, scalar2=None,
                                              op0=ALU.max), reads=[brec], writes=[brec])
            S.op(V, lambda e: e.reciprocal(out=rec[:, 0:6], in_=rec[:, 0:6]), reads=[brec], writes=[brec])
            S.op(V, lambda e: e.tensor_tensor(
                out=hml[:].rearrange("p (h c) -> p h c", h=6), in0=pO3[:, :, 0:64],
                in1=rec[:, 0:6].unsqueeze(2).to_broadcast([128, 6, 64]), op=ALU.mult),
                reads=[bpO, brec], writes=[bhml])
            S.op(G, lambda e: e.tensor_tensor(out=sq[:], in0=hml[:], in1=hml[:], op=ALU.mult), reads=[bhml], writes=[bsq])
            S.op(V, lambda e: e.tensor_reduce(out=st6[:, 0:6], in_=sq[:].rearrange("p (h c) -> p h c", h=6),
                                              axis=AX.X, op=ALU.add), reads=[bsq], writes=[bst6])
            rsqrt_small(st6[:, 0:8], st6[:, 8:16], bst6, 1.0 / 64, 8)
            S.op(V, lambda e: e.tensor_tensor(
                out=hml[:].rearrange("p (h c) -> p h c", h=6), in0=hml[:].rearrange("p (h c) -> p h c", h=6),
                in1=st6[:, 8:14].unsqueeze(2).to_broadcast([128, 6, 64]), op=ALU.mult),
                reads=[bhml, bst6], writes=[bhml])
            S.op(G, lambda e: e.tensor_tensor(out=ybf[:, 384:768], in0=hml[:], in1=gB_t[:], op=ALU.mult),
                 reads=[bhml, bgB], writes=[bybf])
            yield

            for k in range(8):
                S.op(PE, lambda e, k=k: e.transpose(out=pT[:, k, :], in_=ybf[:, k * 128:(k + 1) * 128],
                                                    identity=idb[:]), reads=[bybf, bidb], writes=[bpT],
                     same_ok=(k > 0))
            S.op(V, lambda e: e.tensor_copy(out=yT[:], in_=pT[:]), reads=[bpT], writes=[byT])
            yield
            for half, (pX, bpX) in enumerate(((pO, bpO), (pR, bpR))):
                for k in range(8):
                    S.op(PE, lambda e, k=k, half=half, pX=pX: e.matmul(
                        pX[:, 0:512], lhsT=yT[:, k, :], rhs=Wo[:, k, half * 512:(half + 1) * 512],
                        start=(k == 0), stop=(k == 7)), reads=[byT, bWo], writes=[bpX], same_ok=(k > 0))
                S.op(V, lambda e, half=half, pX=pX: e.tensor_tensor(
                    out=xt_t[:, half * 512:(half + 1) * 512], in0=pX[:, 0:512],
                    in1=xt_t[:, half * 512:(half + 1) * 512], op=ALU.add), reads=[bpX, xt_b], writes=[xt_b])
                yield
            if l < n_layers - 1:
                S.dma("sync", lambda e: e.dma_start(out=x1[t * 128:(t + 1) * 128, :], in_=xt_t[:]),
                      reads=[xt_b], writes=[bx1[t]])
            else:
                S.op(V, lambda e: e.scalar_tensor_tensor(
                    out=ybf[:], in0=xt_t[:], scalar=1.0, in1=xt_t[:], op0=ALU.mult, op1=ALU.mult,
                    accum_out=ss2[:, 0:1]), reads=[xt_b], writes=[bybf, bss2])
                rsqrt_small(ss2[:, 0:1], ss2[:, 1:2], bss2, 1.0 / D, 1)
                S.op(V, lambda e: e.scalar_tensor_tensor(
                    out=xt_t[:], in0=xt_t[:], scalar=ss2[:, 1:2], in1=FG[:], op0=ALU.mult, op1=ALU.mult),
                    reads=[xt_b, bss2, bFG], writes=[xt_b])
                S.dma("sync", lambda e: e.dma_start(out=y_out[t * 128:(t + 1) * 128, :], in_=xt_t[:]),
                      reads=[xt_b], writes=[by[t]])
            if t + 3 < NT:
                srcp = x_in if l == 0 else x1
                rdp = [] if l == 0 else [bx1[t + 3]]
                S.dma("sync", lambda e: e.dma_start(out=xt_t[:], in_=srcp[(t + 3) * 128:(t + 4) * 128, :]), reads=rdp,
                      writes=[xt_b])
            yield

        def drive(*gens):
            gens = [[g, ev] for (g, ev) in gens if g is not None]
            i = 0
            while gens:
                for ent in list(gens):
                    if i % ent[1] == 0 or len(gens) == 1:
                        try:
                            next(ent[0])
                        except StopIteration:
                            gens.remove(ent)
                i += 1

        for l in range(n_layers):
            S.dma("sync", lambda e, l=l: e.dma_start(out=LP[:], in_=lp[l, :, :]), writes=[bLP])
            S.dma("sync", lambda e, l=l: e.dma_start(out=WA[:], in_=walpha[l, :, :]), writes=[bWA])
            for ch in range(3):
                stg_t, stg_b = stg[ch % 2]
                S.dma("sync", lambda e, l=l, ch=ch, stg_t=stg_t: e.dma_start(out=stg_t[:, 0:512], in_=relb[l, :, ch * 512:(ch + 1) * 512]),
                      writes=[stg_b])
                for hh in range(2):
                    h = 2 * ch + hh
                    seg = stg_t[:, hh * 256:(hh + 1) * 256]
                    S.op(V, lambda e, h=h, seg=seg: e.tensor_scalar(out=seg, in0=seg, scalar1=LP[:, P_CB + h:P_CB + h + 1],
                                                                  scalar2=None, op0=ALU.subtract), reads=[stg_b, bLP], writes=[stg_b])
                    S.op(V, lambda e, h=h, seg=seg: e.tensor_copy(out=RBh[:, h, :], in_=seg), reads=[stg_b], writes=[bRBh])
                    S.op(V, lambda e, h=h, seg=seg: e.tensor_tensor(out=RBl[:, h, :], in0=seg, in1=RBh[:, h, :], op=ALU.subtract),
                         reads=[stg_b, bRBh], writes=[bRBl])
            S.op(G, lambda e: e.memset(RBh[64:128, :, 128:192], -30000.0), writes=[bRBh])
            S.op(G, lambda e: e.memset(RBl[64:128, :, 128:192], 0.0), writes=[bRBl])
            S.op(V, lambda e: e.tensor_scalar(out=gsc[:, 0:8], in0=LP[:, P_G:P_G + 8], scalar1=0.125, scalar2=None,
                                              op0=ALU.mult), reads=[bLP], writes=[bgsc])
            S.op(V, lambda e: e.tensor_scalar(out=gsc[:, 8:16], in0=LP[:, P_G:P_G + 8], scalar1=float(32 ** -0.5),
                                              scalar2=None, op0=ALU.mult), reads=[bLP], writes=[bgsc])
            S.op(V, lambda e: e.tensor_copy(out=bgc[:], in_=LP[:, P_BG:P_BG + 12]), reads=[bLP], writes=[bbgc])
            S.op(V, lambda e: e.tensor_copy(out=WAh[:], in_=WA[:]), reads=[bWA], writes=[bWAh])
            S.op(V, lambda e: e.tensor_tensor(out=WAl[:], in0=WA[:], in1=WAh[:], op=ALU.subtract), reads=[bWA, bWAh],
                 writes=[bWAl])
            S.op(V, lambda e: e.tensor_scalar(out=bgc[:, 0:6], in0=bgc[:, 0:6], scalar1=LN8, scalar2=None,
                                              op0=ALU.add), reads=[bbgc], writes=[bbgc])
            for idx in range(24):
                S.op(V, lambda e, idx=idx: e.tensor_scalar(out=DG[:, idx, :], in0=idb[:], scalar1=LP[:, P_CW + idx:P_CW + idx + 1],
                                                           scalar2=None, op0=ALU.mult), reads=[bidb, bLP], writes=[bDG])
            stgs = [stg[0], stg[1], xt[0], xt[1], xt[2]]
            si = 0
            for k in range(8):
                for c0 in range(0, D_IN, SW):
                    cn = min(SW, D_IN - c0)
                    stg_t, stg_b = stgs[si % 5]
                    si += 1
                    S.dma("sync", lambda e, l=l, k=k, c0=c0, cn=cn, stg_t=stg_t: e.dma_start(
                        out=stg_t[:, 0:cn], in_=w_in[l, k * 128:(k + 1) * 128, c0:c0 + cn]), writes=[stg_b])
                    for (a, b_, sc) in ((0, 384, 0), (384, C_GQ, 1), (C_GQ, C_GK, 2), (C_GK, D_IN, 1)):
                        lo, hi = max(a, c0), min(b_, c0 + cn)
                        if lo >= hi:
                            continue
                        if sc == 0:
                            scl = gsc[:, k:k + 1]
                        elif sc == 2:
                            scl = gsc[:, 8 + k:9 + k]
                        else:
                            scl = LP[:, P_G + k:P_G + k + 1]
                        if si % 2 == 0:
                            S.op(A, lambda e, lo=lo, hi=hi, k=k, c0=c0, scl=scl, stg_t=stg_t: e.activation(
                                out=W[:, k, lo:hi], in_=stg_t[:, lo - c0:hi - c0], func=AF.Copy, scale=scl),
                                reads=[stg_b, bgsc, bLP], writes=[bW])
                        else:
                            S.op(V, lambda e, lo=lo, hi=hi, k=k, c0=c0, scl=scl, stg_t=stg_t: e.tensor_scalar(
                                out=W[:, k, lo:hi], in0=stg_t[:, lo - c0:hi - c0], scalar1=scl, scalar2=None, op0=ALU.mult),
                                reads=[stg_b, bgsc, bLP], writes=[bW])
            for k in range(8):
                for hf in range(2):
                    stg_t, stg_b = stgs[si % 5]
                    si += 1
                    S.dma("sync", lambda e, l=l, k=k, hf=hf, stg_t=stg_t: e.dma_start(
                        out=stg_t[:, 0:512], in_=w_out[l, k * 128:(k + 1) * 128, hf * 512:(hf + 1) * 512]), writes=[stg_b])
                    if si % 2 == 0:
                        S.op(A, lambda e, k=k, hf=hf, stg_t=stg_t: e.activation(
                            out=Wo[:, k, hf * 512:(hf + 1) * 512], in_=stg_t[:, 0:512], func=AF.Copy),
                            reads=[stg_b], writes=[bWo])
                    else:
                        S.op(V, lambda e, k=k, hf=hf, stg_t=stg_t: e.tensor_copy(
                            out=Wo[:, k, hf * 512:(hf + 1) * 512], in_=stg_t[:, 0:512]),
                            reads=[stg_b], writes=[bWo])
            S.op(G, lambda e: e.memset(Sm[:], 0.0), writes=bSm)
            S.op(G, lambda e: e.memset(Smb[:], 0.0), writes=bSmb)
            S.op(G, lambda e: e.memset(Sg[:], 0.0), writes=[bSg])
            S.op(G, lambda e: e.memset(Sgb[:], 0.0), writes=[bSgb])
            for i in range(2):
                S.op(G, lambda e, i=i: e.memset(cin[i][0][:], 0.0), writes=[cin[i][1]])

            drive((front(l, 0), 1))
            drive((b1(l, 0), 1), (front(l, 1) if NT > 1 else None, 1))
            for t in range(NT):
                drive((b2(l, t), 1),
                      (b1(l, t + 1) if t + 1 < NT else None, B1_EVERY),
                      (front(l, t + 2) if t + 2 < NT else None, FRONT_EVERY))
        S.finish("sync")
        S.run()
    return nc


def _consts():
    c = np.zeros((128, NCONST), np.float32)
    j = np.arange(128)[:, None]
    t = np.arange(128)[None, :]
    c[:, K_ID:K_ID + 128] = (j == t)
    le = (j <= t).astype(np.float32)
    gt = (j > t).astype(np.float32)
    c[:, K_TN1:K_TN1 + 128] = -le
    c[:, K_TP1:K_TP1 + 128] = le
    c[:, K_SU1:K_SU1 + 128] = -gt
    c[:, K_TN16:K_TN16 + 128] = -le / 16.0
    c[:, K_TP16:K_TP16 + 128] = le / 16.0
    c[:, K_SU16:K_SU16 + 128] = -gt / 16.0
    c[:, K_CAUS:K_CAUS + 512] = np.tile(le, (1, 4))
    c[0:64, K_BDM:K_BDM + 65] = 1.0
    c[64:128, K_BDM + 65:K_BDM + 130] = 1.0
    for i in range(4):
        c[32 * i:32 * i + 32, K_BDG + 64 * i:K_BDG + 64 * i + 64] = 1.0
        c[32 * i:32 * i + 32, K_HM + i] = 1.0
    return c


def _host_layout(norm_g, b_gates, conv_w, w_alpha, b_alpha, rel_bias, ml_norm_g, gla_norm_g, final_g):
    L = norm_g.shape[0]
    lp = np.zeros((L, 128, NLP), np.float32)
    relb = np.zeros((L, 128, 6, 256), np.float32)
    k = np.arange(128)[:, None]
    q = np.arange(128)[None, :]
    for l in range(L):
        lp[l, :, P_G:P_G + 8] = norm_g[l].reshape(8, 128).T
        lp[l, :, P_CW:P_CW + 24] = conv_w[l].reshape(4, 6, 128).transpose(2, 1, 0).reshape(128, 24)
        lp[l, :, P_BG:P_BG + 12] = b_gates[l][None, :]
        lp[l, :, P_BA:P_BA + 128] = b_alpha[l][None, :]
        lp[l, :, P_MLG:P_MLG + 384] = ml_norm_g[l][None, :]
        lp[l, :, P_GLG:P_GLG + 256] = gla_norm_g[l][None, :]
        lp[l, :, P_CB:P_CB + 6] = rel_bias[l][:, 256][None, :]
        for jj, dlt in ((0, 128), (1, 0)):
            idx = np.clip(q - k + dlt, -128, 128) + 128
            relb[l, :, :, jj * 128:(jj + 1) * 128] = rel_bias[l][:, idx].transpose(1, 0, 2)
    fg = np.broadcast_to(final_g[None, :], (128, D)).astype(np.float32).copy()
    return lp, relb.reshape(L, 128, 6 * 256), fg


_NC_CACHE = {}


def kernel(x, norm_g, w_in, b_gates, conv_w, w_alpha, b_alpha, rel_bias, ml_norm_g, gla_norm_g, w_out, final_g,
           _dbg=None, _layers=None):
    x = np.asarray(x, np.float32)
    B, S_len, _ = x.shape
    L = int(_layers) if _layers else int(np.asarray(norm_g).shape[0])
    f = lambda a: np.ascontiguousarray(np.asarray(a, np.float32))
    lp, relb, fg = _host_layout(f(norm_g), f(b_gates), f(conv_w), f(w_alpha), f(b_alpha), f(rel_bias),
                                f(ml_norm_g), f(gla_norm_g), f(final_g))
    key = (S_len, L, _dbg)
    if key not in _NC_CACHE:
        _NC_CACHE[key] = build(S_len, L, _dbg)
    nc = _NC_CACHE[key]
    common = {"w_in": f(w_in)[:L], "w_out": f(w_out)[:L], "consts": _consts(), "lp": lp[:L], "walpha": f(w_alpha)[:L],
              "relb": relb[:L], "fgbc": fg}
    in_maps = [dict(common, x=np.ascontiguousarray(x[b])) for b in range(B)]
    res = run_bass_kernel_spmd(nc, in_maps, core_ids=list(range(B)))
    out = np.stack([np.asarray(res.results[b]["y"], np.float32) for b in range(B)], axis=0)
    if _dbg:
        return out, [np.asarray(res.results[b]["dbg"]) for b in range(B)]
    return out
```
